# Optimizing a Trainium2 kernel written in Bass

```python
import math
import jax, jax.numpy as jnp
from jax import lax
import numpy as np

D_MODEL = 1024
BATCH = 2
SEQ = 8192
DEPTH = 4

N_MIXERS = 3
HEAD_DIM = 64
PLE_DIM = 256
D_FF = 4 * D_MODEL
NORM_EPS = 1e-6
NEG_INF = -1e30
Q_BLOCK = 128

REL_BUCKETS = 32
REL_MAX_DIST = 128
REL_HEADS = D_MODEL // HEAD_DIM

DA_HEADS = D_MODEL // (2 * HEAD_DIM)
DA_IN = 3 * D_MODEL

NSA_HEADS = D_MODEL // HEAD_DIM
NSA_KV_GROUPS = 4
NSA_GROUP_SIZE = NSA_HEADS // NSA_KV_GROUPS
NSA_CMP_LEN = 32
NSA_CMP_STRIDE = 16
NSA_CMP_HIDDEN = 256
NSA_SEL_LEN = 64
NSA_TOP_N = 16
NSA_WINDOW = 512
NSA_Q_CHUNK = 64
NSA_FORCE_SCORE = 1e4
NSA_KV_DIM = NSA_KV_GROUPS * HEAD_DIM
NSA_IN = NSA_HEADS * HEAD_DIM + 6 * NSA_KV_DIM + 3 * NSA_HEADS

FOX_HEADS = D_MODEL // HEAD_DIM
FOX_IN = 3 * D_MODEL + FOX_HEADS

N_DA = (DEPTH + 2) // 3
N_NSA = (DEPTH + 1) // 3
N_FOX = DEPTH // 3

kernel_name = 'hybrid_diff_nsa_fox_trunk'


def rmsnorm(x, g):
    xf = x.astype(jnp.float32)
    y = xf * lax.rsqrt(jnp.mean(xf * xf, axis=-1, keepdims=True) + NORM_EPS)
    return (y * g.astype(jnp.float32)).astype(x.dtype)


def t5_bucket(dist):
    n = jnp.maximum(dist, 0)
    max_exact = REL_BUCKETS // 2
    nf = jnp.maximum(n, 1).astype(jnp.float32)
    large = max_exact + (jnp.log(nf / max_exact) / math.log(REL_MAX_DIST / max_exact)
                         * (REL_BUCKETS - max_exact)).astype(jnp.int32)
    large = jnp.minimum(large, REL_BUCKETS - 1)
    return jnp.where(n < max_exact, n, large)


def masked_softmax(s, mask):
    s = jnp.where(mask, s.astype(jnp.float32), NEG_INF)
    return jnp.where(mask, jax.nn.softmax(s, axis=-1), 0.0)


def query_blocks(a, blk):
    B, S = a.shape[:2]
    return jnp.moveaxis(a.reshape(B, S // blk, blk, *a.shape[2:]), 1, 0)


def merge_blocks(o):
    o = jnp.moveaxis(o, 0, 1)
    return o.reshape(o.shape[0], -1, *o.shape[3:])


def diff_attention(h, w_in, lam, subln_g, w_out, rel_bias, lam_init):
    B, S, _ = h.shape
    q, k, v = jnp.split(h @ w_in, 3, axis=-1)
    q = q.reshape(B, S, DA_HEADS, 2, HEAD_DIM)
    k = k.reshape(B, S, DA_HEADS, 2, HEAD_DIM)
    v = v.reshape(B, S, DA_HEADS, 2 * HEAD_DIM)
    lf = lam.astype(jnp.float32)
    lam_full = jnp.exp(jnp.sum(lf[0] * lf[1])) - jnp.exp(jnp.sum(lf[2] * lf[3])) + lam_init
    scale = HEAD_DIM ** -0.5
    key_pos = jnp.arange(S)

    def block(args):
        qb, q0 = args
        t = q0 + jnp.arange(Q_BLOCK)
        dist = t[:, None] - key_pos[None, :]
        mask = dist >= 0
        bias = rel_bias[t5_bucket(dist)].astype(jnp.float32)
        bias = jnp.moveaxis(bias, -1, 0).reshape(DA_HEADS, 2, Q_BLOCK, S)
        s = jnp.einsum('bqhcd,bkhcd->bhcqk', qb, k).astype(jnp.float32) * scale + bias[None]
        pr = masked_softmax(s, mask)
        a = pr[:, :, 0] - lam_full * pr[:, :, 1]
        return jnp.einsum('bhqk,bkhe->bqhe', a.astype(v.dtype), v)

    starts = jnp.arange(S // Q_BLOCK) * Q_BLOCK
    o = merge_blocks(lax.map(block, (query_blocks(q, Q_BLOCK), starts)))
    o = rmsnorm(o, subln_g) * (1.0 - lam_init)
    return o.reshape(B, S, D_MODEL) @ w_out


def nsa_attention(h, w_in, cmp_pe, cmp_w1, cmp_w2, w_out, rel_bias):
    B, S, _ = h.shape
    G, R, hd, Qc = NSA_KV_GROUPS, NSA_GROUP_SIZE, HEAD_DIM, NSA_Q_CHUNK
    proj = h @ w_in
    nq = NSA_HEADS * hd
    q = proj[..., :nq].reshape(B, S, G, R, hd)
    kv = proj[..., nq:nq + 6 * NSA_KV_DIM].reshape(B, S, 6, G, hd)
    k_cmp, v_cmp, k_sel, v_sel, k_win, v_win = [kv[:, :, i] for i in range(6)]
    gates = jax.nn.sigmoid(proj[..., nq + 6 * NSA_KV_DIM:].reshape(B, S, G, R, 3))
    scale = hd ** -0.5

    n_cmp = (S - NSA_CMP_LEN) // NSA_CMP_STRIDE + 1
    tok = jnp.arange(n_cmp)[:, None] * NSA_CMP_STRIDE + jnp.arange(NSA_CMP_LEN)[None, :]

    def compress(a, pe, w1, w2):
        blk = a[:, tok] + pe[None, None, :, None, :]
        blk = jnp.moveaxis(blk, 3, 2).reshape(B, n_cmp, G, NSA_CMP_LEN * hd)
        return jax.nn.gelu(blk @ w1) @ w2

    kc = compress(k_cmp, cmp_pe[0], cmp_w1[0], cmp_w2[0])
    vc = compress(v_cmp, cmp_pe[1], cmp_w1[1], cmp_w2[1])
    cmp_start = jnp.arange(n_cmp) * NSA_CMP_STRIDE
    cmp_end = cmp_start + NSA_CMP_LEN - 1

    n_sel_blk = S // NSA_SEL_LEN
    n_top = min(NSA_TOP_N, n_sel_blk)
    sel_start = jnp.arange(n_sel_blk) * NSA_SEL_LEN
    overlap = ((cmp_start[:, None] < sel_start[None, :] + NSA_SEL_LEN)
               & (cmp_end[:, None] >= sel_start[None, :])).astype(jnp.float32)
    ks_blk = jnp.moveaxis(k_sel.reshape(B, n_sel_blk, NSA_SEL_LEN, G, hd), 3, 1)
    vs_blk = jnp.moveaxis(v_sel.reshape(B, n_sel_blk, NSA_SEL_LEN, G, hd), 3, 1)

    pad = ((0, 0), (NSA_WINDOW, 0), (0, 0), (0, 0))
    kw_pad = jnp.pad(k_win, pad)
    vw_pad = jnp.pad(v_win, pad)

    rel = rel_bias.reshape(REL_BUCKETS, G, R)
    b_idx = jnp.arange(B)[:, None, None, None]
    g_idx = jnp.arange(G)[None, :, None, None]
    j_blk = jnp.arange(n_sel_blk)
    K_sel = n_top * NSA_SEL_LEN

    def chunk(args):
        qc, gc, q0 = args
        t = q0 + jnp.arange(Qc)
        d_c = t[:, None] - cmp_end[None, :]
        m_c = d_c >= 0
        b_c = jnp.moveaxis(rel[t5_bucket(d_c)], (2, 3), (0, 1)).astype(jnp.float32)
        s_c = jnp.einsum('bqgrd,bngd->bgrqn', qc, kc).astype(jnp.float32) * scale + b_c
        p_c = masked_softmax(s_c, m_c)
        o_c = jnp.einsum('bgrqn,bngd->bqgrd', p_c.astype(vc.dtype), vc)
        imp = jnp.einsum('bgrqn,nj->bgqj', p_c, overlap)
        cur = t // NSA_SEL_LEN
        forced = (j_blk[None, :] == 0) | (j_blk[None, :] == cur[:, None]) | (j_blk[None, :] == cur[:, None] - 1)
        valid = sel_start[None, :] <= t[:, None]
        imp = jnp.where(valid, jnp.where(forced, NSA_FORCE_SCORE, imp), -1.0)
        _, sel = lax.top_k(imp, n_top)
        k_g = ks_blk[b_idx, g_idx, sel].reshape(B, G, Qc, K_sel, hd)
        v_g = vs_blk[b_idx, g_idx, sel].reshape(B, G, Qc, K_sel, hd)
        pos = (sel[..., None] * NSA_SEL_LEN + jnp.arange(NSA_SEL_LEN)).reshape(B, G, Qc, K_sel)
        d_s = t[None, None, :, None] - pos
        m_s = (d_s >= 0)[:, :, None]
        b_s = jnp.moveaxis(rel[t5_bucket(d_s), g_idx], -1, 2).astype(jnp.float32)
        s_s = jnp.einsum('bqgrd,bgqkd->bgrqk', qc, k_g).astype(jnp.float32) * scale + b_s
        p_s = masked_softmax(s_s, m_s)
        o_s = jnp.einsum('bgrqk,bgqkd->bqgrd', p_s.astype(v_g.dtype), v_g)
        kw = lax.dynamic_slice_in_dim(kw_pad, q0, NSA_WINDOW + Qc, axis=1)
        vw = lax.dynamic_slice_in_dim(vw_pad, q0, NSA_WINDOW + Qc, axis=1)
        w_pos = q0 - NSA_WINDOW + jnp.arange(NSA_WINDOW + Qc)
        d_w = t[:, None] - w_pos[None, :]
        m_w = (d_w >= 0) & (d_w < NSA_WINDOW) & (w_pos[None, :] >= 0)
        b_w = jnp.moveaxis(rel[t5_bucket(d_w)], (2, 3), (0, 1)).astype(jnp.float32)
        s_w = jnp.einsum('bqgrd,bkgd->bgrqk', qc, kw).astype(jnp.float32) * scale + b_w
        p_w = masked_softmax(s_w, m_w)
        o_w = jnp.einsum('bgrqk,bkgd->bqgrd', p_w.astype(vw.dtype), vw)
        return gc[..., 0:1] * o_c + gc[..., 1:2] * o_s + gc[..., 2:3] * o_w

    starts = jnp.arange(S // Qc) * Qc
    o = merge_blocks(lax.map(chunk, (query_blocks(q, Qc), query_blocks(gates, Qc), starts)))
    return o.reshape(B, S, D_MODEL) @ w_out


def forgetting_attention(h, w_in, b_f, w_out):
    B, S, _ = h.shape
    proj = h @ w_in
    q = proj[..., :D_MODEL].reshape(B, S, FOX_HEADS, HEAD_DIM)
    k = proj[..., D_MODEL:2 * D_MODEL].reshape(B, S, FOX_HEADS, HEAD_DIM)
    v = proj[..., 2 * D_MODEL:3 * D_MODEL].reshape(B, S, FOX_HEADS, HEAD_DIM)
    log_f = jax.nn.log_sigmoid((proj[..., 3 * D_MODEL:] + b_f).astype(jnp.float32))
    c = jnp.cumsum(log_f, axis=1)
    c_k = jnp.moveaxis(c, 1, 2)
    scale = HEAD_DIM ** -0.5
    key_pos = jnp.arange(S)

    def block(args):
        qb, cq, q0 = args
        t = q0 + jnp.arange(Q_BLOCK)
        mask = t[:, None] >= key_pos[None, :]
        decay = jnp.moveaxis(cq, 1, 2)[..., None] - c_k[:, :, None, :]
        s = jnp.einsum('bqhd,bkhd->bhqk', qb, k).astype(jnp.float32) * scale + decay
        pr = masked_softmax(s, mask)
        return jnp.einsum('bhqk,bkhd->bqhd', pr.astype(v.dtype), v)

    starts = jnp.arange(S // Q_BLOCK) * Q_BLOCK
    o = merge_blocks(lax.map(block, (query_blocks(q, Q_BLOCK), query_blocks(c, Q_BLOCK), starts)))
    return o.reshape(B, S, D_MODEL) @ w_out


def sqrelu_mlp(h, w1, w2):
    return jnp.square(jax.nn.relu(h @ w1)) @ w2


def setup_inputs(seed: int = 0) -> dict:
    key = jax.random.key(seed)
    ks = list(jax.random.split(key, 24))

    def nrm(i, shape, scale):
        return jax.random.normal(ks[i], shape, jnp.float32) * scale

    D = D_MODEL
    return {
        'x': nrm(0, (BATCH, SEQ, D), 1.0),
        'p': nrm(1, (DEPTH, BATCH, SEQ, PLE_DIM), 1.0),
        'rel_bias': nrm(2, (REL_BUCKETS, REL_HEADS), 0.5),
        'norm_g': 1.0 + nrm(3, (DEPTH, 4, D), 0.02),
        'mlp_w1': nrm(4, (DEPTH, D, D_FF), D ** -0.5),
        'mlp_w2': nrm(5, (DEPTH, D_FF, D), D_FF ** -0.5),
        'ple_w': nrm(6, (DEPTH, PLE_DIM, D), PLE_DIM ** -0.5),
        'ple_gate_w': nrm(7, (DEPTH, D, D), D ** -0.5),
        'da_w_in': nrm(8, (N_DA, D, DA_IN), D ** -0.5),
        'da_lambda': nrm(9, (N_DA, 4, HEAD_DIM), 0.1),
        'da_subln': 1.0 + nrm(10, (N_DA, 2 * HEAD_DIM), 0.02),
        'da_w_out': nrm(11, (N_DA, D, D), D ** -0.5),
        'nsa_w_in': nrm(12, (N_NSA, D, NSA_IN), D ** -0.5),
        'nsa_cmp_pe': nrm(13, (N_NSA, 2, NSA_CMP_LEN, HEAD_DIM), 0.1),
        'nsa_cmp_w1': nrm(14, (N_NSA, 2, NSA_CMP_LEN * HEAD_DIM, NSA_CMP_HIDDEN), (NSA_CMP_LEN * HEAD_DIM) ** -0.5),
        'nsa_cmp_w2': nrm(15, (N_NSA, 2, NSA_CMP_HIDDEN, HEAD_DIM), NSA_CMP_HIDDEN ** -0.5),
        'nsa_w_out': nrm(16, (N_NSA, D, D), D ** -0.5),
        'fox_w_in': nrm(17, (N_FOX, D, FOX_IN), D ** -0.5),
        'fox_b_f': 3.0 + nrm(18, (N_FOX, FOX_HEADS), 1.0),
        'fox_w_out': nrm(19, (N_FOX, D, D), D ** -0.5),
    }


def reference(x, p, rel_bias, norm_g, mlp_w1, mlp_w2, ple_w, ple_gate_w,
              da_w_in, da_lambda, da_subln, da_w_out,
              nsa_w_in, nsa_cmp_pe, nsa_cmp_w1, nsa_cmp_w2, nsa_w_out,
              fox_w_in, fox_b_f, fox_w_out):
    ia, ib, ic = 0, 0, 0
    for i in range(DEPTH):
        g = norm_g[i]
        h = rmsnorm(x, g[0])
        kind = i % N_MIXERS
        if kind == 0:
            lam_init = 0.8 - 0.6 * math.exp(-0.3 * i)
            y = diff_attention(h, da_w_in[ia], da_lambda[ia], da_subln[ia], da_w_out[ia], rel_bias, lam_init)
            ia += 1
        elif kind == 1:
            y = nsa_attention(h, nsa_w_in[ib], nsa_cmp_pe[ib], nsa_cmp_w1[ib], nsa_cmp_w2[ib], nsa_w_out[ib], rel_bias)
            ib += 1
        else:
            y = forgetting_attention(h, fox_w_in[ic], fox_b_f[ic], fox_w_out[ic])
            ic += 1
        x = x + rmsnorm(y, g[1])
        y = sqrelu_mlp(rmsnorm(x, g[2]), mlp_w1[i], mlp_w2[i])
        x = x + rmsnorm(y, g[3])
        x = x + jax.nn.sigmoid(x @ ple_gate_w[i]) * (p[i] @ ple_w[i])
    return x
```

```python
import math
from contextlib import ExitStack
import numpy as np
import ml_dtypes
import concourse.bass as bass
import concourse.mybir as mybir
from concourse.bass_utils import run_bass_kernel_spmd

F32 = mybir.dt.float32
BF16 = mybir.dt.bfloat16
AF = mybir.ActivationFunctionType
ALU = mybir.AluOpType
NPBF = ml_dtypes.bfloat16

D = 1024
DFF = 4096
EPS = 1e-6
NEG = -30000.0
_STAGES = 'ame12rbcBdD'


class Prog:
    ENGS = ("pe", "act", "dve", "pool", "sp")

    def __init__(self, nc):
        self.nc = nc
        self.es = ExitStack()
        self.ops = {e: [] for e in self.ENGS}
        self.cnt = {}
        self.sems = {}
        self.waited = {e: {} for e in self.ENGS}
        self.W = {}
        self.R = {}
        for e in ("pe", "act", "dve", "pool"):
            self._sem("c_" + e)
        self.n_instr = 0
        self.rot = {}

    def _sem(self, name):
        if name not in self.sems:
            self.sems[name] = self.es.enter_context(self.nc.semaphore(name))
            self.cnt[name] = 0
        return self.sems[name]

    def sbuf(self, name, shape, dt):
        return self.es.enter_context(self.nc.sbuf_tensor("s_" + name, list(shape), dt))

    def psum(self, name, shape, dt=F32):
        return self.es.enter_context(self.nc.psum_tensor("p_" + name, list(shape), dt))

    def _deps(self, eng, reads, writes):
        deps = {}
        for k in reads:
            w = self.W.get(k)
            if w:
                deps[w[0]] = max(deps.get(w[0], 0), w[1])
        for k in writes:
            w = self.W.get(k)
            if w:
                deps[w[0]] = max(deps.get(w[0], 0), w[1])
            for r in self.R.get(k, ()):
                deps[r[0]] = max(deps.get(r[0], 0), r[1])
        for s, n in deps.items():
            if eng == "pe" and s == "c_pe":
                continue
            if s.startswith("d_"):
                n = self.cnt[s]
            if self.waited[eng].get(s, 0) >= n:
                continue
            self.waited[eng][s] = n
            mult = 16 if s.startswith("d_") else 1
            self.ops[eng].append(("wait", self.sems[s], n * mult))

    def _finish(self, semname, reads, writes):
        self.cnt[semname] += 1
        n = self.cnt[semname]
        for k in writes:
            self.W[k] = (semname, n)
            self.R[k] = []
        for k in reads:
            if k in writes:
                continue
            lst = self.R.setdefault(k, [])
            lst[:] = [r for r in lst if r[0] != semname]
            lst.append((semname, n))
        self.n_instr += 1
        return n

    def op(self, eng, fn, reads=(), writes=()):
        semname = "c_" + eng
        self._deps(eng, reads, writes)
        self.ops[eng].append(("op", fn, self.sems[semname], 1))
        return self._finish(semname, reads, writes)

    def dma(self, q, lane, out, in_, reads=(), writes=(), **kw):
        semname = "d_" + lane
        self._sem(semname)
        self._deps(q, reads, writes)
        self.ops[q].append(("op", lambda e: e.dma_start(out=out, in_=in_, **kw), self.sems[semname], 16))
        return self._finish(semname, reads, writes)

    def wait_all(self, eng):
        for s, c in self.cnt.items():
            if c == 0 or self.waited[eng].get(s, 0) >= c:
                continue
            self.waited[eng][s] = c
            mult = 16 if s.startswith("d_") else 1
            self.ops[eng].append(("wait", self.sems[s], c * mult))

    def nxt(self, name, n):
        i = self.rot.get(name, 0)
        self.rot[name] = i + 1
        return i % n

    def emit(self):
        nc = self.nc
        ops = self.ops

        def run(e, eo):
            for it in ops[e]:
                if it[0] == "wait":
                    eo.wait_ge(it[1], it[2])
                else:
                    it[1](eo).then_inc(it[2], it[3])

        with nc.Block() as block:
            @block.tensor
            def _(t):
                run("pe", t)

            @block.scalar
            def _(t):
                run("act", t)

            @block.vector
            def _(t):
                run("dve", t)

            @block.gpsimd
            def _(t):
                run("pool", t)

            @block.sync
            def _(t):
                run("sp", t)
        self.es.close()


def _rstd(P, ss, lnv, out, epsb, dim, rk, wk):
    P.op("act", lambda e: e.activation(out=lnv, in_=ss, func=AF.Ln, scale=1.0 / dim, bias=epsb),
         reads=list(rk) + ["epsb"], writes=[wk + "_ln"])
    P.op("act", lambda e: e.activation(out=out, in_=lnv, func=AF.Exp, scale=-0.5),
         reads=[wk + "_ln"], writes=[wk])


def build_token(TS, first):
    nc = bass.Bass("TRN2", target_bir_lowering=False)

    def din(name, shape, dt=F32):
        return nc.dram_tensor(name, list(shape), dt, kind="ExternalInput").ap()

    x = din("x", [TS, D])
    gn = din("gn", [D])
    identd = din("ident", [128, 128], BF16)
    hT = nc.dram_tensor("hT", [D, TS], BF16, kind="ExternalOutput").ap()
    if not first:
        oT = din("oT", [D, TS], BF16)
        pin = din("p", [TS, 256])
        w_out = din("w_out", [D, D])
        w1 = din("w1", [D, DFF])
        w2 = din("w2", [DFF, D])
        wp = din("ple_w", [256, D])
        wg = din("gate_w", [D, D])
        g3 = din("g", [3, D])
        xo = nc.dram_tensor("xo", [TS, D], F32, kind="ExternalOutput").ap()

    P = Prog(nc)
    ident = P.sbuf("ident", [128, 128], BF16)
    epsb = P.sbuf("epsb", [128, 1], F32)
    gnb = P.sbuf("gnb", [128, D], F32)
    xt = P.sbuf("xt", [128, 4, D], F32)
    actT = P.sbuf("actT", [128, 8, 512], BF16)
    hb = P.sbuf("hb", [128, D], BF16)
    junk = P.sbuf("junk", [128, D], BF16)
    st = P.sbuf("st", [128, 64], F32)
    tp = [P.psum("tp%d" % i, [128, 4, 128], BF16) for i in range(2)]
    P.dma("sp", "c0", ident[:], identd[:, :], writes=["ident"])
    P.dma("sp", "c1", gnb[:], gn.partition_broadcast(128), writes=["gnb"])
    P.op("dve", lambda e: e.memset(epsb[:], EPS), writes=["epsb"])
    if not first:
        gb = [P.sbuf("gb%d" % i, [128, D], F32) for i in range(3)]
        for i in range(3):
            P.dma("sp", "c2", gb[i][:], g3[i].partition_broadcast(128), writes=["gb%d" % i])
        yt = P.sbuf("yt", [128, 4, D], F32)
        oTs = P.sbuf("oTs", [128, 8, 512], BF16)
        aT = P.sbuf("aT", [128, 32, 512], BF16)
        wpool = [P.sbuf("wpool%d" % i, [128, 8, 512], BF16) for i in range(4)]
        wps = P.sbuf("wps", [128, 2, 512], BF16)
        pT = P.sbuf("pT", [128, 2, 512], BF16)
        pt = P.sbuf("pt", [128, 4, 256], F32)
        pb = P.sbuf("pb", [128, 256], BF16)
        sq = [P.sbuf("sq%d" % i, [128, 512], BF16) for i in range(2)]
        ev = [P.sbuf("ev%d" % i, [128, 512], F32) for i in range(2)]
        pm = [P.psum("pm%d" % i, [128, 512], F32) for i in range(6)]

    def wload(src3d, nchunk=8):
        i = P.nxt("wpool", 4)
        P.dma("pool", "w%d" % i, wpool[i][:, 0:nchunk, :], src3d, writes=["wpool%d" % i])
        return wpool[i], "wpool%d" % i

    def transposes(src_bf, srck, dst3, dstk, i, nchunk):
        for c0 in range(0, nchunk, 4):
            nn = min(4, nchunk - c0)
            b = P.nxt("tp", 2)
            for c in range(nn):
                P.op("pe", lambda e, c=c, b=b, c0=c0: e.transpose(out=tp[b][:, c, :], in_=src_bf[:, (c0 + c) * 128:(c0 + c + 1) * 128], identity=ident[:]),
                     reads=[srck, "ident"], writes=["tp%d" % b])
            eng = "act" if (P.nxt("tpe", 2) == 0) else "dve"
            if eng == "act":
                P.op("act", lambda e, b=b, c0=c0, nn=nn: e.copy(out=dst3[:, c0:c0 + nn, i * 128:(i + 1) * 128], in_=tp[b][:, 0:nn, :]),
                     reads=["tp%d" % b], writes=[dstk])
            else:
                P.op("dve", lambda e, b=b, c0=c0, nn=nn: e.tensor_copy(out=dst3[:, c0:c0 + nn, i * 128:(i + 1) * 128], in_=tp[b][:, 0:nn, :]),
                     reads=["tp%d" % b], writes=[dstk])

    def norm_to_bf(i, gtile, gk, sc):
        P.op("act", lambda e: e.activation(out=junk[:], in_=xt[:, i, :], func=AF.Square, accum_out=st[:, sc:sc + 1]),
             reads=["xt%d" % i], writes=["junk", "st%d" % sc])
        _rstd(P, st[:, sc:sc + 1], st[:, sc + 1:sc + 2], st[:, sc + 2:sc + 3], epsb[:], D, ["st%d" % sc], "st%d" % (sc + 2))
        P.op("dve", lambda e: e.scalar_tensor_tensor(out=hb[:], in0=xt[:, i, :], scalar=st[:, sc + 2:sc + 3], in1=gtile[:], op0=ALU.mult, op1=ALU.mult),
             reads=["xt%d" % i, "st%d" % (sc + 2), gk], writes=["hb"])

    def resid_norm(i, gtile, gk, sc):
        P.op("dve", lambda e: e.tensor_tensor(out=st[:, sc + 2:sc + 3], in0=st[:, sc:sc + 1], in1=st[:, sc + 1:sc + 2], op=ALU.add),
             reads=["st%d" % sc, "st%d" % (sc + 1)], writes=["st%d" % (sc + 2)])
        _rstd(P, st[:, sc + 2:sc + 3], st[:, sc + 3:sc + 4], st[:, sc + 4:sc + 5], epsb[:], D, ["st%d" % (sc + 2)], "st%d" % (sc + 4))
        P.op("dve", lambda e: e.scalar_tensor_tensor(out=yt[:, i, :], in0=yt[:, i, :], scalar=st[:, sc + 4:sc + 5], in1=gtile[:], op0=ALU.mult, op1=ALU.mult),
             reads=["st%d" % (sc + 4), gk], writes=["yt%d" % i])
        P.op("dve", lambda e: e.tensor_tensor(out=xt[:, i, :], in0=yt[:, i, :], in1=xt[:, i, :], op=ALU.add),
             reads=["yt%d" % i], writes=["xt%d" % i])

    def evac_y(bank, i, n, sc):
        b = P.nxt("sq", 2)
        P.op("dve", lambda e: e.tensor_copy(out=yt[:, i, n * 512:(n + 1) * 512], in_=pm[bank][:]),
             reads=["pm%d" % bank], writes=["yt%d" % i])
        P.op("act", lambda e: e.activation(out=sq[b][:], in_=yt[:, i, n * 512:(n + 1) * 512], func=AF.Square, accum_out=st[:, sc + n:sc + n + 1]),
             reads=["yt%d" % i], writes=["sq%d" % b, "st%d" % (sc + n)])

    for grp in range(TS // 512):
        t0 = grp * 512
        for i in range(4):
            P.dma("sp", "xt", xt[:, i, :], x[t0 + i * 128:t0 + (i + 1) * 128, :], writes=["xt%d" % i])
        if not first:
            P.dma("sp", "oTs", oTs[:], oT[:, t0:t0 + 512].rearrange("(c p) t -> p c t", p=128), writes=["oTs"])
            P.dma("sp", "pt", pt[:], pin[t0:t0 + 512, :].rearrange("(i p) d -> p i d", p=128), writes=["pt"])
            for n in range(2 if 'a' in _STAGES else 0):
                wt, wk = wload(w_out[:, n * 512:(n + 1) * 512].rearrange("(c p) f -> p c f", p=128))
                for i in range(4 if 'm' in _STAGES else 0):
                    bank = P.nxt("pm", 6)
                    for c in range(8):
                        P.op("pe", lambda e, c=c, i=i, wt=wt, bank=bank: e.matmul(pm[bank][:], lhsT=oTs[:, c, i * 128:(i + 1) * 128], rhs=wt[:, c, :], start=(c == 0), stop=(c == 7)),
                             reads=["oTs", wk], writes=["pm%d" % bank])
                    if 'e' in _STAGES:
                        evac_y(bank, i, n, 8 * i)
            for i in range(4 if 'r' in _STAGES else 0):
                resid_norm(i, gb[0], "gb0", 8 * i)
            for i in range(4 if 'b' in _STAGES else 0):
                norm_to_bf(i, gb[1], "gb1", 8 * i + 5)
                transposes(hb, "hb", actT, "actT", i, 8)
            for ffc in range(8 if 'c' in _STAGES else 0):
                wt, wk = wload(w1[:, ffc * 512:(ffc + 1) * 512].rearrange("(c p) f -> p c f", p=128))
                for sub in range(4):
                    bank = P.nxt("pm", 6)
                    for c in range(8):
                        P.op("pe", lambda e, c=c, sub=sub, wt=wt, bank=bank: e.matmul(pm[bank][:], lhsT=wt[:, c, sub * 128:(sub + 1) * 128], rhs=actT[:, c, :], start=(c == 0), stop=(c == 7)),
                             reads=["actT", wk], writes=["pm%d" % bank])
                    b = P.nxt("sq", 2)
                    s = ffc * 4 + sub
                    P.op("act", lambda e, b=b, bank=bank: e.activation(out=sq[b][:], in_=pm[bank][:], func=AF.Square),
                         reads=["pm%d" % bank], writes=["sq%d" % b])
                    P.op("dve", lambda e, b=b, bank=bank, s=s: e.scalar_tensor_tensor(out=aT[:, s, :], in0=pm[bank][:], scalar=0.0, in1=sq[b][:], op0=ALU.is_gt, op1=ALU.mult),
                         reads=["pm%d" % bank, "sq%d" % b], writes=["aT%d" % s])
            for n in range(2 if 'B' in _STAGES else 0):
                banks = [P.nxt("pm", 6) for _ in range(4)]
                for q in range(4):
                    wt, wk = wload(w2[q * 1024:(q + 1) * 1024, n * 512:(n + 1) * 512].rearrange("(s p) m -> p s m", p=128))
                    for i in range(4):
                        for s in range(8):
                            ss_ = q * 8 + s
                            P.op("pe", lambda e, i=i, s=s, ss_=ss_, wt=wt, bank=banks[i], q=q: e.matmul(pm[bank][:], lhsT=aT[:, ss_, i * 128:(i + 1) * 128], rhs=wt[:, s, :], start=(ss_ == 0), stop=(ss_ == 31)),
                                 reads=["aT%d" % ss_, wk], writes=["pm%d" % banks[i]])
                for i in range(4):
                    evac_y(banks[i], i, n, 8 * i)
            for i in range(4 if 'B' in _STAGES else 0):
                resid_norm(i, gb[2], "gb2", 8 * i)
            for i in range(4 if 'd' in _STAGES else 0):
                P.op("act", lambda e, i=i: e.copy(out=hb[:], in_=xt[:, i, :]), reads=["xt%d" % i], writes=["hb"])
                transposes(hb, "hb", actT, "actT", i, 8)
                P.op("dve", lambda e, i=i: e.tensor_copy(out=pb[:], in_=pt[:, i, :]), reads=["pt"], writes=["pb"])
                transposes(pb, "pb", pT, "pT", i, 2)
            for n in range(2 if 'D' in _STAGES else 0):
                wt, wk = wload(wg[:, n * 512:(n + 1) * 512].rearrange("(c p) f -> p c f", p=128))
                P.dma("pool", "wps", wps[:], wp[:, n * 512:(n + 1) * 512].rearrange("(c p) f -> p c f", p=128), writes=["wps"])
                for i in range(4):
                    bank = P.nxt("pm", 6)
                    for c in range(8):
                        P.op("pe", lambda e, c=c, i=i, wt=wt, bank=bank: e.matmul(pm[bank][:], lhsT=actT[:, c, i * 128:(i + 1) * 128], rhs=wt[:, c, :], start=(c == 0), stop=(c == 7)),
                             reads=["actT", wk], writes=["pm%d" % bank])
                    bank2 = P.nxt("pm", 6)
                    for c in range(2):
                        P.op("pe", lambda e, c=c, i=i, bank2=bank2: e.matmul(pm[bank2][:], lhsT=pT[:, c, i * 128:(i + 1) * 128], rhs=wps[:, c, :], start=(c == 0), stop=(c == 1)),
                             reads=["pT", "wps"], writes=["pm%d" % bank2])
                    b = P.nxt("ev", 2)
                    P.op("act", lambda e, b=b, bank=bank: e.activation(out=ev[b][:], in_=pm[bank][:], func=AF.Exp, scale=-1.0),
                         reads=["pm%d" % bank], writes=["ev%d" % b])
                    P.op("dve", lambda e, b=b: e.tensor_scalar_add(out=ev[b][:], in0=ev[b][:], scalar1=1.0), reads=[], writes=["ev%d" % b])
                    P.op("dve", lambda e, b=b: e.reciprocal(out=ev[b][:], in_=ev[b][:]), reads=[], writes=["ev%d" % b])
                    P.op("dve", lambda e, b=b, bank2=bank2: e.tensor_tensor(out=ev[b][:], in0=pm[bank2][:], in1=ev[b][:], op=ALU.mult),
                         reads=["pm%d" % bank2], writes=["ev%d" % b])
                    P.op("dve", lambda e, b=b, i=i, n=n: e.tensor_tensor(out=xt[:, i, n * 512:(n + 1) * 512], in0=xt[:, i, n * 512:(n + 1) * 512], in1=ev[b][:], op=ALU.add),
                         reads=["ev%d" % b], writes=["xt%d" % i])
            for i in range(4):
                P.dma("sp", "xo", xo[t0 + i * 128:t0 + (i + 1) * 128, :], xt[:, i, :], reads=["xt%d" % i], writes=["xo"])
        for i in range(4):
            norm_to_bf(i, gnb, "gnb", 8 * i + 5)
            transposes(hb, "hb", actT, "actT", i, 8)
        P.dma("sp", "hT", hT[:, t0:t0 + 512].rearrange("(c p) t -> p c t", p=128), actT[:], reads=["actT"], writes=["hTd"])
    P.wait_all("sp")
    P.emit()
    return nc


def build_head(kind, S, lam_init=0.0):
    nc = bass.Bass("TRN2", target_bir_lowering=False)
    NT = S // 512
    NKB = S // 128

    def din(name, shape, dt=F32):
        return nc.dram_tensor(name, list(shape), dt, kind="ExternalInput").ap()

    hT = din("hT", [D, S], BF16)
    identd = din("ident", [128, 128], BF16)
    wq = din("wq", [D, 256])
    wk = din("wk", [D, 256])
    wv = din("wv", [D, 256])
    oT = nc.dram_tensor("oT", [256, S], BF16, kind="ExternalOutput").ap()
    if kind == "da":
        tabd = din("tab", [4, 128, 1024], BF16)
        farbd = din("farb", [128, 4])
        lamd = din("lam", [256])
        sublnd = din("subln", [128])
        lamcd = din("lamc", [128, 2])
        KQ, NVH, DV, NTAB = 64, 2, 128, 4
    else:
        tabd = din("tab", [1, 128, 1024], BF16)
        wfd = din("wf", [D, 4])
        bfd = din("bf", [16])
        trid = din("tri", [128, 128])
        onesd = din("ones", [128, 128])
        KQ, NVH, DV, NTAB = 66, 4, 64, 1

    P = Prog(nc)
    ident = P.sbuf("ident", [128, 128], BF16)
    wqs = P.sbuf("wqs", [128, 8, 256], BF16)
    wks = P.sbuf("wks", [128, 8, 256], BF16)
    wvs = P.sbuf("wvs", [128, 8, 256], BF16)
    kT = [P.sbuf("kT%d" % h, [KQ, S], BF16) for h in range(4)]
    qT = [[P.sbuf("qT%d_%d" % (h, b), [KQ, 512], BF16) for b in range(2)] for h in range(4)]
    vext = P.sbuf("vext", [128, NKB, NVH, DV + 1], BF16)
    hts = [P.sbuf("hts%d" % b, [128, 8, 512], BF16) for b in range(2)]
    tabs = P.sbuf("tabs", [128, NTAB, 1024], BF16)
    pTb = [P.sbuf("pT%d" % i, [128, 512], BF16) for i in range(3)]
    small = P.sbuf("small", [128, 64], F32)
    epsb = P.sbuf("epsb", [128, 1], F32)
    ofin = P.sbuf("ofin", [128, 4, 256 if kind == "fox" else 128], BF16)
    oTs = P.sbuf("oTs", [128, 2, 512], BF16)
    gbank = [P.psum("g%d" % i, [128, 512], F32) for i in range(3)]
    if kind == "da":
        ob = [P.psum("ob%d" % i, [128, 2, DV + 1], F32) for i in range(4)]
    else:
        ob = [P.psum("ob%d" % i, [128, 4, DV + 1], F32) for i in range(3)]
        px = P.psum("px", [128, 3, 16], F32)
    tp = P.psum("tp", [128, 4, 128], BF16)

    P.dma("sp", "c0", ident[:], identd[:, :], writes=["ident"])
    P.dma("sp", "c1", tabs[:], tabd.rearrange("h p m -> p h m"), writes=["tabs"])
    P.dma("pool", "w0", wqs[:], wq.rearrange("(c p) f -> p c f", p=128), writes=["wqs"])
    P.dma("pool", "w1", wks[:], wk.rearrange("(c p) f -> p c f", p=128), writes=["wks"])
    P.dma("pool", "w2", wvs[:], wv.rearrange("(c p) f -> p c f", p=128), writes=["wvs"])
    P.op("dve", lambda e: e.memset(epsb[:], EPS), writes=["epsb"])
    P.op("pool", lambda e: e.memset(vext[:], 1.0), writes=["vext%d" % kb for kb in range(NKB)])
    smallc = [0]

    def scol(n=1):
        c = smallc[0]
        if c + n > 64:
            c = 0
        smallc[0] = c + n
        return c

    if kind == "da":
        farb = P.sbuf("farb", [128, 4], F32)
        lamb = P.sbuf("lamb", [128, 256], F32)
        gsub = P.sbuf("gsub", [128, 128], F32)
        ltmp = P.sbuf("ltmp", [128, 64], F32)
        lsm = P.sbuf("lsm", [128, 8], F32)
        on0 = P.sbuf("on0", [128, 4, 128], F32)
        comb = P.sbuf("comb", [128, 4, 128], F32)
        junk = P.sbuf("junk", [128, 128], BF16)
        ssb = P.sbuf("ssb", [128, 12], F32)
        P.dma("sp", "c2", farb[:], farbd[:, :], writes=["farb"])
        P.dma("sp", "c3", lamb[:], lamd.partition_broadcast(128), writes=["lamb"])
        P.dma("sp", "c4", gsub[:], sublnd.partition_broadcast(128), writes=["gsub"])
        lamc = P.sbuf("lamc", [128, 2], F32)
        P.dma("sp", "c5", lamc[:], lamcd[:, :], writes=["lamc"])
        P.op("dve", lambda e: e.tensor_scalar_mul(out=gsub[:], in0=gsub[:], scalar1=lamc[:, 0:1]), reads=["lamc"], writes=["gsub"])
        for q in range(2):
            P.op("dve", lambda e, q=q: e.tensor_tensor(out=ltmp[:], in0=lamb[:, q * 128:q * 128 + 64], in1=lamb[:, q * 128 + 64:q * 128 + 128], op=ALU.mult),
                 reads=["lamb"], writes=["ltmp"])
            P.op("dve", lambda e, q=q: e.reduce_sum(out=lsm[:, q:q + 1], in_=ltmp[:], axis=mybir.AxisListType.X),
                 reads=["ltmp"], writes=["lsm%d" % q])
        P.op("act", lambda e: e.activation(out=lsm[:, 2:4], in_=lsm[:, 0:2], func=AF.Exp), reads=["lsm0", "lsm1"], writes=["lsm23"])
        P.op("dve", lambda e: e.tensor_tensor(out=lsm[:, 4:5], in0=lsm[:, 3:4], in1=lsm[:, 2:3], op=ALU.subtract), reads=["lsm23"], writes=["lsm4"])
        P.op("dve", lambda e: e.tensor_tensor(out=lsm[:, 5:6], in0=lsm[:, 4:5], in1=lamc[:, 1:2], op=ALU.add), reads=["lsm4", "lamc"], writes=["nlam"])
        nlam = lsm[:, 5:6]
    else:
        wfs = P.sbuf("wfs", [128, 8, 4], BF16)
        bfb = P.sbuf("bfb", [128, 16], F32)
        tri = P.sbuf("tri", [128, 128], F32)
        onesm = P.sbuf("onesm", [128, 128], F32)
        ncall = P.sbuf("ncall", [128, NKB, 4], F32)
        ncb = P.sbuf("ncb", [128, NKB + 1, 4], F32)
        biasK = [P.sbuf("biasK%d" % i, [128, NKB, 4], F32) for i in range(2)]
        fu = P.sbuf("fu", [128, 16], F32)
        fsp = P.sbuf("fsp", [128, 16], F32)
        cwq = P.sbuf("cwq", [128, 4, 4], F32)
        chi = P.sbuf("chi", [128, 4, 4], BF16)
        chf = P.sbuf("chf", [128, 4, 4], F32)
        cwx = P.sbuf("cwx", [128, 4, 4, 66], BF16)
        P.dma("pool", "w3", wfs[:], wfd.rearrange("(c p) f -> p c f", p=128), writes=["wfs"])
        P.dma("sp", "c2", bfb[:], bfd.partition_broadcast(128), writes=["bfb"])
        P.dma("sp", "c3", tri[:], trid[:, :], writes=["tri"])
        P.dma("sp", "c4", onesm[:], onesd[:, :], writes=["onesm"])
        P.op("dve", lambda e: e.memset(cwx[:], 0.0), writes=["cwx"])
        P.op("dve", lambda e: e.memset(ncb[:, 0, :], 0.0), writes=["ncb0"])
        for h in range(4):
            P.op("dve", lambda e, h=h: e.memset(kT[h][64:66, :], 1.0), writes=["kT%d_ones" % h])

    def project(T):
        b = T % 2
        P.dma("sp", "hts%d" % b, hts[b][:], hT[:, T * 512:(T + 1) * 512].rearrange("(c p) t -> p c t", p=128), writes=["hts%d" % b])
        for h in range(4):
            g = P.nxt("g", 3)
            for c in range(8):
                P.op("pe", lambda e, c=c, h=h, g=g: e.matmul(gbank[g][0:64, :], lhsT=wqs[:, c, h * 64:(h + 1) * 64], rhs=hts[b][:, c, :], start=(c == 0), stop=(c == 7)),
                     reads=["wqs", "hts%d" % b], writes=["g%d" % g])
            P.op("dve", lambda e, h=h, g=g: e.tensor_scalar_mul(out=qT[h][b][0:64, :], in0=gbank[g][0:64, :], scalar1=0.125),
                 reads=["g%d" % g], writes=["qT%d_%d" % (h, b)])
            g = P.nxt("g", 3)
            for c in range(8):
                P.op("pe", lambda e, c=c, h=h, g=g: e.matmul(gbank[g][0:64, :], lhsT=wks[:, c, h * 64:(h + 1) * 64], rhs=hts[b][:, c, :], start=(c == 0), stop=(c == 7)),
                     reads=["wks", "hts%d" % b], writes=["g%d" % g])
            P.op("dve", lambda e, h=h, g=g: e.tensor_copy(out=kT[h][0:64, T * 512:(T + 1) * 512], in_=gbank[g][0:64, :]),
                 reads=["g%d" % g], writes=["kT%d_%d" % (h, T)])
        for i in range(4):
            g = P.nxt("g", 3)
            kb = 4 * T + i
            for c in range(8):
                P.op("pe", lambda e, c=c, i=i, g=g: e.matmul(gbank[g][:, 0:256], lhsT=hts[b][:, c, i * 128:(i + 1) * 128], rhs=wvs[:, c, :], start=(c == 0), stop=(c == 7)),
                     reads=["wvs", "hts%d" % b], writes=["g%d" % g])
            P.op("dve", lambda e, g=g, kb=kb: e.tensor_copy(out=vext[:, kb, :, 0:DV], in_=gbank[g][:, 0:256].rearrange("p (v d) -> p v d", v=NVH)),
                 reads=["g%d" % g], writes=["vext%d" % kb])
        if kind == "fox":
            for i in range(4):
                for c in range(8):
                    P.op("pe", lambda e, c=c, i=i: e.matmul(px[:, 0, i * 4:(i + 1) * 4], lhsT=hts[b][:, c, i * 128:(i + 1) * 128], rhs=wfs[:, c, :], start=(c == 0 and i == 0), stop=(c == 7), skip_group_check=True),
                         reads=["wfs", "hts%d" % b], writes=["px"])
            P.op("dve", lambda e: e.tensor_tensor(out=fu[:], in0=px[:, 0, :], in1=bfb[:], op=ALU.add), reads=["px", "bfb"], writes=["fu"])
            P.op("act", lambda e: e.activation(out=fu[:], in_=fu[:], func=AF.Exp, scale=-1.0), reads=[], writes=["fu"])
            P.op("act", lambda e: e.activation(out=fsp[:], in_=fu[:], func=AF.Ln, bias=1.0), reads=["fu"], writes=["fsp"])
            P.op("pe", lambda e: e.matmul(px[:, 1, :], lhsT=tri[:], rhs=fsp[:], start=True, stop=True), reads=["tri", "fsp"], writes=["px"])
            P.op("pe", lambda e: e.matmul(px[:, 2, :], lhsT=onesm[:], rhs=fsp[:], start=True, stop=True), reads=["onesm", "fsp"], writes=["px"])
            for i in range(4):
                kb = 4 * T + i
                P.op("dve", lambda e, i=i, kb=kb: e.tensor_tensor(out=ncall[:, kb, :], in0=px[:, 1, i * 4:(i + 1) * 4], in1=ncb[:, kb, :], op=ALU.add),
                     reads=["px", "ncb%d" % kb], writes=["ncall%d" % kb])
                P.op("dve", lambda e, i=i, kb=kb: e.tensor_tensor(out=ncb[:, kb + 1, :], in0=px[:, 2, i * 4:(i + 1) * 4], in1=ncb[:, kb, :], op=ALU.add),
                     reads=["px", "ncb%d" % kb], writes=["ncb%d" % (kb + 1)])
            bk = biasK[T % 2]
            for kb in range(4 * T + 4):
                P.op("dve", lambda e, kb=kb, bk=bk: e.tensor_tensor(out=bk[:, kb, :], in0=ncall[:, kb, :], in1=ncb[:, 4 * T, :], op=ALU.subtract),
                     reads=["ncall%d" % kb, "ncb%d" % (4 * T)], writes=["biasK%d_%d" % (T % 2, kb)])
            for i in range(4):
                P.op("dve", lambda e, i=i: e.tensor_tensor(out=cwq[:, i, :], in0=ncb[:, 4 * T, :], in1=ncall[:, 4 * T + i, :], op=ALU.subtract),
                     reads=["ncall%d" % (4 * T + i), "ncb%d" % (4 * T)], writes=["cwq"])
            P.op("dve", lambda e: e.tensor_copy(out=chi[:], in_=cwq[:]), reads=["cwq"], writes=["chi"])
            P.op("dve", lambda e: e.tensor_copy(out=chf[:], in_=chi[:]), reads=["chi"], writes=["chf"])
            P.op("dve", lambda e: e.tensor_tensor(out=chf[:], in0=cwq[:], in1=chf[:], op=ALU.subtract), reads=["cwq"], writes=["chf"])
            P.op("dve", lambda e: e.tensor_copy(out=cwx[:, :, :, 64], in_=chi[:]), reads=["chi"], writes=["cwx"])
            P.op("dve", lambda e: e.tensor_copy(out=cwx[:, :, :, 65], in_=chf[:]), reads=["chf"], writes=["cwx"])
            for h in range(4):
                g = P.nxt("g", 3)
                for i in range(4):
                    P.op("pe", lambda e, i=i, h=h, g=g: e.matmul(gbank[g][0:66, i * 128:(i + 1) * 128], lhsT=cwx[:, i, h, :], rhs=ident[:], start=True, stop=True),
                         reads=["cwx", "ident"], writes=["g%d" % g])
                P.op("dve", lambda e, h=h, g=g: e.tensor_copy(out=qT[h][b][64:66, :], in_=gbank[g][64:66, :]),
                     reads=["g%d" % g], writes=["qT%d_%d" % (h, b)])

    def attention(T, h, vh, obanks, oregion):
        b = T % 2
        nkb = 4 * T + 4
        pend = None

        def pv(kb, pb):
            for j in range(4):
                if kb > 4 * T + j:
                    continue
                oap, okey, bfirst = oregion(j)
                P.op("pe", lambda e, j=j, oap=oap, kb=kb, pb=pb, bfirst=bfirst: e.matmul(oap, lhsT=pTb[pb][:, j * 128:(j + 1) * 128], rhs=vext[:, kb, vh, :], start=(kb == 0 and bfirst), stop=(kb == 4 * T + j), skip_group_check=True),
                     reads=["pT%d" % pb, "vext%d" % kb], writes=[okey])

        for kb in range(nkb):
            a = T * 512 - kb * 128
            near = a < 256
            g = P.nxt("g", 3)
            kreads = ["kT%d_%d" % (h, kb // 4), "qT%d_%d" % (h, b)] + (["kT%d_ones" % h] if kind == "fox" else [])
            P.op("pe", lambda e, g=g, kb=kb, near=near: e.matmul(gbank[g][:], lhsT=kT[h][:, kb * 128:(kb + 1) * 128], rhs=qT[h][b][:, :], start=True, stop=not near),
                 reads=kreads, writes=["g%d" % g])
            if near:
                off = a + 384
                ti = h if kind == "da" else 0
                P.op("pe", lambda e, g=g, off=off, ti=ti: e.matmul(gbank[g][:], lhsT=ident[:], rhs=tabs[:, ti, off:off + 512], start=False, stop=True),
                     reads=["ident", "tabs"], writes=["g%d" % g])
            pb = P.nxt("pT", 3)
            if kind == "da":
                bias = 0.0 if near else farb[:, h:h + 1]
                br = [] if near else ["farb"]
            else:
                bias = biasK[T % 2][:, kb, h:h + 1]
                br = ["biasK%d_%d" % (T % 2, kb)]
            P.op("act", lambda e, g=g, pb=pb, bias=bias: e.activation(out=pTb[pb][:], in_=gbank[g][:], func=AF.Exp, bias=bias, scale=1.0),
                 reads=["g%d" % g] + br, writes=["pT%d" % pb])
            if pend is not None:
                pv(*pend)
            pend = (kb, pb)
        pv(*pend)

    def flush_out(T, nchunk):
        P.dma("sp", "oT", oT[:, T * 512:(T + 1) * 512].rearrange("(c p) t -> p c t", p=128), oTs[:, 0:nchunk, :], reads=["oTs"], writes=["oTd"])

    for T in range(NT):
        project(T)
        for h in range(4):
            if kind == "da":
                H, c = h // 2, h % 2
                s = (T * 4 + h) % 2
                okeys = ["ob%d" % (2 * s), "ob%d" % (2 * s + 1)]
                attention(T, h, H, None, lambda j: (ob[2 * s + j // 2][:, j % 2, :], okeys[j // 2], j % 2 == 0))
                for j in range(4):
                    Oj = ob[2 * s + j // 2][:, j % 2, :]
                    ok = okeys[j // 2]
                    cc = scol(2)
                    P.op("dve", lambda e, Oj=Oj, cc=cc: e.reciprocal(out=small[:, cc:cc + 1], in_=Oj[:, DV:DV + 1]), reads=[ok], writes=["sm%d" % cc])
                    if c == 0:
                        P.op("dve", lambda e, Oj=Oj, cc=cc, j=j: e.tensor_scalar_mul(out=on0[:, j, :], in0=Oj[:, 0:DV], scalar1=small[:, cc:cc + 1]),
                             reads=[ok, "sm%d" % cc], writes=["on0_%d" % j])
                    else:
                        P.op("dve", lambda e, cc=cc: e.tensor_tensor(out=small[:, cc + 1:cc + 2], in0=small[:, cc:cc + 1], in1=nlam, op=ALU.mult),
                             reads=["sm%d" % cc, "nlam"], writes=["sm%d" % (cc + 1)])
                        P.op("dve", lambda e, Oj=Oj, cc=cc, j=j: e.scalar_tensor_tensor(out=comb[:, j, :], in0=Oj[:, 0:DV], scalar=small[:, cc + 1:cc + 2], in1=on0[:, j, :], op0=ALU.mult, op1=ALU.add),
                             reads=[ok, "sm%d" % (cc + 1), "on0_%d" % j], writes=["comb%d" % j])
                        P.op("act", lambda e, j=j: e.activation(out=junk[:], in_=comb[:, j, :], func=AF.Square, accum_out=ssb[:, j:j + 1]),
                             reads=["comb%d" % j], writes=["junk", "ssb%d" % j])
                if c == 1:
                    _rstd(P, ssb[:, 0:4], ssb[:, 4:8], ssb[:, 8:12], epsb[:], DV, ["ssb%d" % j for j in range(4)], "rstd")
                    for j in range(4):
                        P.op("dve", lambda e, j=j: e.scalar_tensor_tensor(out=ofin[:, j, :], in0=comb[:, j, :], scalar=ssb[:, 8 + j:9 + j], in1=gsub[:], op0=ALU.mult, op1=ALU.mult),
                             reads=["comb%d" % j, "rstd", "gsub"], writes=["ofin%d" % j])
                    for j in range(4):
                        P.op("pe", lambda e, j=j: e.transpose(out=tp[:, j, :], in_=ofin[:, j, :], identity=ident[:]),
                             reads=["ofin%d" % j, "ident"], writes=["tp"])
                    P.op("dve", lambda e, H=H: e.tensor_copy(out=oTs[:, H, :].rearrange("p (j q) -> p j q", j=4), in_=tp[:]),
                         reads=["tp"], writes=["oTs"])
            else:
                bnk = (T * 4 + h) % 3
                attention(T, h, h, None, lambda j: (ob[bnk][:, j, :], "ob%d" % bnk, j == 0))
                for j in range(4):
                    cc = scol(1)
                    P.op("dve", lambda e, cc=cc, j=j, bnk=bnk: e.reciprocal(out=small[:, cc:cc + 1], in_=ob[bnk][:, j, DV:DV + 1]), reads=["ob%d" % bnk], writes=["sm%d" % cc])
                    P.op("dve", lambda e, cc=cc, j=j, h=h, bnk=bnk: e.tensor_scalar_mul(out=ofin[:, j, h * 64:(h + 1) * 64], in0=ob[bnk][:, j, 0:DV], scalar1=small[:, cc:cc + 1]),
                         reads=["ob%d" % bnk, "sm%d" % cc], writes=["ofin%d" % j])
                if h == 3:
                    for cch in range(2):
                        for j in range(4):
                            P.op("pe", lambda e, j=j, cch=cch: e.transpose(out=tp[:, j, :], in_=ofin[:, j, cch * 128:(cch + 1) * 128], identity=ident[:]),
                                 reads=["ofin%d" % j, "ident"], writes=["tp"])
                        P.op("dve", lambda e, cch=cch: e.tensor_copy(out=oTs[:, cch, :].rearrange("p (j q) -> p j q", j=4), in_=tp[:]),
                             reads=["tp"], writes=["oTs"])
        flush_out(T, 2)
    P.wait_all("sp")
    P.emit()
    return nc


def t5_bucket_np(d):
    n = np.maximum(d, 0)
    nf = np.maximum(n, 1).astype(np.float32)
    large = 16 + (np.log(nf / np.float32(16)) / np.float32(math.log(8.0)) * np.float32(16)).astype(np.int32)
    large = np.minimum(large, 31)
    return np.where(n < 16, n, large)


def toeplitz_idx(width, amin, dmax=None):
    i = np.arange(128)[:, None]
    m = np.arange(width)[None, :]
    d = m + amin - i
    idx = t5_bucket_np(d)
    bad = d < 0
    if dmax is not None:
        bad = bad | (d >= dmax)
    return np.where(bad, 32, idx)


def build_nsa(S):
    nc = bass.Bass("TRN2", target_bir_lowering=False)
    NT = S // 512
    NKB = S // 128
    NCB = 4 if S >= 8192 else max(1, (S // 16 + 127) // 128)
    NCP = NCB * 128

    def din(name, shape, dt=F32):
        return nc.dram_tensor(name, list(shape), dt, kind="ExternalInput").ap()

    hT = din("hT", [D, S], BF16)
    identd = din("ident", [128, 128], BF16)
    wq = din("wq", [D, 256])
    wkv = din("wkv", [D, 384])
    wgt = din("wgt", [D, 12])
    peTd = din("peT", [64, 2, 32])
    cw1d = din("cw1", [64, 2, 32, 256])
    cw2d = din("cw2", [128, 2, 2, 64])
    tseld = din("tsel", [4, 128, 1024], BF16)
    tcmpd = din("tcmp", [4, 128, 2560], BF16)
    twind = din("twin", [4, 128, 1408], BF16)
    farbd = din("farb", [128, 4])
    gseld = din("gsel", [128, S], BF16)
    ovld = din("ovl", [128, NCB, 128], BF16)
    mMd = din("mM", [S, 128])
    mACd = din("mAC", [S, 128])
    oT = nc.dram_tensor("oT", [256, S], BF16, kind="ExternalOutput").ap()

    P = Prog(nc)
    ident = P.sbuf("ident", [128, 128], BF16)
    wqs = P.sbuf("wqs", [128, 8, 256], BF16)
    wkvs = P.sbuf("wkvs", [128, 8, 384], BF16)
    wgs = P.sbuf("wgs", [128, 8, 12], BF16)
    peT = P.sbuf("peT", [64, 2, 32], BF16)
    cw1 = P.sbuf("cw1", [64, 2, 32, 256], BF16)
    cw2 = P.sbuf("cw2", [128, 2, 2, 64], BF16)
    hbias = P.sbuf("hbias", [128, 4], F32)
    tsel = P.sbuf("tsel", [128, 4, 1024], BF16)
    tcmp = P.sbuf("tcmp", [128, 4, 2560], BF16)
    twin = P.sbuf("twin", [128, 4, 1408], BF16)
    farb = P.sbuf("farb", [128, 4], F32)
    gsel = P.sbuf("gsel", [128, S], BF16)
    kselT = P.sbuf("kselT", [64, S], BF16)
    kwinT = P.sbuf("kwinT", [64, S], BF16)
    vsw = P.sbuf("vsw", [128, NKB, 2, 65], BF16)
    cwin = [P.sbuf("cwin%d" % b, [64, 2, 528], BF16) for b in range(2)]
    kcT = P.sbuf("kcT", [64, NCP], BF16)
    vcT = P.sbuf("vcT", [64, NCP], BF16)
    vcx = P.sbuf("vcx", [128, NCB, 193], BF16)
    qT = [[P.sbuf("qT%d_%d" % (h, b), [64, 512], BF16) for b in range(2)] for h in range(4)]
    hts = [P.sbuf("hts%d" % b, [128, 8, 512], BF16) for b in range(2)]
    pTb = [P.sbuf("pT%d" % i, [128, 512], BF16) for i in range(3)]
    small = P.sbuf("small", [128, 64], F32)
    gates = P.sbuf("gates", [128, 4, 12], F32)
    ofin32 = P.sbuf("ofin32", [128, 4, 256], F32)
    ofin = P.sbuf("ofin", [128, 4, 256], BF16)
    oTs = P.sbuf("oTs", [128, 2, 512], BF16)
    impacc = P.sbuf("impacc", [128, 4, 128], F32)
    mM = P.sbuf("mM", [128, 4, 128], F32)
    mAC = P.sbuf("mAC", [128, 4, 128], F32)
    imp2 = P.sbuf("imp2", [128, 128], F32)
    m8 = P.sbuf("m8", [128, 16], F32)
    selb = P.sbuf("selb", [128, 128], BF16)
    selbT = P.sbuf("selbT", [128, 512], BF16)
    gh = P.sbuf("gh", [128, 2, 32], F32)
    gt = P.sbuf("gt", [128, 2, 32], F32)
    ghb = P.sbuf("ghb", [128, 2, 32], BF16)
    gbank = [P.psum("g%d" % i, [128, 512], F32) for i in range(3)]
    ob = [P.psum("ob%d" % i, [128, 512], F32) for i in range(3)]
    tp = P.psum("tp", [128, 4, 128], BF16)
    px = P.psum("px", [128, 128], F32)

    P.dma("sp", "c0", ident[:], identd[:, :], writes=["ident"])
    P.dma("sp", "c1", tsel[:], tseld.rearrange("h p m -> p h m"), writes=["tsel"])
    P.dma("sp", "c2", tcmp[:], tcmpd.rearrange("h p m -> p h m"), writes=["tcmp"])
    P.dma("sp", "c3", twin[:], twind.rearrange("h p m -> p h m"), writes=["twin"])
    P.dma("sp", "c4", farb[:], farbd[:, :], writes=["farb"])
    P.dma("sp", "c5", gsel[:], gseld[:, :], writes=["gsel"])
    P.dma("pool", "w0", wqs[:], wq.rearrange("(c p) f -> p c f", p=128), writes=["wqs"])
    P.dma("pool", "w1", wkvs[:], wkv.rearrange("(c p) f -> p c f", p=128), writes=["wkvs"])
    P.dma("pool", "w2", wgs[:], wgt.rearrange("(c p) f -> p c f", p=128), writes=["wgs"])
    P.dma("pool", "w3", peT[:], peTd[:, :, :], writes=["peT"])
    P.dma("pool", "w4", cw1[:], cw1d[:, :, :, :], writes=["cw1"])
    P.dma("pool", "w5", cw2[:], cw2d[:, :, :, :], writes=["cw2"])
    P.op("pool", lambda e: e.memset(vsw[:], 1.0), writes=["vsw%d" % kb for kb in range(NKB)])
    P.op("pool", lambda e: e.memset(vcx[:], 1.0), writes=["vcx"])
    P.dma("sp", "c6", vcx[:, :, 65:193], ovld[:, :, :], writes=["vcx"])
    P.op("dve", lambda e: e.memset(kcT[:], 0.0), writes=["kcT"])
    P.op("dve", lambda e: e.memset(vcT[:], 0.0), writes=["vcT"])
    P.op("dve", lambda e: e.memset(cwin[1][:], 0.0), writes=["cwin1"])
    for kv in range(2):
        for hc in range(2):
            col = kv * 2 + hc
            for l in range(32):
                P.op("pe", lambda e, kv=kv, hc=hc, l=l, col=col: e.matmul(px[:, col:col + 1], lhsT=cw1[:, kv, l, hc * 128:(hc + 1) * 128], rhs=peT[:, kv, l:l + 1], start=(l == 0 and col == 0), stop=(l == 31), skip_group_check=True),
                     reads=["cw1", "peT"], writes=["px"])
    P.op("dve", lambda e: e.tensor_copy(out=hbias[:], in_=px[:, 0:4]), reads=["px"], writes=["hbias"])
    smallc = [0]

    def scol(n=1):
        c = smallc[0]
        if c + n > 64:
            c = 0
        smallc[0] = c + n
        return c

    def project(T):
        b = T % 2
        P.dma("sp", "hts%d" % b, hts[b][:], hT[:, T * 512:(T + 1) * 512].rearrange("(c p) t -> p c t", p=128), writes=["hts%d" % b])
        P.dma("sp", "mM", mM[:], mMd[T * 512:(T + 1) * 512, :].rearrange("(j p) n -> p j n", p=128), writes=["mM"])
        P.dma("sp", "mAC", mAC[:], mACd[T * 512:(T + 1) * 512, :].rearrange("(j p) n -> p j n", p=128), writes=["mAC"])
        for h in range(4):
            g = P.nxt("g", 3)
            for c in range(8):
                P.op("pe", lambda e, c=c, h=h, g=g: e.matmul(gbank[g][0:64, :], lhsT=wqs[:, c, h * 64:(h + 1) * 64], rhs=hts[b][:, c, :], start=(c == 0), stop=(c == 7)),
                     reads=["wqs", "hts%d" % b], writes=["g%d" % g])
            P.op("dve", lambda e, h=h, g=g: e.tensor_scalar_mul(out=qT[h][b][:, :], in0=gbank[g][0:64, :], scalar1=0.125),
                 reads=["g%d" % g], writes=["qT%d_%d" % (h, b)])
        if T > 0:
            P.op("dve", lambda e: e.tensor_copy(out=cwin[b][:, :, 0:16], in_=cwin[1 - b][:, :, 512:528]), reads=["cwin%d" % (1 - b)], writes=["cwin%d" % b])
        dsts = [(cwin[b][:, 0, 16:528], "cwin%d" % b), (cwin[b][:, 1, 16:528], "cwin%d" % b),
                (kselT[:, T * 512:(T + 1) * 512], "kselT%d" % T), (kwinT[:, T * 512:(T + 1) * 512], "kwinT%d" % T)]
        for q in range(4):
            g = P.nxt("g", 3)
            for c in range(8):
                P.op("pe", lambda e, c=c, q=q, g=g: e.matmul(gbank[g][0:64, :], lhsT=wkvs[:, c, q * 64:(q + 1) * 64], rhs=hts[b][:, c, :], start=(c == 0), stop=(c == 7)),
                     reads=["wkvs", "hts%d" % b], writes=["g%d" % g])
            dst, dk = dsts[q]
            P.op("dve", lambda e, dst=dst, g=g: e.tensor_copy(out=dst, in_=gbank[g][0:64, :]), reads=["g%d" % g], writes=[dk])
        for i in range(4):
            g = P.nxt("g", 3)
            kb = 4 * T + i
            for c in range(8):
                P.op("pe", lambda e, c=c, i=i, g=g: e.matmul(gbank[g][:, 0:128], lhsT=hts[b][:, c, i * 128:(i + 1) * 128], rhs=wkvs[:, c, 256:384], start=(c == 0), stop=(c == 7)),
                     reads=["wkvs", "hts%d" % b], writes=["g%d" % g])
            P.op("dve", lambda e, g=g, kb=kb: e.tensor_copy(out=vsw[:, kb, :, 0:64], in_=gbank[g][:, 0:128].rearrange("p (v d) -> p v d", v=2)),
                 reads=["g%d" % g], writes=["vsw%d" % kb])
        for i in range(4):
            for c in range(8):
                P.op("pe", lambda e, c=c, i=i: e.matmul(px[:, i * 12:(i + 1) * 12], lhsT=hts[b][:, c, i * 128:(i + 1) * 128], rhs=wgs[:, c, :], start=(c == 0 and i == 0), stop=(c == 7), skip_group_check=True),
                     reads=["wgs", "hts%d" % b], writes=["px"])
        P.op("act", lambda e: e.activation(out=gates[:].rearrange("p i c -> p (i c)"), in_=px[:, 0:48], func=AF.Exp, scale=-1.0), reads=["px"], writes=["gates"])
        P.op("dve", lambda e: e.tensor_scalar_add(out=gates[:], in0=gates[:], scalar1=1.0), reads=[], writes=["gates"])
        P.op("dve", lambda e: e.reciprocal(out=gates[:], in_=gates[:]), reads=[], writes=["gates"])

    def compress(T):
        b = T % 2
        u0 = 1 if T == 0 else 0
        NU = 32 - u0
        n0 = 32 * T - 1 + u0
        for kv in range(2):
            for hc in range(2):
                g = P.nxt("g", 3)
                for l in range(32):
                    c0 = 16 * u0 + l
                    P.op("pe", lambda e, kv=kv, hc=hc, l=l, c0=c0, g=g: e.matmul(gbank[g][:, 0:NU], lhsT=cw1[:, kv, l, hc * 128:(hc + 1) * 128], rhs=cwin[b][:, kv, c0:c0 + 16 * (NU - 1) + 1:16], start=(l == 0), stop=(l == 31)),
                         reads=["cw1", "cwin%d" % b], writes=["g%d" % g])
                P.op("dve", lambda e, kv=kv, hc=hc, g=g: e.tensor_scalar_add(out=gh[:, hc, 0:NU], in0=gbank[g][:, 0:NU], scalar1=hbias[:, kv * 2 + hc:kv * 2 + hc + 1]),
                     reads=["g%d" % g, "hbias"], writes=["gh%d" % hc])
                P.op("dve", lambda e, hc=hc: e.tensor_tensor(out=gt[:, hc, 0:NU], in0=gh[:, hc, 0:NU], in1=gh[:, hc, 0:NU], op=ALU.mult), reads=["gh%d" % hc], writes=["gt%d" % hc])
                P.op("dve", lambda e, hc=hc: e.tensor_scalar(out=gt[:, hc, 0:NU], in0=gt[:, hc, 0:NU], scalar1=0.044715, scalar2=1.0, op0=ALU.mult, op1=ALU.add), reads=[], writes=["gt%d" % hc])
                P.op("dve", lambda e, hc=hc: e.tensor_tensor(out=gt[:, hc, 0:NU], in0=gt[:, hc, 0:NU], in1=gh[:, hc, 0:NU], op=ALU.mult), reads=["gh%d" % hc], writes=["gt%d" % hc])
                P.op("act", lambda e, hc=hc: e.activation(out=gt[:, hc, 0:NU], in_=gt[:, hc, 0:NU], func=AF.Exp, scale=-1.5957691216057308), reads=[], writes=["gt%d" % hc])
                P.op("dve", lambda e, hc=hc: e.tensor_scalar_add(out=gt[:, hc, 0:NU], in0=gt[:, hc, 0:NU], scalar1=1.0), reads=[], writes=["gt%d" % hc])
                P.op("dve", lambda e, hc=hc: e.reciprocal(out=gt[:, hc, 0:NU], in_=gt[:, hc, 0:NU]), reads=[], writes=["gt%d" % hc])
                P.op("dve", lambda e, hc=hc: e.tensor_tensor(out=ghb[:, hc, 0:NU], in0=gt[:, hc, 0:NU], in1=gh[:, hc, 0:NU], op=ALU.mult), reads=["gt%d" % hc, "gh%d" % hc], writes=["ghb%d" % hc])
            g = P.nxt("g", 3)
            for hc in range(2):
                P.op("pe", lambda e, kv=kv, hc=hc, g=g: e.matmul(gbank[g][0:64, 0:NU], lhsT=cw2[:, kv, hc, :], rhs=ghb[:, hc, 0:NU], start=(hc == 0), stop=(hc == 1)),
                     reads=["cw2", "ghb%d" % hc], writes=["g%d" % g])
            dstT = kcT if kv == 0 else vcT
            P.op("dve", lambda e, g=g, dstT=dstT: e.tensor_copy(out=dstT[:, n0:n0 + NU], in_=gbank[g][0:64, 0:NU]),
                 reads=["g%d" % g], writes=["kcT" if kv == 0 else "vcT"])
        for nb in sorted(set([n0 // 128, (n0 + NU - 1) // 128])):
            P.op("pe", lambda e, nb=nb: e.transpose(out=tp[0:128, 0, 0:64], in_=vcT[:, nb * 128:(nb + 1) * 128], identity=ident[0:64, 0:64]),
                 reads=["vcT", "ident"], writes=["tp"])
            P.op("dve", lambda e, nb=nb: e.tensor_copy(out=vcx[:, nb, 0:64], in_=tp[:, 0, 0:64]), reads=["tp"], writes=["vcx"])

    def attention(T, h, kblist, qk, extra, biasf, vr, first, last, oregion):
        b = T % 2
        pend = None

        def pv(kb, pb):
            rhs, rk = vr(kb)
            for j in range(4):
                if kb < first(j) or kb > last(j):
                    continue
                oap, okey, bfirst = oregion(j)
                st_ = bool(kb == first(j) and bfirst)
                sp_ = bool(kb == last(j))
                P.op("pe", lambda e, j=j, oap=oap, kb=kb, pb=pb, st_=st_, sp_=sp_, rhs=rhs: e.matmul(oap, lhsT=pTb[pb][:, j * 128:(j + 1) * 128], rhs=rhs, start=st_, stop=sp_, skip_group_check=True),
                     reads=["pT%d" % pb] + rk, writes=[okey])

        for kb in kblist:
            g = P.nxt("g", 3)
            lhsT, kreads = qk(kb)
            ex = extra(kb)
            P.op("pe", lambda e, g=g, lhsT=lhsT, ex=ex: e.matmul(gbank[g][:], lhsT=lhsT, rhs=qT[h][b][:, :], start=True, stop=(len(ex) == 0)),
                 reads=kreads + ["qT%d_%d" % (h, b)], writes=["g%d" % g])
            for xi, (xl, xr, xk) in enumerate(ex):
                P.op("pe", lambda e, g=g, xl=xl, xr=xr, xi=xi, ex=ex: e.matmul(gbank[g][:], lhsT=xl, rhs=xr, start=False, stop=(xi == len(ex) - 1)),
                     reads=xk, writes=["g%d" % g])
            pb = P.nxt("pT", 3)
            bias, br = biasf(kb)
            P.op("act", lambda e, g=g, pb=pb, bias=bias: e.activation(out=pTb[pb][:], in_=gbank[g][:], func=AF.Exp, bias=bias, scale=1.0),
                 reads=["g%d" % g] + br, writes=["pT%d" % pb])
            if pend is not None:
                pv(*pend)
            pend = (kb, pb)
        pv(*pend)

    for T in range(NT):
        b = T % 2
        project(T)
        compress(T)
        nbs = [nb for nb in range(NCB) if T - 4 * nb >= 0]
        for r in range(4):
            bA = P.nxt("ob", 3)
            bB = P.nxt("ob", 3)
            bks = [bA, bB]

            def c_qk(nb):
                return kcT[:, nb * 128:(nb + 1) * 128], ["kcT"]

            def c_extra(nb, r=r):
                dl = T - 4 * nb
                if dl >= 5:
                    return []
                return [(ident[:], tcmp[:, r, 512 * dl:512 * dl + 512], ["ident", "tcmp"])]

            def c_bias(nb, r=r):
                if T - 4 * nb >= 5:
                    return farb[:, r:r + 1], ["farb"]
                return 0.0, []

            attention(T, r, nbs, c_qk, c_extra, c_bias, lambda nb: (vcx[:, nb, :], ["vcx"]),
                      lambda j: nbs[0], lambda j: nbs[-1],
                      lambda j, bks=bks: (ob[bks[j // 2]][:, (j % 2) * 193:(j % 2) * 193 + 193], "ob%d" % bks[j // 2], j % 2 == 0))
            for j in range(4):
                Oj = ob[bks[j // 2]][:, (j % 2) * 193:(j % 2) * 193 + 193]
                ok = "ob%d" % bks[j // 2]
                cc = scol(3)
                P.op("dve", lambda e, Oj=Oj, cc=cc: e.tensor_scalar_max(out=small[:, cc:cc + 1], in0=Oj[:, 64:65], scalar1=1e-30), reads=[ok], writes=["sm%d" % cc])
                P.op("dve", lambda e, cc=cc: e.reciprocal(out=small[:, cc + 1:cc + 2], in_=small[:, cc:cc + 1]), reads=["sm%d" % cc], writes=["sm%d" % (cc + 1)])
                P.op("dve", lambda e, cc=cc, j=j, r=r: e.tensor_tensor(out=small[:, cc + 2:cc + 3], in0=small[:, cc + 1:cc + 2], in1=gates[:, j, r * 3:r * 3 + 1], op=ALU.mult),
                     reads=["sm%d" % (cc + 1), "gates"], writes=["sm%d" % (cc + 2)])
                P.op("dve", lambda e, Oj=Oj, cc=cc, j=j, r=r: e.tensor_scalar_mul(out=ofin32[:, j, r * 64:(r + 1) * 64], in0=Oj[:, 0:64], scalar1=small[:, cc + 2:cc + 3]),
                     reads=[ok, "sm%d" % (cc + 2)], writes=["ofin32_%d_%d" % (j, r)])
                if r == 0:
                    P.op("dve", lambda e, Oj=Oj, cc=cc, j=j: e.tensor_scalar_mul(out=impacc[:, j, :], in0=Oj[:, 65:193], scalar1=small[:, cc + 1:cc + 2]),
                         reads=[ok, "sm%d" % (cc + 1)], writes=["imp%d" % j])
                else:
                    P.op("dve", lambda e, Oj=Oj, cc=cc, j=j: e.scalar_tensor_tensor(out=impacc[:, j, :], in0=Oj[:, 65:193], scalar=small[:, cc + 1:cc + 2], in1=impacc[:, j, :], op0=ALU.mult, op1=ALU.add),
                         reads=[ok, "sm%d" % (cc + 1)], writes=["imp%d" % j])
        for j in range(4):
            P.op("dve", lambda e, j=j: e.tensor_tensor(out=imp2[:], in0=impacc[:, j, :], in1=mM[:, j, :], op=ALU.mult), reads=["imp%d" % j, "mM"], writes=["imp2"])
            P.op("dve", lambda e, j=j: e.tensor_tensor(out=imp2[:], in0=imp2[:], in1=mAC[:, j, :], op=ALU.add), reads=["mAC"], writes=["imp2"])
            P.op("dve", lambda e: e.max(out=m8[:, 0:8], in_=imp2[:]), reads=["imp2"], writes=["m8a"])
            P.op("dve", lambda e, j=j: e.match_replace(out=impacc[:, j, :], in_to_replace=m8[:, 0:8], in_values=imp2[:], imm_value=-1e30), reads=["imp2", "m8a"], writes=["imp%d" % j])
            P.op("dve", lambda e, j=j: e.max(out=m8[:, 8:16], in_=impacc[:, j, :]), reads=["imp%d" % j], writes=["m8b"])
            P.op("dve", lambda e: e.tensor_scalar(out=selb[:], in0=imp2[:], scalar1=m8[:, 15:16], scalar2=NEG, op0=ALU.is_lt, op1=ALU.mult), reads=["imp2", "m8b"], writes=["selb"])
            P.op("pe", lambda e, j=j: e.transpose(out=tp[:, j, :], in_=selb[:], identity=ident[:]), reads=["selb", "ident"], writes=["tp"])
        P.op("dve", lambda e: e.tensor_copy(out=selbT[:].rearrange("p (j q) -> p j q", j=4), in_=tp[:]), reads=["tp"], writes=["selbT"])
        for br_i, (kT_, tab_, name) in enumerate(((kselT, tsel, "sel"), (kwinT, twin, "win"))):
            for r in range(4):
                bnk = P.nxt("ob", 3)
                if name == "sel":
                    kbl = list(range(4 * T + 4))
                    fst = lambda j: 0
                else:
                    kbl = list(range(max(0, 4 * T - 4), 4 * T + 4))
                    fst = lambda j: max(0, 4 * T + j - 4)

                def s_qk(kb, kT_=kT_, name=name):
                    return kT_[:, kb * 128:(kb + 1) * 128], ["k%sT%d" % (name, kb // 4)]

                def s_extra(kb, r=r, name=name, tab_=tab_):
                    a = T * 512 - kb * 128
                    ex = []
                    if name == "win" or a < 256:
                        ex.append((ident[:], tab_[:, r, a + 384:a + 384 + 512], ["ident", "t" + name]))
                    if name == "sel":
                        ex.append((gsel[:, kb * 128:(kb + 1) * 128], selbT[:], ["gsel", "selbT"]))
                    return ex

                def s_bias(kb, r=r, name=name):
                    a = T * 512 - kb * 128
                    if name == "sel" and a >= 256:
                        return farb[:, r:r + 1], ["farb"]
                    return 0.0, []

                attention(T, r, kbl, s_qk, s_extra, s_bias, lambda kb, br_i=br_i: (vsw[:, kb, br_i, :], ["vsw%d" % kb]),
                          fst, lambda j: 4 * T + j,
                          lambda j, bnk=bnk: (ob[bnk][:, j * 65:(j + 1) * 65], "ob%d" % bnk, j == 0))
                for j in range(4):
                    cc = scol(2)
                    Oj = ob[bnk][:, j * 65:(j + 1) * 65]
                    P.op("dve", lambda e, Oj=Oj, cc=cc: e.reciprocal(out=small[:, cc:cc + 1], in_=Oj[:, 64:65]), reads=["ob%d" % bnk], writes=["sm%d" % cc])
                    P.op("dve", lambda e, cc=cc, j=j, r=r, br_i=br_i: e.tensor_tensor(out=small[:, cc + 1:cc + 2], in0=small[:, cc:cc + 1], in1=gates[:, j, r * 3 + 1 + br_i:r * 3 + 2 + br_i], op=ALU.mult),
                         reads=["sm%d" % cc, "gates"], writes=["sm%d" % (cc + 1)])
                    P.op("dve", lambda e, Oj=Oj, cc=cc, j=j, r=r: e.scalar_tensor_tensor(out=ofin32[:, j, r * 64:(r + 1) * 64], in0=Oj[:, 0:64], scalar=small[:, cc + 1:cc + 2], in1=ofin32[:, j, r * 64:(r + 1) * 64], op0=ALU.mult, op1=ALU.add),
                         reads=["ob%d" % bnk, "sm%d" % (cc + 1)], writes=["ofin32_%d_%d" % (j, r)])
        for j in range(4):
            P.op("dve", lambda e, j=j: e.tensor_copy(out=ofin[:, j, :], in_=ofin32[:, j, :]), reads=["ofin32_%d_%d" % (j, r) for r in range(4)], writes=["ofin%d" % j])
        for cch in range(2):
            for j in range(4):
                P.op("pe", lambda e, j=j, cch=cch: e.transpose(out=tp[:, j, :], in_=ofin[:, j, cch * 128:(cch + 1) * 128], identity=ident[:]),
                     reads=["ofin%d" % j, "ident"], writes=["tp"])
            P.op("dve", lambda e, cch=cch: e.tensor_copy(out=oTs[:, cch, :].rearrange("p (j q) -> p j q", j=4), in_=tp[:]), reads=["tp"], writes=["oTs"])
        P.dma("sp", "oT", oT[:, T * 512:(T + 1) * 512].rearrange("(c p) t -> p c t", p=128), oTs[:], reads=["oTs"], writes=["oTd"])
    P.wait_all("sp")
    P.emit()
    return nc


def nsa_consts(S):
    n_sel = S // 64
    ncb = 4 if S >= 8192 else max(1, (S // 16 + 127) // 128)
    n_cmp = (S - 32) // 16 + 1
    j = np.arange(128)[:, None]
    m = np.arange(S)[None, :]
    gsel = ((m // 64) == j).astype(np.float32)
    n = np.arange(ncb * 128)[:, None]
    jb = np.arange(128)[None, :]
    cs = n * 16
    ce = cs + 31
    ovl = ((cs < jb * 64 + 64) & (ce >= jb * 64) & (n < n_cmp) & (jb < n_sel)).astype(np.float32)
    ovl = ovl.reshape(ncb, 128, 128).transpose(1, 0, 2)
    t = np.arange(S)[:, None]
    cur = t // 64
    valid = (jb * 64 <= t) & (jb < n_sel)
    f0 = (jb == 0)
    f1 = (jb == cur)
    f2 = (jb == cur - 1)
    forced = f0 | f1 | f2
    mM = (valid & ~forced).astype(np.float32)
    fv = np.where(f2, 3e4, np.where(f1, 2e4, 1e4)).astype(np.float32)
    mAC = np.where(valid, np.where(forced, fv, 0.0), -1.0).astype(np.float32)
    i = np.arange(128)[:, None]
    mm = np.arange(2560)[None, :]
    d = mm - 16 * i - 31
    idx_cmp = np.where(d < 0, 32, t5_bucket_np(d))
    return dict(gsel=gsel.astype(NPBF), ovl=ovl.astype(NPBF), mM=mM, mAC=mAC, idx_cmp=idx_cmp,
                idx_sel=toeplitz_idx(1024, -384), idx_win=toeplitz_idx(1408, -384, 512))


_PROGS = {}


def _prog(key, fn):
    if key not in _PROGS:
        _PROGS[key] = fn()
    return _PROGS[key]


def _run(nc, in_maps):
    return run_bass_kernel_spmd(nc, in_maps, core_ids=list(range(8))).results


def kernel(x, p, rel_bias, norm_g, mlp_w1, mlp_w2, ple_w, ple_gate_w,
           da_w_in, da_lambda, da_subln, da_w_out,
           nsa_w_in, nsa_cmp_pe, nsa_cmp_w1, nsa_cmp_w2, nsa_w_out,
           fox_w_in, fox_b_f, fox_w_out):
    f32 = lambda a: np.ascontiguousarray(np.asarray(a, dtype=np.float32))
    x = f32(x)
    B, S, _ = x.shape
    TS = (B * S) // 8
    QS = S // TS
    depth = norm_g.shape[0]
    p = f32(p); rel_bias = f32(rel_bias); norm_g = f32(norm_g)
    ident = np.eye(128, dtype=np.float32).astype(NPBF)
    rel_ext = np.concatenate([rel_bias, np.full((1, 16), NEG, np.float32)], 0)
    idx_da = toeplitz_idx(1024, -384)
    xs = x.reshape(B * S, D)
    xsh = [np.ascontiguousarray(xs[c * TS:(c + 1) * TS]) for c in range(8)]

    nc0 = _prog(("tok", TS, True), lambda: build_token(TS, True))
    res = _run(nc0, [{"x": xsh[c], "gn": f32(norm_g[0, 0]), "ident": ident} for c in range(8)])
    hTs = [res[c]["hT"] for c in range(8)]
    ia = ib = ic = 0
    for i in range(depth):
        hTb = [np.ascontiguousarray(np.concatenate(hTs[b * QS:(b + 1) * QS], axis=1)) for b in range(B)]
        kind = i % 3
        ims = []
        if kind == 0:
            lam_init = 0.8 - 0.6 * math.exp(-0.3 * i)
            w_in = f32(da_w_in[ia]); w_o = f32(da_w_out[ia])
            lamc = np.empty((128, 2), np.float32); lamc[:, 0] = 1.0 - lam_init; lamc[:, 1] = -lam_init
            for b in range(B):
                for hp in range(4):
                    cols = [4 * hp + s_ for s_ in range(4)]
                    ims.append({"hT": hTb[b], "ident": ident,
                                "wq": np.ascontiguousarray(w_in[:, 256 * hp:256 * hp + 256]),
                                "wk": np.ascontiguousarray(w_in[:, 1024 + 256 * hp:1024 + 256 * hp + 256]),
                                "wv": np.ascontiguousarray(w_in[:, 2048 + 256 * hp:2048 + 256 * hp + 256]),
                                "tab": np.stack([rel_ext[idx_da, c] for c in cols], 0).astype(NPBF),
                                "farb": np.ascontiguousarray(np.broadcast_to(rel_bias[31, cols][None, :], (128, 4))),
                                "lam": f32(da_lambda[ia]).reshape(-1), "subln": f32(da_subln[ia]), "lamc": lamc})
            nch = _prog(("da", S), lambda: build_head("da", S))
            ia += 1
        elif kind == 1:
            w_in = f32(nsa_w_in[ib]); w_o = f32(nsa_w_out[ib])
            C = _prog(("nsac", S), lambda: nsa_consts(S))
            pe_ = f32(nsa_cmp_pe[ib]); c1 = f32(nsa_cmp_w1[ib]); c2 = f32(nsa_cmp_w2[ib])
            peT = np.ascontiguousarray(pe_.transpose(2, 0, 1))
            cw1 = np.ascontiguousarray(c1.reshape(2, 32, 64, 256).transpose(2, 0, 1, 3))
            cw2 = np.ascontiguousarray(c2.reshape(2, 2, 128, 64).transpose(2, 0, 1, 3))
            for b in range(B):
                for g in range(4):
                    cols = [4 * g + r for r in range(4)]
                    kvc = lambda k_: w_in[:, 1024 + k_ * 256 + g * 64: 1024 + k_ * 256 + g * 64 + 64]
                    ims.append({"hT": hTb[b], "ident": ident,
                                "wq": np.ascontiguousarray(w_in[:, g * 256:(g + 1) * 256]),
                                "wkv": np.ascontiguousarray(np.concatenate([kvc(0), kvc(1), kvc(2), kvc(4), kvc(3), kvc(5)], 1)),
                                "wgt": np.ascontiguousarray(w_in[:, 2560 + g * 12: 2560 + g * 12 + 12]),
                                "peT": peT, "cw1": cw1, "cw2": cw2,
                                "tsel": np.stack([rel_ext[C["idx_sel"], c] for c in cols], 0).astype(NPBF),
                                "tcmp": np.stack([rel_ext[C["idx_cmp"], c] for c in cols], 0).astype(NPBF),
                                "twin": np.stack([rel_ext[C["idx_win"], c] for c in cols], 0).astype(NPBF),
                                "farb": np.ascontiguousarray(np.broadcast_to(rel_bias[31, cols][None, :], (128, 4))),
                                "gsel": C["gsel"], "ovl": C["ovl"], "mM": C["mM"], "mAC": C["mAC"]})
            nch = _prog(("nsa", S), lambda: build_nsa(S))
            ib += 1
        else:
            w_in = f32(fox_w_in[ic]); w_o = f32(fox_w_out[ic]); b_f = f32(fox_b_f[ic])
            mask_ext = np.concatenate([np.zeros((32,), np.float32), np.full((1,), NEG, np.float32)], 0)
            tab = mask_ext[idx_da][None].astype(NPBF)
            tri = np.triu(np.ones((128, 128), np.float32))
            ones = np.ones((128, 128), np.float32)
            for b in range(B):
                for hg in range(4):
                    ims.append({"hT": hTb[b], "ident": ident,
                                "wq": np.ascontiguousarray(w_in[:, 256 * hg:256 * hg + 256]),
                                "wk": np.ascontiguousarray(w_in[:, 1024 + 256 * hg:1024 + 256 * hg + 256]),
                                "wv": np.ascontiguousarray(w_in[:, 2048 + 256 * hg:2048 + 256 * hg + 256]),
                                "tab": tab, "wf": np.ascontiguousarray(w_in[:, 3072 + 4 * hg:3072 + 4 * hg + 4]),
                                "bf": np.ascontiguousarray(np.tile(b_f[4 * hg:4 * hg + 4], 4)), "tri": tri, "ones": ones})
            nch = _prog(("fox", S), lambda: build_head("fox", S))
            ic += 1
        res = _run(nch, ims)
        oTb = [np.concatenate([res[b * 4 + g]["oT"] for g in range(4)], axis=0) for b in range(B)]
        gn = f32(norm_g[i + 1, 0]) if i + 1 < depth else f32(norm_g[i, 0])
        ims = []
        for c in range(8):
            b, q = c // QS, c % QS
            ims.append({"x": xsh[c], "gn": gn, "ident": ident,
                        "oT": np.ascontiguousarray(oTb[b][:, q * TS:(q + 1) * TS]),
                        "p": np.ascontiguousarray(p[i].reshape(B * S, -1)[c * TS:(c + 1) * TS]),
                        "w_out": w_o, "w1": f32(mlp_w1[i]), "w2": f32(mlp_w2[i]),
                        "ple_w": f32(ple_w[i]), "gate_w": f32(ple_gate_w[i]), "g": f32(norm_g[i, 1:4])})
        nct = _prog(("tok", TS, False), lambda: build_token(TS, False))
        res = _run(nct, ims)
        xsh = [res[c]["xo"] for c in range(8)]
        hTs = [res[c]["hT"] for c in range(8)]
    return np.concatenate(xsh, axis=0).reshape(B, S, D).astype(np.float32)
```

```python
import math
from contextlib import ExitStack
import numpy as np
import ml_dtypes
import concourse.bass as bass
import concourse.mybir as mybir
from concourse.bass_utils import run_bass_kernel_spmd

F32 = mybir.dt.float32
BF16 = mybir.dt.bfloat16
AF = mybir.ActivationFunctionType
ALU = mybir.AluOpType
NPBF = ml_dtypes.bfloat16

D = 1024
DFF = 4096
EPS = 1e-6
NEG = -30000.0
_STAGES = 'ame12rbcBdD'
_WQ = ['pool']


class Prog:
    ENGS = ("pe", "act", "dve", "pool", "sp")

    def __init__(self, nc):
        self.nc = nc
        self.es = ExitStack()
        self.ops = {e: [] for e in self.ENGS}
        self.cnt = {}
        self.sems = {}
        self.waited = {e: {} for e in self.ENGS}
        self.W = {}
        self.R = {}
        for e in ("pe", "act", "dve", "pool"):
            self._sem("c_" + e)
        self.n_instr = 0
        self.rot = {}

    def _sem(self, name):
        if name not in self.sems:
            self.sems[name] = self.es.enter_context(self.nc.semaphore(name))
            self.cnt[name] = 0
        return self.sems[name]

    def sbuf(self, name, shape, dt):
        return self.es.enter_context(self.nc.sbuf_tensor("s_" + name, list(shape), dt))

    def psum(self, name, shape, dt=F32):
        return self.es.enter_context(self.nc.psum_tensor("p_" + name, list(shape), dt))

    def _deps(self, eng, reads, writes):
        deps = {}
        for k in reads:
            w = self.W.get(k)
            if w:
                deps[w[0]] = max(deps.get(w[0], 0), w[1])
        for k in writes:
            w = self.W.get(k)
            if w:
                deps[w[0]] = max(deps.get(w[0], 0), w[1])
            for r in self.R.get(k, ()):
                deps[r[0]] = max(deps.get(r[0], 0), r[1])
        for s, n in deps.items():
            if eng == "pe" and s == "c_pe":
                continue
            if s.startswith("d_"):
                n = self.cnt[s]
            if self.waited[eng].get(s, 0) >= n:
                continue
            self.waited[eng][s] = n
            mult = 16 if s.startswith("d_") else 1
            self.ops[eng].append(("wait", self.sems[s], n * mult))

    def _finish(self, semname, reads, writes, sig=True):
        if sig:
            self.cnt[semname] += 1
            n = self.cnt[semname]
        else:
            n = self.cnt[semname] + 1
        for k in writes:
            self.W[k] = (semname, n)
            self.R[k] = []
        for k in reads:
            if k in writes:
                continue
            lst = self.R.setdefault(k, [])
            lst[:] = [r for r in lst if r[0] != semname]
            lst.append((semname, n))
        self.n_instr += 1
        return n

    def op(self, eng, fn, reads=(), writes=(), sig=True):
        semname = "c_" + eng
        self._deps(eng, reads, writes)
        self.ops[eng].append(("op", fn, self.sems[semname], 1 if sig else 0))
        return self._finish(semname, reads, writes, sig)

    def dma(self, q, lane, out, in_, reads=(), writes=(), **kw):
        semname = "d_" + lane
        self._sem(semname)
        self._deps(q, reads, writes)
        self.ops[q].append(("op", lambda e: e.dma_start(out=out, in_=in_, **kw), self.sems[semname], 16))
        return self._finish(semname, reads, writes)

    def wait_all(self, eng):
        for s, c in self.cnt.items():
            if c == 0 or self.waited[eng].get(s, 0) >= c:
                continue
            self.waited[eng][s] = c
            mult = 16 if s.startswith("d_") else 1
            self.ops[eng].append(("wait", self.sems[s], c * mult))

    def nxt(self, name, n):
        i = self.rot.get(name, 0)
        self.rot[name] = i + 1
        return i % n

    def emit(self):
        nc = self.nc
        ops = self.ops

        def run(e, eo):
            for it in ops[e]:
                if it[0] == "wait":
                    eo.wait_ge(it[1], it[2])
                elif it[3] == 0:
                    it[1](eo)
                else:
                    it[1](eo).then_inc(it[2], it[3])

        with nc.Block() as block:
            @block.tensor
            def _(t):
                run("pe", t)

            @block.scalar
            def _(t):
                run("act", t)

            @block.vector
            def _(t):
                run("dve", t)

            @block.gpsimd
            def _(t):
                run("pool", t)

            @block.sync
            def _(t):
                run("sp", t)
        self.es.close()


def _rstd(P, ss, lnv, out, epsb, dim, rk, wk):
    P.op("act", lambda e: e.activation(out=lnv, in_=ss, func=AF.Ln, scale=1.0 / dim, bias=epsb),
         reads=list(rk) + ["epsb"], writes=[wk + "_ln"])
    P.op("act", lambda e: e.activation(out=out, in_=lnv, func=AF.Exp, scale=-0.5),
         reads=[wk + "_ln"], writes=[wk])


def build_token(TS, first):
    nc = bass.Bass("TRN2", target_bir_lowering=False)

    def din(name, shape, dt=F32):
        return nc.dram_tensor(name, list(shape), dt, kind="ExternalInput").ap()

    x = din("x", [TS, D])
    gn = din("gn", [D])
    identd = din("ident", [128, 128], BF16)
    hT = nc.dram_tensor("hT", [D, TS], BF16, kind="ExternalOutput").ap()
    if not first:
        oT = din("oT", [D, TS], BF16)
        pin = din("p", [TS, 256])
        w_out = din("w_out", [D, D])
        w1 = din("w1", [D, DFF])
        w2 = din("w2", [DFF, D])
        wp = din("ple_w", [256, D])
        wg = din("gate_w", [D, D])
        g3 = din("g", [3, D])
        xo = nc.dram_tensor("xo", [TS, D], F32, kind="ExternalOutput").ap()

    P = Prog(nc)
    ident = P.sbuf("ident", [128, 128], BF16)
    epsb = P.sbuf("epsb", [128, 1], F32)
    gnb = P.sbuf("gnb", [128, D], F32)
    xt = P.sbuf("xt", [128, 4, D], F32)
    actT = P.sbuf("actT", [128, 8, 512], BF16)
    hb = P.sbuf("hb", [128, D], BF16)
    junk = P.sbuf("junk", [128, D], BF16)
    st = P.sbuf("st", [128, 64], F32)
    tp = [P.psum("tp%d" % i, [128, 4, 128], BF16) for i in range(2)]
    P.dma("sp", "c0", ident[:], identd[:, :], writes=["ident"])
    P.dma("sp", "c1", gnb[:], gn.partition_broadcast(128), writes=["gnb"])
    P.op("dve", lambda e: e.memset(epsb[:], EPS), writes=["epsb"])
    if not first:
        gb = [P.sbuf("gb%d" % i, [128, D], F32) for i in range(3)]
        for i in range(3):
            P.dma("sp", "c2", gb[i][:], g3[i].partition_broadcast(128), writes=["gb%d" % i])
        yt = P.sbuf("yt", [128, 4, D], F32)
        oTs = P.sbuf("oTs", [128, 8, 512], BF16)
        aT = P.sbuf("aT", [128, 32, 512], BF16)
        wpool = [P.sbuf("wpool%d" % i, [128, 8, 512], BF16) for i in range(4)]
        wps = P.sbuf("wps", [128, 2, 512], BF16)
        pT = P.sbuf("pT", [128, 2, 512], BF16)
        pt = P.sbuf("pt", [128, 4, 256], F32)
        pb = P.sbuf("pb", [128, 256], BF16)
        sq = [P.sbuf("sq%d" % i, [128, 512], BF16) for i in range(2)]
        ev = [P.sbuf("ev%d" % i, [128, 512], F32) for i in range(2)]
        pm = [P.psum("pm%d" % i, [128, 512], F32) for i in range(6)]

    def wload(src3d, nchunk=8):
        i = P.nxt("wpool", 4)
        P.dma(_WQ[P.nxt("wq", len(_WQ))], "w%d" % i, wpool[i][:, 0:nchunk, :], src3d, writes=["wpool%d" % i])
        return wpool[i], "wpool%d" % i

    def transposes(src_bf, srck, dst3, dstk, i, nchunk):
        for c0 in range(0, nchunk, 4):
            nn = min(4, nchunk - c0)
            b = P.nxt("tp", 2)
            for c in range(nn):
                P.op("pe", lambda e, c=c, b=b, c0=c0: e.transpose(out=tp[b][:, c, :], in_=src_bf[:, (c0 + c) * 128:(c0 + c + 1) * 128], identity=ident[:]),
                     reads=[srck, "ident"], writes=["tp%d" % b])
            eng = "act" if (P.nxt("tpe", 2) == 0) else "dve"
            if eng == "act":
                P.op("act", lambda e, b=b, c0=c0, nn=nn: e.copy(out=dst3[:, c0:c0 + nn, i * 128:(i + 1) * 128], in_=tp[b][:, 0:nn, :]),
                     reads=["tp%d" % b], writes=[dstk])
            else:
                P.op("dve", lambda e, b=b, c0=c0, nn=nn: e.tensor_copy(out=dst3[:, c0:c0 + nn, i * 128:(i + 1) * 128], in_=tp[b][:, 0:nn, :]),
                     reads=["tp%d" % b], writes=[dstk])

    def norm_to_bf(i, gtile, gk, sc):
        P.op("act", lambda e: e.activation(out=junk[:], in_=xt[:, i, :], func=AF.Square, accum_out=st[:, sc:sc + 1]),
             reads=["xt%d" % i], writes=["junk", "st%d" % sc])
        _rstd(P, st[:, sc:sc + 1], st[:, sc + 1:sc + 2], st[:, sc + 2:sc + 3], epsb[:], D, ["st%d" % sc], "st%d" % (sc + 2))
        P.op("dve", lambda e: e.scalar_tensor_tensor(out=hb[:], in0=xt[:, i, :], scalar=st[:, sc + 2:sc + 3], in1=gtile[:], op0=ALU.mult, op1=ALU.mult),
             reads=["xt%d" % i, "st%d" % (sc + 2), gk], writes=["hb"])

    def resid_norm(i, gtile, gk, sc):
        P.op("dve", lambda e: e.tensor_tensor(out=st[:, sc + 2:sc + 3], in0=st[:, sc:sc + 1], in1=st[:, sc + 1:sc + 2], op=ALU.add),
             reads=["st%d" % sc, "st%d" % (sc + 1)], writes=["st%d" % (sc + 2)])
        _rstd(P, st[:, sc + 2:sc + 3], st[:, sc + 3:sc + 4], st[:, sc + 4:sc + 5], epsb[:], D, ["st%d" % (sc + 2)], "st%d" % (sc + 4))
        P.op("dve", lambda e: e.scalar_tensor_tensor(out=yt[:, i, :], in0=yt[:, i, :], scalar=st[:, sc + 4:sc + 5], in1=gtile[:], op0=ALU.mult, op1=ALU.mult),
             reads=["st%d" % (sc + 4), gk], writes=["yt%d" % i])
        P.op("dve", lambda e: e.tensor_tensor(out=xt[:, i, :], in0=yt[:, i, :], in1=xt[:, i, :], op=ALU.add),
             reads=["yt%d" % i], writes=["xt%d" % i])

    def evac_y(bank, i, n, sc):
        b = P.nxt("sq", 2)
        P.op("dve", lambda e: e.tensor_copy(out=yt[:, i, n * 512:(n + 1) * 512], in_=pm[bank][:]),
             reads=["pm%d" % bank], writes=["yt%d" % i])
        P.op("act", lambda e: e.activation(out=sq[b][:], in_=yt[:, i, n * 512:(n + 1) * 512], func=AF.Square, accum_out=st[:, sc + n:sc + n + 1]),
             reads=["yt%d" % i], writes=["sq%d" % b, "st%d" % (sc + n)])

    for grp in range(TS // 512):
        t0 = grp * 512
        for i in range(4):
            P.dma("sp", "xt", xt[:, i, :], x[t0 + i * 128:t0 + (i + 1) * 128, :], writes=["xt%d" % i])
        if not first:
            P.dma("sp", "oTs", oTs[:], oT[:, t0:t0 + 512].rearrange("(c p) t -> p c t", p=128), writes=["oTs"])
            P.dma("sp", "pt", pt[:], pin[t0:t0 + 512, :].rearrange("(i p) d -> p i d", p=128), writes=["pt"])
            for n in range(2 if 'a' in _STAGES else 0):
                wt, wk = wload(w_out[:, n * 512:(n + 1) * 512].rearrange("(c p) f -> p c f", p=128))
                for i in range(4 if 'm' in _STAGES else 0):
                    bank = P.nxt("pm", 6)
                    for c in range(8):
                        P.op("pe", lambda e, c=c, i=i, wt=wt, bank=bank: e.matmul(pm[bank][:], lhsT=oTs[:, c, i * 128:(i + 1) * 128], rhs=wt[:, c, :], start=(c == 0), stop=(c == 7)), sig=(c == 7),
                             reads=["oTs", wk], writes=["pm%d" % bank])
                    if 'e' in _STAGES:
                        evac_y(bank, i, n, 8 * i)
            for i in range(4 if 'r' in _STAGES else 0):
                resid_norm(i, gb[0], "gb0", 8 * i)
            for i in range(4 if 'b' in _STAGES else 0):
                norm_to_bf(i, gb[1], "gb1", 8 * i + 5)
                transposes(hb, "hb", actT, "actT", i, 8)
            for ffc in range(8 if 'c' in _STAGES else 0):
                wt, wk = wload(w1[:, ffc * 512:(ffc + 1) * 512].rearrange("(c p) f -> p c f", p=128))
                for sub in range(4):
                    bank = P.nxt("pm", 6)
                    for c in range(8):
                        P.op("pe", lambda e, c=c, sub=sub, wt=wt, bank=bank: e.matmul(pm[bank][:], lhsT=wt[:, c, sub * 128:(sub + 1) * 128], rhs=actT[:, c, :], start=(c == 0), stop=(c == 7)), sig=(c == 7),
                             reads=["actT", wk], writes=["pm%d" % bank])
                    b = P.nxt("sq", 2)
                    s = ffc * 4 + sub
                    P.op("act", lambda e, b=b, bank=bank: e.activation(out=sq[b][:], in_=pm[bank][:], func=AF.Square),
                         reads=["pm%d" % bank], writes=["sq%d" % b])
                    P.op("dve", lambda e, b=b, bank=bank, s=s: e.scalar_tensor_tensor(out=aT[:, s, :], in0=pm[bank][:], scalar=0.0, in1=sq[b][:], op0=ALU.is_gt, op1=ALU.mult),
                         reads=["pm%d" % bank, "sq%d" % b], writes=["aT%d" % s])
            for n in range(2 if 'B' in _STAGES else 0):
                banks = [P.nxt("pm", 6) for _ in range(4)]
                for q in range(4):
                    wt, wk = wload(w2[q * 1024:(q + 1) * 1024, n * 512:(n + 1) * 512].rearrange("(s p) m -> p s m", p=128))
                    for i in range(4):
                        for s in range(8):
                            ss_ = q * 8 + s
                            P.op("pe", lambda e, i=i, s=s, ss_=ss_, wt=wt, bank=banks[i], q=q: e.matmul(pm[bank][:], lhsT=aT[:, ss_, i * 128:(i + 1) * 128], rhs=wt[:, s, :], start=(ss_ == 0), stop=(ss_ == 31)),
                                 reads=["aT%d" % ss_, wk], writes=["pm%d" % banks[i]])
                for i in range(4):
                    evac_y(banks[i], i, n, 8 * i)
            for i in range(4 if 'B' in _STAGES else 0):
                resid_norm(i, gb[2], "gb2", 8 * i)
            for i in range(4 if 'd' in _STAGES else 0):
                P.op("act", lambda e, i=i: e.copy(out=hb[:], in_=xt[:, i, :]), reads=["xt%d" % i], writes=["hb"])
                transposes(hb, "hb", actT, "actT", i, 8)
                P.op("dve", lambda e, i=i: e.tensor_copy(out=pb[:], in_=pt[:, i, :]), reads=["pt"], writes=["pb"])
                transposes(pb, "pb", pT, "pT", i, 2)
            for n in range(2 if 'D' in _STAGES else 0):
                wt, wk = wload(wg[:, n * 512:(n + 1) * 512].rearrange("(c p) f -> p c f", p=128))
                P.dma("pool", "wps", wps[:], wp[:, n * 512:(n + 1) * 512].rearrange("(c p) f -> p c f", p=128), writes=["wps"])
                for i in range(4):
                    bank = P.nxt("pm", 6)
                    for c in range(8):
                        P.op("pe", lambda e, c=c, i=i, wt=wt, bank=bank: e.matmul(pm[bank][:], lhsT=actT[:, c, i * 128:(i + 1) * 128], rhs=wt[:, c, :], start=(c == 0), stop=(c == 7)), sig=(c == 7),
                             reads=["actT", wk], writes=["pm%d" % bank])
                    bank2 = P.nxt("pm", 6)
                    for c in range(2):
                        P.op("pe", lambda e, c=c, i=i, bank2=bank2: e.matmul(pm[bank2][:], lhsT=pT[:, c, i * 128:(i + 1) * 128], rhs=wps[:, c, :], start=(c == 0), stop=(c == 1)),
                             reads=["pT", "wps"], writes=["pm%d" % bank2])
                    b = P.nxt("ev", 2)
                    P.op("act", lambda e, b=b, bank=bank: e.activation(out=ev[b][:], in_=pm[bank][:], func=AF.Exp, scale=-1.0),
                         reads=["pm%d" % bank], writes=["ev%d" % b])
                    P.op("dve", lambda e, b=b: e.tensor_scalar_add(out=ev[b][:], in0=ev[b][:], scalar1=1.0), reads=[], writes=["ev%d" % b])
                    P.op("dve", lambda e, b=b: e.reciprocal(out=ev[b][:], in_=ev[b][:]), reads=[], writes=["ev%d" % b])
                    P.op("dve", lambda e, b=b, bank2=bank2: e.tensor_tensor(out=ev[b][:], in0=pm[bank2][:], in1=ev[b][:], op=ALU.mult),
                         reads=["pm%d" % bank2], writes=["ev%d" % b])
                    P.op("dve", lambda e, b=b, i=i, n=n: e.tensor_tensor(out=xt[:, i, n * 512:(n + 1) * 512], in0=xt[:, i, n * 512:(n + 1) * 512], in1=ev[b][:], op=ALU.add),
                         reads=["ev%d" % b], writes=["xt%d" % i])
            for i in range(4):
                P.dma("sp", "xo", xo[t0 + i * 128:t0 + (i + 1) * 128, :], xt[:, i, :], reads=["xt%d" % i], writes=["xo"])
        for i in range(4):
            norm_to_bf(i, gnb, "gnb", 8 * i + 5)
            transposes(hb, "hb", actT, "actT", i, 8)
        P.dma("sp", "hT", hT[:, t0:t0 + 512].rearrange("(c p) t -> p c t", p=128), actT[:], reads=["actT"], writes=["hTd"])
    P.wait_all("sp")
    P.emit()
    return nc


def build_head(kind, S, lam_init=0.0):
    nc = bass.Bass("TRN2", target_bir_lowering=False)
    NT = S // 512
    NKB = S // 128

    def din(name, shape, dt=F32):
        return nc.dram_tensor(name, list(shape), dt, kind="ExternalInput").ap()

    hT = din("hT", [D, S], BF16)
    identd = din("ident", [128, 128], BF16)
    wq = din("wq", [D, 256])
    wk = din("wk", [D, 256])
    wv = din("wv", [D, 256])
    oT = nc.dram_tensor("oT", [256, S], BF16, kind="ExternalOutput").ap()
    if kind == "da":
        tabd = din("tab", [4, 128, 1024], BF16)
        farbd = din("farb", [128, 4])
        lamd = din("lam", [256])
        sublnd = din("subln", [128])
        lamcd = din("lamc", [128, 2])
        KQ, NVH, DV, NTAB = 64, 2, 128, 4
    else:
        tabd = din("tab", [1, 128, 1024], BF16)
        wfd = din("wf", [D, 4])
        bfd = din("bf", [16])
        trid = din("tri", [128, 128])
        onesd = din("ones", [128, 128])
        KQ, NVH, DV, NTAB = 66, 4, 64, 1

    P = Prog(nc)
    ident = P.sbuf("ident", [128, 128], BF16)
    wqs = P.sbuf("wqs", [128, 8, 256], BF16)
    wks = P.sbuf("wks", [128, 8, 256], BF16)
    wvs = P.sbuf("wvs", [128, 8, 256], BF16)
    kT = [P.sbuf("kT%d" % h, [KQ, S], BF16) for h in range(4)]
    qT = [[P.sbuf("qT%d_%d" % (h, b), [KQ, 512], BF16) for b in range(2)] for h in range(4)]
    vext = P.sbuf("vext", [128, NKB, NVH, DV + 1], BF16)
    hts = [P.sbuf("hts%d" % b, [128, 8, 512], BF16) for b in range(2)]
    tabs = P.sbuf("tabs", [128, NTAB, 1024], BF16)
    pTb = [P.sbuf("pT%d" % i, [128, 512], BF16) for i in range(4)]
    small = P.sbuf("small", [128, 64], F32)
    epsb = P.sbuf("epsb", [128, 1], F32)
    ofin = P.sbuf("ofin", [128, 4, 256 if kind == "fox" else 128], BF16)
    oTs = P.sbuf("oTs", [128, 2, 512], BF16)
    gbank = [P.psum("g%d" % i, [128, 512], F32) for i in range(3)]
    if kind == "da":
        ob = [P.psum("ob%d" % i, [128, 2, DV + 1], F32) for i in range(4)]
    else:
        ob = [P.psum("ob%d" % i, [128, 4, DV + 1], F32) for i in range(3)]
        px = P.psum("px", [128, 3, 16], F32)
    tp = P.psum("tp", [128, 4, 128], BF16)

    P.dma("sp", "c0", ident[:], identd[:, :], writes=["ident"])
    P.dma("sp", "c1", tabs[:], tabd.rearrange("h p m -> p h m"), writes=["tabs"])
    P.dma("pool", "w0", wqs[:], wq.rearrange("(c p) f -> p c f", p=128), writes=["wqs"])
    P.dma("pool", "w1", wks[:], wk.rearrange("(c p) f -> p c f", p=128), writes=["wks"])
    P.dma("pool", "w2", wvs[:], wv.rearrange("(c p) f -> p c f", p=128), writes=["wvs"])
    P.op("dve", lambda e: e.memset(epsb[:], EPS), writes=["epsb"])
    P.op("pool", lambda e: e.memset(vext[:], 1.0), writes=["vext%d" % kb for kb in range(NKB)])
    smallc = [0]

    def scol(n=1):
        c = smallc[0]
        if c + n > 64:
            c = 0
        smallc[0] = c + n
        return c

    if kind == "da":
        farb = P.sbuf("farb", [128, 4], F32)
        lamb = P.sbuf("lamb", [128, 256], F32)
        gsub = P.sbuf("gsub", [128, 128], F32)
        ltmp = P.sbuf("ltmp", [128, 64], F32)
        lsm = P.sbuf("lsm", [128, 8], F32)
        on0 = P.sbuf("on0", [128, 4, 128], F32)
        comb = P.sbuf("comb", [128, 4, 128], F32)
        junk = P.sbuf("junk", [128, 128], BF16)
        ssb = P.sbuf("ssb", [128, 12], F32)
        P.dma("sp", "c2", farb[:], farbd[:, :], writes=["farb"])
        P.dma("sp", "c3", lamb[:], lamd.partition_broadcast(128), writes=["lamb"])
        P.dma("sp", "c4", gsub[:], sublnd.partition_broadcast(128), writes=["gsub"])
        lamc = P.sbuf("lamc", [128, 2], F32)
        P.dma("sp", "c5", lamc[:], lamcd[:, :], writes=["lamc"])
        P.op("dve", lambda e: e.tensor_scalar_mul(out=gsub[:], in0=gsub[:], scalar1=lamc[:, 0:1]), reads=["lamc"], writes=["gsub"])
        for q in range(2):
            P.op("dve", lambda e, q=q: e.tensor_tensor(out=ltmp[:], in0=lamb[:, q * 128:q * 128 + 64], in1=lamb[:, q * 128 + 64:q * 128 + 128], op=ALU.mult),
                 reads=["lamb"], writes=["ltmp"])
            P.op("dve", lambda e, q=q: e.reduce_sum(out=lsm[:, q:q + 1], in_=ltmp[:], axis=mybir.AxisListType.X),
                 reads=["ltmp"], writes=["lsm%d" % q])
        P.op("act", lambda e: e.activation(out=lsm[:, 2:4], in_=lsm[:, 0:2], func=AF.Exp), reads=["lsm0", "lsm1"], writes=["lsm23"])
        P.op("dve", lambda e: e.tensor_tensor(out=lsm[:, 4:5], in0=lsm[:, 3:4], in1=lsm[:, 2:3], op=ALU.subtract), reads=["lsm23"], writes=["lsm4"])
        P.op("dve", lambda e: e.tensor_tensor(out=lsm[:, 5:6], in0=lsm[:, 4:5], in1=lamc[:, 1:2], op=ALU.add), reads=["lsm4", "lamc"], writes=["nlam"])
        nlam = lsm[:, 5:6]
    else:
        wfs = P.sbuf("wfs", [128, 8, 4], BF16)
        bfb = P.sbuf("bfb", [128, 16], F32)
        tri = P.sbuf("tri", [128, 128], F32)
        onesm = P.sbuf("onesm", [128, 128], F32)
        ncall = P.sbuf("ncall", [128, NKB, 4], F32)
        ncb = P.sbuf("ncb", [128, NKB + 1, 4], F32)
        biasK = [P.sbuf("biasK%d" % i, [128, NKB, 4], F32) for i in range(2)]
        fu = P.sbuf("fu", [128, 16], F32)
        fsp = P.sbuf("fsp", [128, 16], F32)
        cwq = P.sbuf("cwq", [128, 4, 4], F32)
        chi = P.sbuf("chi", [128, 4, 4], BF16)
        chf = P.sbuf("chf", [128, 4, 4], F32)
        cwx = P.sbuf("cwx", [128, 4, 4, 66], BF16)
        P.dma("pool", "w3", wfs[:], wfd.rearrange("(c p) f -> p c f", p=128), writes=["wfs"])
        P.dma("sp", "c2", bfb[:], bfd.partition_broadcast(128), writes=["bfb"])
        P.dma("sp", "c3", tri[:], trid[:, :], writes=["tri"])
        P.dma("sp", "c4", onesm[:], onesd[:, :], writes=["onesm"])
        P.op("dve", lambda e: e.memset(cwx[:], 0.0), writes=["cwx"])
        P.op("dve", lambda e: e.memset(ncb[:, 0, :], 0.0), writes=["ncb0"])
        for h in range(4):
            P.op("dve", lambda e, h=h: e.memset(kT[h][64:66, :], 1.0), writes=["kT%d_ones" % h])

    def project(T):
        b = T % 2
        P.dma("sp", "hts%d" % b, hts[b][:], hT[:, T * 512:(T + 1) * 512].rearrange("(c p) t -> p c t", p=128), writes=["hts%d" % b])
        for h in range(4):
            g = P.nxt("g", 3)
            for c in range(8):
                P.op("pe", lambda e, c=c, h=h, g=g: e.matmul(gbank[g][0:64, :], lhsT=wqs[:, c, h * 64:(h + 1) * 64], rhs=hts[b][:, c, :], start=(c == 0), stop=(c == 7)), sig=(c == 7),
                     reads=["wqs", "hts%d" % b], writes=["g%d" % g])
            P.op("dve", lambda e, h=h, g=g: e.tensor_scalar_mul(out=qT[h][b][0:64, :], in0=gbank[g][0:64, :], scalar1=0.125),
                 reads=["g%d" % g], writes=["qT%d_%d" % (h, b)])
            g = P.nxt("g", 3)
            for c in range(8):
                P.op("pe", lambda e, c=c, h=h, g=g: e.matmul(gbank[g][0:64, :], lhsT=wks[:, c, h * 64:(h + 1) * 64], rhs=hts[b][:, c, :], start=(c == 0), stop=(c == 7)), sig=(c == 7),
                     reads=["wks", "hts%d" % b], writes=["g%d" % g])
            P.op("dve", lambda e, h=h, g=g: e.tensor_copy(out=kT[h][0:64, T * 512:(T + 1) * 512], in_=gbank[g][0:64, :]),
                 reads=["g%d" % g], writes=["kT%d_%d" % (h, T)])
        for i in range(4):
            g = P.nxt("g", 3)
            kb = 4 * T + i
            for c in range(8):
                P.op("pe", lambda e, c=c, i=i, g=g: e.matmul(gbank[g][:, 0:256], lhsT=hts[b][:, c, i * 128:(i + 1) * 128], rhs=wvs[:, c, :], start=(c == 0), stop=(c == 7)), sig=(c == 7),
                     reads=["wvs", "hts%d" % b], writes=["g%d" % g])
            P.op("dve", lambda e, g=g, kb=kb: e.tensor_copy(out=vext[:, kb, :, 0:DV], in_=gbank[g][:, 0:256].rearrange("p (v d) -> p v d", v=NVH)),
                 reads=["g%d" % g], writes=["vext%d" % kb])
        if kind == "fox":
            for i in range(4):
                for c in range(8):
                    P.op("pe", lambda e, c=c, i=i: e.matmul(px[:, 0, i * 4:(i + 1) * 4], lhsT=hts[b][:, c, i * 128:(i + 1) * 128], rhs=wfs[:, c, :], start=(c == 0 and i == 0), stop=(c == 7), skip_group_check=True),
                         reads=["wfs", "hts%d" % b], writes=["px"])
            P.op("dve", lambda e: e.tensor_tensor(out=fu[:], in0=px[:, 0, :], in1=bfb[:], op=ALU.add), reads=["px", "bfb"], writes=["fu"])
            P.op("act", lambda e: e.activation(out=fu[:], in_=fu[:], func=AF.Exp, scale=-1.0), reads=[], writes=["fu"])
            P.op("act", lambda e: e.activation(out=fsp[:], in_=fu[:], func=AF.Ln, bias=1.0), reads=["fu"], writes=["fsp"])
            P.op("pe", lambda e: e.matmul(px[:, 1, :], lhsT=tri[:], rhs=fsp[:], start=True, stop=True), reads=["tri", "fsp"], writes=["px"])
            P.op("pe", lambda e: e.matmul(px[:, 2, :], lhsT=onesm[:], rhs=fsp[:], start=True, stop=True), reads=["onesm", "fsp"], writes=["px"])
            for i in range(4):
                kb = 4 * T + i
                P.op("dve", lambda e, i=i, kb=kb: e.tensor_tensor(out=ncall[:, kb, :], in0=px[:, 1, i * 4:(i + 1) * 4], in1=ncb[:, kb, :], op=ALU.add),
                     reads=["px", "ncb%d" % kb], writes=["ncall%d" % kb])
                P.op("dve", lambda e, i=i, kb=kb: e.tensor_tensor(out=ncb[:, kb + 1, :], in0=px[:, 2, i * 4:(i + 1) * 4], in1=ncb[:, kb, :], op=ALU.add),
                     reads=["px", "ncb%d" % kb], writes=["ncb%d" % (kb + 1)])
            bk = biasK[T % 2]
            for kb in range(4 * T + 4):
                P.op("dve", lambda e, kb=kb, bk=bk: e.tensor_tensor(out=bk[:, kb, :], in0=ncall[:, kb, :], in1=ncb[:, 4 * T, :], op=ALU.subtract),
                     reads=["ncall%d" % kb, "ncb%d" % (4 * T)], writes=["biasK%d_%d" % (T % 2, kb)])
            for i in range(4):
                P.op("dve", lambda e, i=i: e.tensor_tensor(out=cwq[:, i, :], in0=ncb[:, 4 * T, :], in1=ncall[:, 4 * T + i, :], op=ALU.subtract),
                     reads=["ncall%d" % (4 * T + i), "ncb%d" % (4 * T)], writes=["cwq"])
            P.op("dve", lambda e: e.tensor_copy(out=chi[:], in_=cwq[:]), reads=["cwq"], writes=["chi"])
            P.op("dve", lambda e: e.tensor_copy(out=chf[:], in_=chi[:]), reads=["chi"], writes=["chf"])
            P.op("dve", lambda e: e.tensor_tensor(out=chf[:], in0=cwq[:], in1=chf[:], op=ALU.subtract), reads=["cwq"], writes=["chf"])
            P.op("dve", lambda e: e.tensor_copy(out=cwx[:, :, :, 64], in_=chi[:]), reads=["chi"], writes=["cwx"])
            P.op("dve", lambda e: e.tensor_copy(out=cwx[:, :, :, 65], in_=chf[:]), reads=["chf"], writes=["cwx"])
            for h in range(4):
                g = P.nxt("g", 3)
                for i in range(4):
                    P.op("pe", lambda e, i=i, h=h, g=g: e.matmul(gbank[g][0:66, i * 128:(i + 1) * 128], lhsT=cwx[:, i, h, :], rhs=ident[:], start=True, stop=True),
                         reads=["cwx", "ident"], writes=["g%d" % g])
                P.op("dve", lambda e, h=h, g=g: e.tensor_copy(out=qT[h][b][64:66, :], in_=gbank[g][64:66, :]),
                     reads=["g%d" % g], writes=["qT%d_%d" % (h, b)])

    def attention(T, h, vh, obanks, oregion):
        b = T % 2
        nkb = 4 * T + 4
        pend = []

        def pv(kb, pb):
            js = [j for j in range(4) if kb <= 4 * T + j]
            for j in js:
                oap, okey, bfirst = oregion(j)
                P.op("pe", lambda e, j=j, oap=oap, kb=kb, pb=pb, bfirst=bfirst: e.matmul(oap, lhsT=pTb[pb][:, j * 128:(j + 1) * 128], rhs=vext[:, kb, vh, :], start=(kb == 0 and bfirst), stop=(kb == 4 * T + j), skip_group_check=True),
                     reads=["pT%d" % pb, "vext%d" % kb], writes=[okey], sig=(j == js[-1]))

        for kb in range(nkb):
            a = T * 512 - kb * 128
            near = a < 256
            g = P.nxt("g", 3)
            kreads = ["kT%d_%d" % (h, kb // 4), "qT%d_%d" % (h, b)] + (["kT%d_ones" % h] if kind == "fox" else [])
            P.op("pe", lambda e, g=g, kb=kb, near=near: e.matmul(gbank[g][:], lhsT=kT[h][:, kb * 128:(kb + 1) * 128], rhs=qT[h][b][:, :], start=True, stop=not near),
                 reads=kreads, writes=["g%d" % g], sig=(not near))
            if near:
                off = a + 384
                ti = h if kind == "da" else 0
                P.op("pe", lambda e, g=g, off=off, ti=ti: e.matmul(gbank[g][:], lhsT=ident[:], rhs=tabs[:, ti, off:off + 512], start=False, stop=True),
                     reads=["ident", "tabs"], writes=["g%d" % g])
            pb = P.nxt("pT", 4)
            if kind == "da":
                bias = 0.0 if near else farb[:, h:h + 1]
                br = [] if near else ["farb"]
            else:
                bias = biasK[T % 2][:, kb, h:h + 1]
                br = ["biasK%d_%d" % (T % 2, kb)]
            P.op("act", lambda e, g=g, pb=pb, bias=bias: e.activation(out=pTb[pb][:], in_=gbank[g][:], func=AF.Exp, bias=bias, scale=1.0),
                 reads=["g%d" % g] + br, writes=["pT%d" % pb])
            pend.append((kb, pb))
            if len(pend) > 2:
                pv(*pend.pop(0))
        for it_ in pend:
            pv(*it_)

    def flush_out(T, nchunk):
        P.dma("sp", "oT", oT[:, T * 512:(T + 1) * 512].rearrange("(c p) t -> p c t", p=128), oTs[:, 0:nchunk, :], reads=["oTs"], writes=["oTd"])

    for T in range(NT):
        project(T)
        for h in range(4):
            if kind == "da":
                H, c = h // 2, h % 2
                s = (T * 4 + h) % 2
                okeys = ["ob%d" % (2 * s), "ob%d" % (2 * s + 1)]
                attention(T, h, H, None, lambda j: (ob[2 * s + j // 2][:, j % 2, :], okeys[j // 2], j % 2 == 0))
                for j in range(4):
                    Oj = ob[2 * s + j // 2][:, j % 2, :]
                    ok = okeys[j // 2]
                    cc = scol(2)
                    P.op("dve", lambda e, Oj=Oj, cc=cc: e.reciprocal(out=small[:, cc:cc + 1], in_=Oj[:, DV:DV + 1]), reads=[ok], writes=["sm%d" % cc])
                    if c == 0:
                        P.op("dve", lambda e, Oj=Oj, cc=cc, j=j: e.tensor_scalar_mul(out=on0[:, j, :], in0=Oj[:, 0:DV], scalar1=small[:, cc:cc + 1]),
                             reads=[ok, "sm%d" % cc], writes=["on0_%d" % j])
                    else:
                        P.op("dve", lambda e, cc=cc: e.tensor_tensor(out=small[:, cc + 1:cc + 2], in0=small[:, cc:cc + 1], in1=nlam, op=ALU.mult),
                             reads=["sm%d" % cc, "nlam"], writes=["sm%d" % (cc + 1)])
                        P.op("dve", lambda e, Oj=Oj, cc=cc, j=j: e.scalar_tensor_tensor(out=comb[:, j, :], in0=Oj[:, 0:DV], scalar=small[:, cc + 1:cc + 2], in1=on0[:, j, :], op0=ALU.mult, op1=ALU.add),
                             reads=[ok, "sm%d" % (cc + 1), "on0_%d" % j], writes=["comb%d" % j])
                        P.op("act", lambda e, j=j: e.activation(out=junk[:], in_=comb[:, j, :], func=AF.Square, accum_out=ssb[:, j:j + 1]),
                             reads=["comb%d" % j], writes=["junk", "ssb%d" % j])
                if c == 1:
                    _rstd(P, ssb[:, 0:4], ssb[:, 4:8], ssb[:, 8:12], epsb[:], DV, ["ssb%d" % j for j in range(4)], "rstd")
                    for j in range(4):
                        P.op("dve", lambda e, j=j: e.scalar_tensor_tensor(out=ofin[:, j, :], in0=comb[:, j, :], scalar=ssb[:, 8 + j:9 + j], in1=gsub[:], op0=ALU.mult, op1=ALU.mult),
                             reads=["comb%d" % j, "rstd", "gsub"], writes=["ofin%d" % j])
                    for j in range(4):
                        P.op("pe", lambda e, j=j: e.transpose(out=tp[:, j, :], in_=ofin[:, j, :], identity=ident[:]),
                             reads=["ofin%d" % j, "ident"], writes=["tp"])
                    P.op("dve", lambda e, H=H: e.tensor_copy(out=oTs[:, H, :].rearrange("p (j q) -> p j q", j=4), in_=tp[:]),
                         reads=["tp"], writes=["oTs"])
            else:
                bnk = (T * 4 + h) % 3
                attention(T, h, h, None, lambda j: (ob[bnk][:, j, :], "ob%d" % bnk, j == 0))
                for j in range(4):
                    cc = scol(1)
                    P.op("dve", lambda e, cc=cc, j=j, bnk=bnk: e.reciprocal(out=small[:, cc:cc + 1], in_=ob[bnk][:, j, DV:DV + 1]), reads=["ob%d" % bnk], writes=["sm%d" % cc])
                    P.op("dve", lambda e, cc=cc, j=j, h=h, bnk=bnk: e.tensor_scalar_mul(out=ofin[:, j, h * 64:(h + 1) * 64], in0=ob[bnk][:, j, 0:DV], scalar1=small[:, cc:cc + 1]),
                         reads=["ob%d" % bnk, "sm%d" % cc], writes=["ofin%d" % j])
                if h == 3:
                    for cch in range(2):
                        for j in range(4):
                            P.op("pe", lambda e, j=j, cch=cch: e.transpose(out=tp[:, j, :], in_=ofin[:, j, cch * 128:(cch + 1) * 128], identity=ident[:]),
                                 reads=["ofin%d" % j, "ident"], writes=["tp"])
                        P.op("dve", lambda e, cch=cch: e.tensor_copy(out=oTs[:, cch, :].rearrange("p (j q) -> p j q", j=4), in_=tp[:]),
                             reads=["tp"], writes=["oTs"])
        flush_out(T, 2)
    P.wait_all("sp")
    P.emit()
    return nc


def t5_bucket_np(d):
    n = np.maximum(d, 0)
    nf = np.maximum(n, 1).astype(np.float32)
    large = 16 + (np.log(nf / np.float32(16)) / np.float32(math.log(8.0)) * np.float32(16)).astype(np.int32)
    large = np.minimum(large, 31)
    return np.where(n < 16, n, large)


def toeplitz_idx(width, amin, dmax=None):
    i = np.arange(128)[:, None]
    m = np.arange(width)[None, :]
    d = m + amin - i
    idx = t5_bucket_np(d)
    bad = d < 0
    if dmax is not None:
        bad = bad | (d >= dmax)
    return np.where(bad, 32, idx)


def build_nsa(S):
    nc = bass.Bass("TRN2", target_bir_lowering=False)
    NT = S // 512
    NKB = S // 128
    NCB = 4 if S >= 8192 else max(1, (S // 16 + 127) // 128)
    NCP = NCB * 128

    def din(name, shape, dt=F32):
        return nc.dram_tensor(name, list(shape), dt, kind="ExternalInput").ap()

    hT = din("hT", [D, S], BF16)
    identd = din("ident", [128, 128], BF16)
    wq = din("wq", [D, 256])
    wkv = din("wkv", [D, 384])
    wgt = din("wgt", [D, 12])
    peTd = din("peT", [64, 2, 32])
    cw1d = din("cw1", [64, 2, 32, 256])
    cw2d = din("cw2", [128, 2, 2, 64])
    tseld = din("tsel", [4, 128, 1024], BF16)
    tcmpd = din("tcmp", [4, 128, 2560], BF16)
    twind = din("twin", [4, 128, 1408], BF16)
    farbd = din("farb", [128, 4])
    gseld = din("gsel", [128, S], BF16)
    ovld = din("ovl", [128, NCB, 128], BF16)
    mMd = din("mM", [S, 128])
    mACd = din("mAC", [S, 128])
    oT = nc.dram_tensor("oT", [256, S], BF16, kind="ExternalOutput").ap()

    P = Prog(nc)
    ident = P.sbuf("ident", [128, 128], BF16)
    wqs = P.sbuf("wqs", [128, 8, 256], BF16)
    wkvs = P.sbuf("wkvs", [128, 8, 384], BF16)
    wgs = P.sbuf("wgs", [128, 8, 12], BF16)
    peT = P.sbuf("peT", [64, 2, 32], BF16)
    cw1 = P.sbuf("cw1", [64, 2, 32, 256], BF16)
    cw2 = P.sbuf("cw2", [128, 2, 2, 64], BF16)
    hbias = P.sbuf("hbias", [128, 4], F32)
    tsel = P.sbuf("tsel", [128, 4, 1024], BF16)
    tcmp = P.sbuf("tcmp", [128, 4, 2560], BF16)
    twin = P.sbuf("twin", [128, 4, 1408], BF16)
    farb = P.sbuf("farb", [128, 4], F32)
    gsel = P.sbuf("gsel", [128, S], BF16)
    kselT = P.sbuf("kselT", [64, S], BF16)
    kwinT = P.sbuf("kwinT", [64, S], BF16)
    vsw = P.sbuf("vsw", [128, NKB, 2, 65], BF16)
    cwin = [P.sbuf("cwin%d" % b, [64, 2, 528], BF16) for b in range(2)]
    kcT = P.sbuf("kcT", [64, NCP], BF16)
    vcT = P.sbuf("vcT", [64, NCP], BF16)
    vcx = P.sbuf("vcx", [128, NCB, 193], BF16)
    qT = [[P.sbuf("qT%d_%d" % (h, b), [64, 512], BF16) for b in range(2)] for h in range(4)]
    hts = [P.sbuf("hts%d" % b, [128, 8, 512], BF16) for b in range(2)]
    pTb = [P.sbuf("pT%d" % i, [128, 512], BF16) for i in range(4)]
    small = P.sbuf("small", [128, 64], F32)
    gates = P.sbuf("gates", [128, 4, 12], F32)
    ofin32 = P.sbuf("ofin32", [128, 4, 256], F32)
    ofin = P.sbuf("ofin", [128, 4, 256], BF16)
    oTs = P.sbuf("oTs", [128, 2, 512], BF16)
    impacc = P.sbuf("impacc", [128, 4, 128], F32)
    mM = P.sbuf("mM", [128, 4, 128], F32)
    mAC = P.sbuf("mAC", [128, 4, 128], F32)
    imp2 = P.sbuf("imp2", [128, 128], F32)
    m8 = P.sbuf("m8", [128, 16], F32)
    selb = P.sbuf("selb", [128, 128], BF16)
    selbT = P.sbuf("selbT", [128, 512], BF16)
    gh = P.sbuf("gh", [128, 2, 32], F32)
    gt = P.sbuf("gt", [128, 2, 32], F32)
    ghb = P.sbuf("ghb", [128, 2, 32], BF16)
    gbank = [P.psum("g%d" % i, [128, 512], F32) for i in range(3)]
    ob = [P.psum("ob%d" % i, [128, 512], F32) for i in range(3)]
    tp = P.psum("tp", [128, 4, 128], BF16)
    px = P.psum("px", [128, 128], F32)

    P.dma("sp", "c0", ident[:], identd[:, :], writes=["ident"])
    P.dma("sp", "c1", tsel[:], tseld.rearrange("h p m -> p h m"), writes=["tsel"])
    P.dma("sp", "c2", tcmp[:], tcmpd.rearrange("h p m -> p h m"), writes=["tcmp"])
    P.dma("sp", "c3", twin[:], twind.rearrange("h p m -> p h m"), writes=["twin"])
    P.dma("sp", "c4", farb[:], farbd[:, :], writes=["farb"])
    P.dma("sp", "c5", gsel[:], gseld[:, :], writes=["gsel"])
    P.dma("pool", "w0", wqs[:], wq.rearrange("(c p) f -> p c f", p=128), writes=["wqs"])
    P.dma("pool", "w1", wkvs[:], wkv.rearrange("(c p) f -> p c f", p=128), writes=["wkvs"])
    P.dma("pool", "w2", wgs[:], wgt.rearrange("(c p) f -> p c f", p=128), writes=["wgs"])
    P.dma("pool", "w3", peT[:], peTd[:, :, :], writes=["peT"])
    P.dma("pool", "w4", cw1[:], cw1d[:, :, :, :], writes=["cw1"])
    P.dma("pool", "w5", cw2[:], cw2d[:, :, :, :], writes=["cw2"])
    P.op("pool", lambda e: e.memset(vsw[:], 1.0), writes=["vsw%d" % kb for kb in range(NKB)])
    P.op("pool", lambda e: e.memset(vcx[:], 1.0), writes=["vcx"])
    P.dma("sp", "c6", vcx[:, :, 65:193], ovld[:, :, :], writes=["vcx"])
    P.op("dve", lambda e: e.memset(kcT[:], 0.0), writes=["kcT"])
    P.op("dve", lambda e: e.memset(vcT[:], 0.0), writes=["vcT"])
    P.op("dve", lambda e: e.memset(cwin[1][:], 0.0), writes=["cwin1"])
    for kv in range(2):
        for hc in range(2):
            col = kv * 2 + hc
            for l in range(32):
                P.op("pe", lambda e, kv=kv, hc=hc, l=l, col=col: e.matmul(px[:, col:col + 1], lhsT=cw1[:, kv, l, hc * 128:(hc + 1) * 128], rhs=peT[:, kv, l:l + 1], start=(l == 0 and col == 0), stop=(l == 31), skip_group_check=True),
                     reads=["cw1", "peT"], writes=["px"])
    P.op("dve", lambda e: e.tensor_copy(out=hbias[:], in_=px[:, 0:4]), reads=["px"], writes=["hbias"])
    smallc = [0]

    def scol(n=1):
        c = smallc[0]
        if c + n > 64:
            c = 0
        smallc[0] = c + n
        return c

    def project(T):
        b = T % 2
        P.dma("sp", "hts%d" % b, hts[b][:], hT[:, T * 512:(T + 1) * 512].rearrange("(c p) t -> p c t", p=128), writes=["hts%d" % b])
        P.dma("sp", "mM", mM[:], mMd[T * 512:(T + 1) * 512, :].rearrange("(j p) n -> p j n", p=128), writes=["mM"])
        P.dma("sp", "mAC", mAC[:], mACd[T * 512:(T + 1) * 512, :].rearrange("(j p) n -> p j n", p=128), writes=["mAC"])
        for h in range(4):
            g = P.nxt("g", 3)
            for c in range(8):
                P.op("pe", lambda e, c=c, h=h, g=g: e.matmul(gbank[g][0:64, :], lhsT=wqs[:, c, h * 64:(h + 1) * 64], rhs=hts[b][:, c, :], start=(c == 0), stop=(c == 7)), sig=(c == 7),
                     reads=["wqs", "hts%d" % b], writes=["g%d" % g])
            P.op("dve", lambda e, h=h, g=g: e.tensor_scalar_mul(out=qT[h][b][:, :], in0=gbank[g][0:64, :], scalar1=0.125),
                 reads=["g%d" % g], writes=["qT%d_%d" % (h, b)])
        if T > 0:
            P.op("dve", lambda e: e.tensor_copy(out=cwin[b][:, :, 0:16], in_=cwin[1 - b][:, :, 512:528]), reads=["cwin%d" % (1 - b)], writes=["cwin%d" % b])
        dsts = [(cwin[b][:, 0, 16:528], "cwin%d" % b), (cwin[b][:, 1, 16:528], "cwin%d" % b),
                (kselT[:, T * 512:(T + 1) * 512], "kselT%d" % T), (kwinT[:, T * 512:(T + 1) * 512], "kwinT%d" % T)]
        for q in range(4):
            g = P.nxt("g", 3)
            for c in range(8):
                P.op("pe", lambda e, c=c, q=q, g=g: e.matmul(gbank[g][0:64, :], lhsT=wkvs[:, c, q * 64:(q + 1) * 64], rhs=hts[b][:, c, :], start=(c == 0), stop=(c == 7)), sig=(c == 7),
                     reads=["wkvs", "hts%d" % b], writes=["g%d" % g])
            dst, dk = dsts[q]
            P.op("dve", lambda e, dst=dst, g=g: e.tensor_copy(out=dst, in_=gbank[g][0:64, :]), reads=["g%d" % g], writes=[dk])
        for i in range(4):
            g = P.nxt("g", 3)
            kb = 4 * T + i
            for c in range(8):
                P.op("pe", lambda e, c=c, i=i, g=g: e.matmul(gbank[g][:, 0:128], lhsT=hts[b][:, c, i * 128:(i + 1) * 128], rhs=wkvs[:, c, 256:384], start=(c == 0), stop=(c == 7)), sig=(c == 7),
                     reads=["wkvs", "hts%d" % b], writes=["g%d" % g])
            P.op("dve", lambda e, g=g, kb=kb: e.tensor_copy(out=vsw[:, kb, :, 0:64], in_=gbank[g][:, 0:128].rearrange("p (v d) -> p v d", v=2)),
                 reads=["g%d" % g], writes=["vsw%d" % kb])
        for i in range(4):
            for c in range(8):
                P.op("pe", lambda e, c=c, i=i: e.matmul(px[:, i * 12:(i + 1) * 12], lhsT=hts[b][:, c, i * 128:(i + 1) * 128], rhs=wgs[:, c, :], start=(c == 0 and i == 0), stop=(c == 7), skip_group_check=True),
                     reads=["wgs", "hts%d" % b], writes=["px"])
        P.op("act", lambda e: e.activation(out=gates[:].rearrange("p i c -> p (i c)"), in_=px[:, 0:48], func=AF.Exp, scale=-1.0), reads=["px"], writes=["gates"])
        P.op("dve", lambda e: e.tensor_scalar_add(out=gates[:], in0=gates[:], scalar1=1.0), reads=[], writes=["gates"])
        P.op("dve", lambda e: e.reciprocal(out=gates[:], in_=gates[:]), reads=[], writes=["gates"])

    def compress(T):
        b = T % 2
        u0 = 1 if T == 0 else 0
        NU = 32 - u0
        n0 = 32 * T - 1 + u0
        for kv in range(2):
            for hc in range(2):
                g = P.nxt("g", 3)
                for l in range(32):
                    c0 = 16 * u0 + l
                    P.op("pe", lambda e, kv=kv, hc=hc, l=l, c0=c0, g=g: e.matmul(gbank[g][:, 0:NU], lhsT=cw1[:, kv, l, hc * 128:(hc + 1) * 128], rhs=cwin[b][:, kv, c0:c0 + 16 * (NU - 1) + 1:16], start=(l == 0), stop=(l == 31)),
                         reads=["cw1", "cwin%d" % b], writes=["g%d" % g])
                P.op("dve", lambda e, kv=kv, hc=hc, g=g: e.tensor_scalar_add(out=gh[:, hc, 0:NU], in0=gbank[g][:, 0:NU], scalar1=hbias[:, kv * 2 + hc:kv * 2 + hc + 1]),
                     reads=["g%d" % g, "hbias"], writes=["gh%d" % hc])
                P.op("dve", lambda e, hc=hc: e.tensor_tensor(out=gt[:, hc, 0:NU], in0=gh[:, hc, 0:NU], in1=gh[:, hc, 0:NU], op=ALU.mult), reads=["gh%d" % hc], writes=["gt%d" % hc])
                P.op("dve", lambda e, hc=hc: e.tensor_scalar(out=gt[:, hc, 0:NU], in0=gt[:, hc, 0:NU], scalar1=0.044715, scalar2=1.0, op0=ALU.mult, op1=ALU.add), reads=[], writes=["gt%d" % hc])
                P.op("dve", lambda e, hc=hc: e.tensor_tensor(out=gt[:, hc, 0:NU], in0=gt[:, hc, 0:NU], in1=gh[:, hc, 0:NU], op=ALU.mult), reads=["gh%d" % hc], writes=["gt%d" % hc])
                P.op("act", lambda e, hc=hc: e.activation(out=gt[:, hc, 0:NU], in_=gt[:, hc, 0:NU], func=AF.Exp, scale=-1.5957691216057308), reads=[], writes=["gt%d" % hc])
                P.op("dve", lambda e, hc=hc: e.tensor_scalar_add(out=gt[:, hc, 0:NU], in0=gt[:, hc, 0:NU], scalar1=1.0), reads=[], writes=["gt%d" % hc])
                P.op("dve", lambda e, hc=hc: e.reciprocal(out=gt[:, hc, 0:NU], in_=gt[:, hc, 0:NU]), reads=[], writes=["gt%d" % hc])
                P.op("dve", lambda e, hc=hc: e.tensor_tensor(out=ghb[:, hc, 0:NU], in0=gt[:, hc, 0:NU], in1=gh[:, hc, 0:NU], op=ALU.mult), reads=["gt%d" % hc, "gh%d" % hc], writes=["ghb%d" % hc])
            g = P.nxt("g", 3)
            for hc in range(2):
                P.op("pe", lambda e, kv=kv, hc=hc, g=g: e.matmul(gbank[g][0:64, 0:NU], lhsT=cw2[:, kv, hc, :], rhs=ghb[:, hc, 0:NU], start=(hc == 0), stop=(hc == 1)),
                     reads=["cw2", "ghb%d" % hc], writes=["g%d" % g])
            dstT = kcT if kv == 0 else vcT
            P.op("dve", lambda e, g=g, dstT=dstT: e.tensor_copy(out=dstT[:, n0:n0 + NU], in_=gbank[g][0:64, 0:NU]),
                 reads=["g%d" % g], writes=["kcT" if kv == 0 else "vcT"])
        for nb in sorted(set([n0 // 128, (n0 + NU - 1) // 128])):
            P.op("pe", lambda e, nb=nb: e.transpose(out=tp[0:128, 0, 0:64], in_=vcT[:, nb * 128:(nb + 1) * 128], identity=ident[0:64, 0:64]),
                 reads=["vcT", "ident"], writes=["tp"])
            P.op("dve", lambda e, nb=nb: e.tensor_copy(out=vcx[:, nb, 0:64], in_=tp[:, 0, 0:64]), reads=["tp"], writes=["vcx"])

    def attention(T, h, kblist, qk, extra, biasf, vr, first, last, oregion):
        b = T % 2
        pend = []

        def pv(kb, pb):
            rhs, rk = vr(kb)
            js = [j for j in range(4) if first(j) <= kb <= last(j)]
            for j in js:
                oap, okey, bfirst = oregion(j)
                st_ = bool(kb == first(j) and bfirst)
                sp_ = bool(kb == last(j))
                P.op("pe", lambda e, j=j, oap=oap, kb=kb, pb=pb, st_=st_, sp_=sp_, rhs=rhs: e.matmul(oap, lhsT=pTb[pb][:, j * 128:(j + 1) * 128], rhs=rhs, start=st_, stop=sp_, skip_group_check=True),
                     reads=["pT%d" % pb] + rk, writes=[okey], sig=(j == js[-1]))

        for kb in kblist:
            g = P.nxt("g", 3)
            lhsT, kreads = qk(kb)
            ex = extra(kb)
            P.op("pe", lambda e, g=g, lhsT=lhsT, ex=ex: e.matmul(gbank[g][:], lhsT=lhsT, rhs=qT[h][b][:, :], start=True, stop=(len(ex) == 0)),
                 reads=kreads + ["qT%d_%d" % (h, b)], writes=["g%d" % g], sig=(len(ex) == 0))
            for xi, (xl, xr, xk) in enumerate(ex):
                P.op("pe", lambda e, g=g, xl=xl, xr=xr, xi=xi, ex=ex: e.matmul(gbank[g][:], lhsT=xl, rhs=xr, start=False, stop=(xi == len(ex) - 1)),
                     reads=xk, writes=["g%d" % g], sig=(xi == len(ex) - 1))
            pb = P.nxt("pT", 4)
            bias, br = biasf(kb)
            P.op("act", lambda e, g=g, pb=pb, bias=bias: e.activation(out=pTb[pb][:], in_=gbank[g][:], func=AF.Exp, bias=bias, scale=1.0),
                 reads=["g%d" % g] + br, writes=["pT%d" % pb])
            pend.append((kb, pb))
            if len(pend) > 2:
                pv(*pend.pop(0))
        for it_ in pend:
            pv(*it_)

    for T in range(NT):
        b = T % 2
        project(T)
        compress(T)
        nbs = [nb for nb in range(NCB) if T - 4 * nb >= 0]
        for r in range(4):
            bA = P.nxt("ob", 3)
            bB = P.nxt("ob", 3)
            bks = [bA, bB]

            def c_qk(nb):
                return kcT[:, nb * 128:(nb + 1) * 128], ["kcT"]

            def c_extra(nb, r=r):
                dl = T - 4 * nb
                if dl >= 5:
                    return []
                return [(ident[:], tcmp[:, r, 512 * dl:512 * dl + 512], ["ident", "tcmp"])]

            def c_bias(nb, r=r):
                if T - 4 * nb >= 5:
                    return farb[:, r:r + 1], ["farb"]
                return 0.0, []

            attention(T, r, nbs, c_qk, c_extra, c_bias, lambda nb: (vcx[:, nb, :], ["vcx"]),
                      lambda j: nbs[0], lambda j: nbs[-1],
                      lambda j, bks=bks: (ob[bks[j // 2]][:, (j % 2) * 193:(j % 2) * 193 + 193], "ob%d" % bks[j // 2], j % 2 == 0))
            for j in range(4):
                Oj = ob[bks[j // 2]][:, (j % 2) * 193:(j % 2) * 193 + 193]
                ok = "ob%d" % bks[j // 2]
                cc = scol(3)
                P.op("dve", lambda e, Oj=Oj, cc=cc: e.tensor_scalar_max(out=small[:, cc:cc + 1], in0=Oj[:, 64:65], scalar1=1e-30), reads=[ok], writes=["sm%d" % cc])
                P.op("dve", lambda e, cc=cc: e.reciprocal(out=small[:, cc + 1:cc + 2], in_=small[:, cc:cc + 1]), reads=["sm%d" % cc], writes=["sm%d" % (cc + 1)])
                P.op("dve", lambda e, cc=cc, j=j, r=r: e.tensor_tensor(out=small[:, cc + 2:cc + 3], in0=small[:, cc + 1:cc + 2], in1=gates[:, j, r * 3:r * 3 + 1], op=ALU.mult),
                     reads=["sm%d" % (cc + 1), "gates"], writes=["sm%d" % (cc + 2)])
                P.op("dve", lambda e, Oj=Oj, cc=cc, j=j, r=r: e.tensor_scalar_mul(out=ofin32[:, j, r * 64:(r + 1) * 64], in0=Oj[:, 0:64], scalar1=small[:, cc + 2:cc + 3]),
                     reads=[ok, "sm%d" % (cc + 2)], writes=["ofin32_%d_%d" % (j, r)])
                if r == 0:
                    P.op("dve", lambda e, Oj=Oj, cc=cc, j=j: e.tensor_scalar_mul(out=impacc[:, j, :], in0=Oj[:, 65:193], scalar1=small[:, cc + 1:cc + 2]),
                         reads=[ok, "sm%d" % (cc + 1)], writes=["imp%d" % j])
                else:
                    P.op("dve", lambda e, Oj=Oj, cc=cc, j=j: e.scalar_tensor_tensor(out=impacc[:, j, :], in0=Oj[:, 65:193], scalar=small[:, cc + 1:cc + 2], in1=impacc[:, j, :], op0=ALU.mult, op1=ALU.add),
                         reads=[ok, "sm%d" % (cc + 1)], writes=["imp%d" % j])
        for j in range(4):
            P.op("dve", lambda e, j=j: e.tensor_tensor(out=imp2[:], in0=impacc[:, j, :], in1=mM[:, j, :], op=ALU.mult), reads=["imp%d" % j, "mM"], writes=["imp2"])
            P.op("dve", lambda e, j=j: e.tensor_tensor(out=imp2[:], in0=imp2[:], in1=mAC[:, j, :], op=ALU.add), reads=["mAC"], writes=["imp2"])
            P.op("dve", lambda e: e.max(out=m8[:, 0:8], in_=imp2[:]), reads=["imp2"], writes=["m8a"])
            P.op("dve", lambda e, j=j: e.match_replace(out=impacc[:, j, :], in_to_replace=m8[:, 0:8], in_values=imp2[:], imm_value=-1e30), reads=["imp2", "m8a"], writes=["imp%d" % j])
            P.op("dve", lambda e, j=j: e.max(out=m8[:, 8:16], in_=impacc[:, j, :]), reads=["imp%d" % j], writes=["m8b"])
            P.op("dve", lambda e: e.tensor_scalar(out=selb[:], in0=imp2[:], scalar1=m8[:, 15:16], scalar2=NEG, op0=ALU.is_lt, op1=ALU.mult), reads=["imp2", "m8b"], writes=["selb"])
            P.op("pe", lambda e, j=j: e.transpose(out=tp[:, j, :], in_=selb[:], identity=ident[:]), reads=["selb", "ident"], writes=["tp"])
        P.op("dve", lambda e: e.tensor_copy(out=selbT[:].rearrange("p (j q) -> p j q", j=4), in_=tp[:]), reads=["tp"], writes=["selbT"])
        for br_i, (kT_, tab_, name) in enumerate(((kselT, tsel, "sel"), (kwinT, twin, "win"))):
            for r in range(4):
                bnk = P.nxt("ob", 3)
                if name == "sel":
                    kbl = list(range(4 * T + 4))
                    fst = lambda j: 0
                else:
                    kbl = list(range(max(0, 4 * T - 4), 4 * T + 4))
                    fst = lambda j: max(0, 4 * T + j - 4)

                def s_qk(kb, kT_=kT_, name=name):
                    return kT_[:, kb * 128:(kb + 1) * 128], ["k%sT%d" % (name, kb // 4)]

                def s_extra(kb, r=r, name=name, tab_=tab_):
                    a = T * 512 - kb * 128
                    ex = []
                    if name == "win" or a < 256:
                        ex.append((ident[:], tab_[:, r, a + 384:a + 384 + 512], ["ident", "t" + name]))
                    if name == "sel":
                        ex.append((gsel[:, kb * 128:(kb + 1) * 128], selbT[:], ["gsel", "selbT"]))
                    return ex

                def s_bias(kb, r=r, name=name):
                    a = T * 512 - kb * 128
                    if name == "sel" and a >= 256:
                        return farb[:, r:r + 1], ["farb"]
                    return 0.0, []

                attention(T, r, kbl, s_qk, s_extra, s_bias, lambda kb, br_i=br_i: (vsw[:, kb, br_i, :], ["vsw%d" % kb]),
                          fst, lambda j: 4 * T + j,
                          lambda j, bnk=bnk: (ob[bnk][:, j * 65:(j + 1) * 65], "ob%d" % bnk, j == 0))
                for j in range(4):
                    cc = scol(2)
                    Oj = ob[bnk][:, j * 65:(j + 1) * 65]
                    P.op("dve", lambda e, Oj=Oj, cc=cc: e.reciprocal(out=small[:, cc:cc + 1], in_=Oj[:, 64:65]), reads=["ob%d" % bnk], writes=["sm%d" % cc])
                    P.op("dve", lambda e, cc=cc, j=j, r=r, br_i=br_i: e.tensor_tensor(out=small[:, cc + 1:cc + 2], in0=small[:, cc:cc + 1], in1=gates[:, j, r * 3 + 1 + br_i:r * 3 + 2 + br_i], op=ALU.mult),
                         reads=["sm%d" % cc, "gates"], writes=["sm%d" % (cc + 1)])
                    P.op("dve", lambda e, Oj=Oj, cc=cc, j=j, r=r: e.scalar_tensor_tensor(out=ofin32[:, j, r * 64:(r + 1) * 64], in0=Oj[:, 0:64], scalar=small[:, cc + 1:cc + 2], in1=ofin32[:, j, r * 64:(r + 1) * 64], op0=ALU.mult, op1=ALU.add),
                         reads=["ob%d" % bnk, "sm%d" % (cc + 1)], writes=["ofin32_%d_%d" % (j, r)])
        for j in range(4):
            P.op("dve", lambda e, j=j: e.tensor_copy(out=ofin[:, j, :], in_=ofin32[:, j, :]), reads=["ofin32_%d_%d" % (j, r) for r in range(4)], writes=["ofin%d" % j])
        for cch in range(2):
            for j in range(4):
                P.op("pe", lambda e, j=j, cch=cch: e.transpose(out=tp[:, j, :], in_=ofin[:, j, cch * 128:(cch + 1) * 128], identity=ident[:]),
                     reads=["ofin%d" % j, "ident"], writes=["tp"])
            P.op("dve", lambda e, cch=cch: e.tensor_copy(out=oTs[:, cch, :].rearrange("p (j q) -> p j q", j=4), in_=tp[:]), reads=["tp"], writes=["oTs"])
        P.dma("sp", "oT", oT[:, T * 512:(T + 1) * 512].rearrange("(c p) t -> p c t", p=128), oTs[:], reads=["oTs"], writes=["oTd"])
    P.wait_all("sp")
    P.emit()
    return nc


def nsa_consts(S):
    n_sel = S // 64
    ncb = 4 if S >= 8192 else max(1, (S // 16 + 127) // 128)
    n_cmp = (S - 32) // 16 + 1
    j = np.arange(128)[:, None]
    m = np.arange(S)[None, :]
    gsel = ((m // 64) == j).astype(np.float32)
    n = np.arange(ncb * 128)[:, None]
    jb = np.arange(128)[None, :]
    cs = n * 16
    ce = cs + 31
    ovl = ((cs < jb * 64 + 64) & (ce >= jb * 64) & (n < n_cmp) & (jb < n_sel)).astype(np.float32)
    ovl = ovl.reshape(ncb, 128, 128).transpose(1, 0, 2)
    t = np.arange(S)[:, None]
    cur = t // 64
    valid = (jb * 64 <= t) & (jb < n_sel)
    f0 = (jb == 0)
    f1 = (jb == cur)
    f2 = (jb == cur - 1)
    forced = f0 | f1 | f2
    mM = (valid & ~forced).astype(np.float32)
    fv = np.where(f2, 3e4, np.where(f1, 2e4, 1e4)).astype(np.float32)
    mAC = np.where(valid, np.where(forced, fv, 0.0), -1.0).astype(np.float32)
    i = np.arange(128)[:, None]
    mm = np.arange(2560)[None, :]
    d = mm - 16 * i - 31
    idx_cmp = np.where(d < 0, 32, t5_bucket_np(d))
    return dict(gsel=gsel.astype(NPBF), ovl=ovl.astype(NPBF), mM=mM, mAC=mAC, idx_cmp=idx_cmp,
                idx_sel=toeplitz_idx(1024, -384), idx_win=toeplitz_idx(1408, -384, 512))


_PROGS = {}


def _prog(key, fn):
    if key not in _PROGS:
        _PROGS[key] = fn()
    return _PROGS[key]


def _run(nc, in_maps):
    return run_bass_kernel_spmd(nc, in_maps, core_ids=list(range(8))).results


def kernel(x, p, rel_bias, norm_g, mlp_w1, mlp_w2, ple_w, ple_gate_w,
           da_w_in, da_lambda, da_subln, da_w_out,
           nsa_w_in, nsa_cmp_pe, nsa_cmp_w1, nsa_cmp_w2, nsa_w_out,
           fox_w_in, fox_b_f, fox_w_out):
    f32 = lambda a: np.ascontiguousarray(np.asarray(a, dtype=np.float32))
    x = f32(x)
    B, S, _ = x.shape
    TS = (B * S) // 8
    QS = S // TS
    depth = norm_g.shape[0]
    p = f32(p); rel_bias = f32(rel_bias); norm_g = f32(norm_g)
    ident = np.eye(128, dtype=np.float32).astype(NPBF)
    rel_ext = np.concatenate([rel_bias, np.full((1, 16), NEG, np.float32)], 0)
    idx_da = toeplitz_idx(1024, -384)
    xs = x.reshape(B * S, D)
    xsh = [np.ascontiguousarray(xs[c * TS:(c + 1) * TS]) for c in range(8)]

    nc0 = _prog(("tok", TS, True), lambda: build_token(TS, True))
    res = _run(nc0, [{"x": xsh[c], "gn": f32(norm_g[0, 0]), "ident": ident} for c in range(8)])
    hTs = [res[c]["hT"] for c in range(8)]
    ia = ib = ic = 0
    for i in range(depth):
        hTb = [np.ascontiguousarray(np.concatenate(hTs[b * QS:(b + 1) * QS], axis=1)) for b in range(B)]
        kind = i % 3
        ims = []
        if kind == 0:
            lam_init = 0.8 - 0.6 * math.exp(-0.3 * i)
            w_in = f32(da_w_in[ia]); w_o = f32(da_w_out[ia])
            lamc = np.empty((128, 2), np.float32); lamc[:, 0] = 1.0 - lam_init; lamc[:, 1] = -lam_init
            for b in range(B):
                for hp in range(4):
                    cols = [4 * hp + s_ for s_ in range(4)]
                    ims.append({"hT": hTb[b], "ident": ident,
                                "wq": np.ascontiguousarray(w_in[:, 256 * hp:256 * hp + 256]),
                                "wk": np.ascontiguousarray(w_in[:, 1024 + 256 * hp:1024 + 256 * hp + 256]),
                                "wv": np.ascontiguousarray(w_in[:, 2048 + 256 * hp:2048 + 256 * hp + 256]),
                                "tab": np.stack([rel_ext[idx_da, c] for c in cols], 0).astype(NPBF),
                                "farb": np.ascontiguousarray(np.broadcast_to(rel_bias[31, cols][None, :], (128, 4))),
                                "lam": f32(da_lambda[ia]).reshape(-1), "subln": f32(da_subln[ia]), "lamc": lamc})
            nch = _prog(("da", S), lambda: build_head("da", S))
            ia += 1
        elif kind == 1:
            w_in = f32(nsa_w_in[ib]); w_o = f32(nsa_w_out[ib])
            C = _prog(("nsac", S), lambda: nsa_consts(S))
            pe_ = f32(nsa_cmp_pe[ib]); c1 = f32(nsa_cmp_w1[ib]); c2 = f32(nsa_cmp_w2[ib])
            peT = np.ascontiguousarray(pe_.transpose(2, 0, 1))
            cw1 = np.ascontiguousarray(c1.reshape(2, 32, 64, 256).transpose(2, 0, 1, 3))
            cw2 = np.ascontiguousarray(c2.reshape(2, 2, 128, 64).transpose(2, 0, 1, 3))
            for b in range(B):
                for g in range(4):
                    cols = [4 * g + r for r in range(4)]
                    kvc = lambda k_: w_in[:, 1024 + k_ * 256 + g * 64: 1024 + k_ * 256 + g * 64 + 64]
                    ims.append({"hT": hTb[b], "ident": ident,
                                "wq": np.ascontiguousarray(w_in[:, g * 256:(g + 1) * 256]),
                                "wkv": np.ascontiguousarray(np.concatenate([kvc(0), kvc(1), kvc(2), kvc(4), kvc(3), kvc(5)], 1)),
                                "wgt": np.ascontiguousarray(w_in[:, 2560 + g * 12: 2560 + g * 12 + 12]),
                                "peT": peT, "cw1": cw1, "cw2": cw2,
                                "tsel": np.stack([rel_ext[C["idx_sel"], c] for c in cols], 0).astype(NPBF),
                                "tcmp": np.stack([rel_ext[C["idx_cmp"], c] for c in cols], 0).astype(NPBF),
                                "twin": np.stack([rel_ext[C["idx_win"], c] for c in cols], 0).astype(NPBF),
                                "farb": np.ascontiguousarray(np.broadcast_to(rel_bias[31, cols][None, :], (128, 4))),
                                "gsel": C["gsel"], "ovl": C["ovl"], "mM": C["mM"], "mAC": C["mAC"]})
            nch = _prog(("nsa", S), lambda: build_nsa(S))
            ib += 1
        else:
            w_in = f32(fox_w_in[ic]); w_o = f32(fox_w_out[ic]); b_f = f32(fox_b_f[ic])
            mask_ext = np.concatenate([np.zeros((32,), np.float32), np.full((1,), NEG, np.float32)], 0)
            tab = mask_ext[idx_da][None].astype(NPBF)
            tri = np.triu(np.ones((128, 128), np.float32))
            ones = np.ones((128, 128), np.float32)
            for b in range(B):
                for hg in range(4):
                    ims.append({"hT": hTb[b], "ident": ident,
                                "wq": np.ascontiguousarray(w_in[:, 256 * hg:256 * hg + 256]),
                                "wk": np.ascontiguousarray(w_in[:, 1024 + 256 * hg:1024 + 256 * hg + 256]),
                                "wv": np.ascontiguousarray(w_in[:, 2048 + 256 * hg:2048 + 256 * hg + 256]),
                                "tab": tab, "wf": np.ascontiguousarray(w_in[:, 3072 + 4 * hg:3072 + 4 * hg + 4]),
                                "bf": np.ascontiguousarray(np.tile(b_f[4 * hg:4 * hg + 4], 4)), "tri": tri, "ones": ones})
            nch = _prog(("fox", S), lambda: build_head("fox", S))
            ic += 1
        res = _run(nch, ims)
        oTb = [np.concatenate([res[b * 4 + g]["oT"] for g in range(4)], axis=0) for b in range(B)]
        gn = f32(norm_g[i + 1, 0]) if i + 1 < depth else f32(norm_g[i, 0])
        ims = []
        for c in range(8):
            b, q = c // QS, c % QS
            ims.append({"x": xsh[c], "gn": gn, "ident": ident,
                        "oT": np.ascontiguousarray(oTb[b][:, q * TS:(q + 1) * TS]),
                        "p": np.ascontiguousarray(p[i].reshape(B * S, -1)[c * TS:(c + 1) * TS]),
                        "w_out": w_o, "w1": f32(mlp_w1[i]), "w2": f32(mlp_w2[i]),
                        "ple_w": f32(ple_w[i]), "gate_w": f32(ple_gate_w[i]), "g": f32(norm_g[i, 1:4])})
        nct = _prog(("tok", TS, False), lambda: build_token(TS, False))
        res = _run(nct, ims)
        xsh = [res[c]["xo"] for c in range(8)]
        hTs = [res[c]["hT"] for c in range(8)]
    return np.concatenate(xsh, axis=0).reshape(B, S, D).astype(np.float32)
```

```python
import math
from contextlib import ExitStack
import numpy as np
import ml_dtypes
import concourse.bass as bass
import concourse.mybir as mybir
from concourse.bass_utils import run_bass_kernel_spmd

F32 = mybir.dt.float32
BF16 = mybir.dt.bfloat16
AF = mybir.ActivationFunctionType
ALU = mybir.AluOpType
NPBF = ml_dtypes.bfloat16

D = 1024
DFF = 4096
EPS = 1e-6
NEG = -30000.0
_STAGES = 'ame12rbcBdD'
_WQ = ['pool']


class Prog:
    ENGS = ("pe", "act", "dve", "pool", "sp")

    def __init__(self, nc):
        self.nc = nc
        self.es = ExitStack()
        self.ops = {e: [] for e in self.ENGS}
        self.cnt = {}
        self.sems = {}
        self.waited = {e: {} for e in self.ENGS}
        self.W = {}
        self.R = {}
        for e in ("pe", "act", "dve", "pool"):
            self._sem("c_" + e)
        self.n_instr = 0
        self.rot = {}

    def _sem(self, name):
        if name not in self.sems:
            self.sems[name] = self.es.enter_context(self.nc.semaphore(name))
            self.cnt[name] = 0
        return self.sems[name]

    def sbuf(self, name, shape, dt):
        return self.es.enter_context(self.nc.sbuf_tensor("s_" + name, list(shape), dt))

    def psum(self, name, shape, dt=F32):
        return self.es.enter_context(self.nc.psum_tensor("p_" + name, list(shape), dt))

    def _deps(self, eng, reads, writes):
        deps = {}
        for k in reads:
            w = self.W.get(k)
            if w:
                deps[w[0]] = max(deps.get(w[0], 0), w[1])
        for k in writes:
            w = self.W.get(k)
            if w:
                deps[w[0]] = max(deps.get(w[0], 0), w[1])
            for r in self.R.get(k, ()):
                deps[r[0]] = max(deps.get(r[0], 0), r[1])
        for s, n in deps.items():
            if eng == "pe" and s == "c_pe":
                continue
            if s.startswith("d_"):
                n = self.cnt[s]
            if self.waited[eng].get(s, 0) >= n:
                continue
            self.waited[eng][s] = n
            mult = 16 if s.startswith("d_") else 1
            self.ops[eng].append(("wait", self.sems[s], n * mult))

    def _finish(self, semname, reads, writes, sig=True):
        if sig:
            self.cnt[semname] += 1
            n = self.cnt[semname]
        else:
            n = self.cnt[semname] + 1
        for k in writes:
            self.W[k] = (semname, n)
            self.R[k] = []
        for k in reads:
            if k in writes:
                continue
            lst = self.R.setdefault(k, [])
            lst[:] = [r for r in lst if r[0] != semname]
            lst.append((semname, n))
        self.n_instr += 1
        return n

    def op(self, eng, fn, reads=(), writes=(), sig=True):
        semname = "c_" + eng
        self._deps(eng, reads, writes)
        self.ops[eng].append(("op", fn, self.sems[semname], 1 if sig else 0))
        return self._finish(semname, reads, writes, sig)

    def dma(self, q, lane, out, in_, reads=(), writes=(), **kw):
        semname = "d_" + lane
        self._sem(semname)
        self._deps(q, reads, writes)
        self.ops[q].append(("op", lambda e: e.dma_start(out=out, in_=in_, **kw), self.sems[semname], 16))
        return self._finish(semname, reads, writes)

    def wait_all(self, eng):
        for s, c in self.cnt.items():
            if c == 0 or self.waited[eng].get(s, 0) >= c:
                continue
            self.waited[eng][s] = c
            mult = 16 if s.startswith("d_") else 1
            self.ops[eng].append(("wait", self.sems[s], c * mult))

    def nxt(self, name, n):
        i = self.rot.get(name, 0)
        self.rot[name] = i + 1
        return i % n

    def emit(self):
        nc = self.nc
        ops = self.ops

        def run(e, eo):
            for it in ops[e]:
                if it[0] == "wait":
                    eo.wait_ge(it[1], it[2])
                elif it[3] == 0:
                    it[1](eo)
                else:
                    it[1](eo).then_inc(it[2], it[3])

        with nc.Block() as block:
            @block.tensor
            def _(t):
                run("pe", t)

            @block.scalar
            def _(t):
                run("act", t)

            @block.vector
            def _(t):
                run("dve", t)

            @block.gpsimd
            def _(t):
                run("pool", t)

            @block.sync
            def _(t):
                run("sp", t)
        self.es.close()


def _rstd(P, ss, lnv, out, epsb, dim, rk, wk):
    P.op("act", lambda e: e.activation(out=lnv, in_=ss, func=AF.Ln, scale=1.0 / dim, bias=epsb),
         reads=list(rk) + ["epsb"], writes=[wk + "_ln"])
    P.op("act", lambda e: e.activation(out=out, in_=lnv, func=AF.Exp, scale=-0.5),
         reads=[wk + "_ln"], writes=[wk])


def build_token(TS, first):
    nc = bass.Bass("TRN2", target_bir_lowering=False)

    def din(name, shape, dt=F32):
        return nc.dram_tensor(name, list(shape), dt, kind="ExternalInput").ap()

    x = din("x", [TS, D])
    gn = din("gn", [D])
    identd = din("ident", [128, 128], BF16)
    hT = nc.dram_tensor("hT", [D, TS], BF16, kind="ExternalOutput").ap()
    if not first:
        oT = din("oT", [D, TS], BF16)
        pin = din("p", [TS, 256])
        w_out = din("w_out", [D, D])
        w1 = din("w1", [D, DFF])
        w2 = din("w2", [DFF, D])
        wp = din("ple_w", [256, D])
        wg = din("gate_w", [D, D])
        g3 = din("g", [3, D])
        xo = nc.dram_tensor("xo", [TS, D], F32, kind="ExternalOutput").ap()

    P = Prog(nc)
    ident = P.sbuf("ident", [128, 128], BF16)
    epsb = P.sbuf("epsb", [128, 1], F32)
    gnb = P.sbuf("gnb", [128, D], F32)
    xt = P.sbuf("xt", [128, 4, D], F32)
    actT = P.sbuf("actT", [128, 8, 512], BF16)
    hb = P.sbuf("hb", [128, D], BF16)
    junk = P.sbuf("junk", [128, D], BF16)
    st = P.sbuf("st", [128, 64], F32)
    tp = [P.psum("tp%d" % i, [128, 4, 128], BF16) for i in range(2)]
    P.dma("sp", "c0", ident[:], identd[:, :], writes=["ident"])
    P.dma("sp", "c1", gnb[:], gn.partition_broadcast(128), writes=["gnb"])
    P.op("dve", lambda e: e.memset(epsb[:], EPS), writes=["epsb"])
    if not first:
        gb = [P.sbuf("gb%d" % i, [128, D], F32) for i in range(3)]
        for i in range(3):
            P.dma("sp", "c2", gb[i][:], g3[i].partition_broadcast(128), writes=["gb%d" % i])
        yt = P.sbuf("yt", [128, 4, D], F32)
        oTs = P.sbuf("oTs", [128, 8, 512], BF16)
        aT = P.sbuf("aT", [128, 32, 512], BF16)
        wpool = [P.sbuf("wpool%d" % i, [128, 8, 512], BF16) for i in range(4)]
        wps = P.sbuf("wps", [128, 2, 512], BF16)
        pT = P.sbuf("pT", [128, 2, 512], BF16)
        pt = P.sbuf("pt", [128, 4, 256], F32)
        pb = P.sbuf("pb", [128, 256], BF16)
        sq = [P.sbuf("sq%d" % i, [128, 512], BF16) for i in range(2)]
        ev = [P.sbuf("ev%d" % i, [128, 512], F32) for i in range(2)]
        pm = [P.psum("pm%d" % i, [128, 512], F32) for i in range(6)]

    def wload(src3d, nchunk=8):
        i = P.nxt("wpool", 4)
        P.dma(_WQ[P.nxt("wq", len(_WQ))], "w%d" % i, wpool[i][:, 0:nchunk, :], src3d, writes=["wpool%d" % i])
        return wpool[i], "wpool%d" % i

    def transposes(src_bf, srck, dst3, dstk, i, nchunk):
        for c0 in range(0, nchunk, 4):
            nn = min(4, nchunk - c0)
            b = P.nxt("tp", 2)
            for c in range(nn):
                P.op("pe", lambda e, c=c, b=b, c0=c0: e.transpose(out=tp[b][:, c, :], in_=src_bf[:, (c0 + c) * 128:(c0 + c + 1) * 128], identity=ident[:]),
                     reads=[srck, "ident"], writes=["tp%d" % b])
            eng = "act" if (P.nxt("tpe", 2) == 0) else "dve"
            if eng == "act":
                P.op("act", lambda e, b=b, c0=c0, nn=nn: e.copy(out=dst3[:, c0:c0 + nn, i * 128:(i + 1) * 128], in_=tp[b][:, 0:nn, :]),
                     reads=["tp%d" % b], writes=[dstk])
            else:
                P.op("dve", lambda e, b=b, c0=c0, nn=nn: e.tensor_copy(out=dst3[:, c0:c0 + nn, i * 128:(i + 1) * 128], in_=tp[b][:, 0:nn, :]),
                     reads=["tp%d" % b], writes=[dstk])

    def norm_to_bf(i, gtile, gk, sc):
        P.op("act", lambda e: e.activation(out=junk[:], in_=xt[:, i, :], func=AF.Square, accum_out=st[:, sc:sc + 1]),
             reads=["xt%d" % i], writes=["junk", "st%d" % sc])
        _rstd(P, st[:, sc:sc + 1], st[:, sc + 1:sc + 2], st[:, sc + 2:sc + 3], epsb[:], D, ["st%d" % sc], "st%d" % (sc + 2))
        P.op("dve", lambda e: e.scalar_tensor_tensor(out=hb[:], in0=xt[:, i, :], scalar=st[:, sc + 2:sc + 3], in1=gtile[:], op0=ALU.mult, op1=ALU.mult),
             reads=["xt%d" % i, "st%d" % (sc + 2), gk], writes=["hb"])

    def resid_norm(i, gtile, gk, sc):
        P.op("dve", lambda e: e.tensor_tensor(out=st[:, sc + 2:sc + 3], in0=st[:, sc:sc + 1], in1=st[:, sc + 1:sc + 2], op=ALU.add),
             reads=["st%d" % sc, "st%d" % (sc + 1)], writes=["st%d" % (sc + 2)])
        _rstd(P, st[:, sc + 2:sc + 3], st[:, sc + 3:sc + 4], st[:, sc + 4:sc + 5], epsb[:], D, ["st%d" % (sc + 2)], "st%d" % (sc + 4))
        P.op("dve", lambda e: e.scalar_tensor_tensor(out=yt[:, i, :], in0=yt[:, i, :], scalar=st[:, sc + 4:sc + 5], in1=gtile[:], op0=ALU.mult, op1=ALU.mult),
             reads=["st%d" % (sc + 4), gk], writes=["yt%d" % i])
        P.op("dve", lambda e: e.tensor_tensor(out=xt[:, i, :], in0=yt[:, i, :], in1=xt[:, i, :], op=ALU.add),
             reads=["yt%d" % i], writes=["xt%d" % i])

    def evac_y(bank, i, n, sc):
        b = P.nxt("sq", 2)
        P.op("dve", lambda e: e.tensor_copy(out=yt[:, i, n * 512:(n + 1) * 512], in_=pm[bank][:]),
             reads=["pm%d" % bank], writes=["yt%d" % i])
        P.op("act", lambda e: e.activation(out=sq[b][:], in_=yt[:, i, n * 512:(n + 1) * 512], func=AF.Square, accum_out=st[:, sc + n:sc + n + 1]),
             reads=["yt%d" % i], writes=["sq%d" % b, "st%d" % (sc + n)])

    for grp in range(TS // 512):
        t0 = grp * 512
        for i in range(4):
            P.dma("sp", "xt", xt[:, i, :], x[t0 + i * 128:t0 + (i + 1) * 128, :], writes=["xt%d" % i])
        if not first:
            P.dma("sp", "oTs", oTs[:], oT[:, t0:t0 + 512].rearrange("(c p) t -> p c t", p=128), writes=["oTs"])
            P.dma("sp", "pt", pt[:], pin[t0:t0 + 512, :].rearrange("(i p) d -> p i d", p=128), writes=["pt"])
            for n in range(2 if 'a' in _STAGES else 0):
                wt, wk = wload(w_out[:, n * 512:(n + 1) * 512].rearrange("(c p) f -> p c f", p=128))
                for i in range(4 if 'm' in _STAGES else 0):
                    bank = P.nxt("pm", 6)
                    for c in range(8):
                        P.op("pe", lambda e, c=c, i=i, wt=wt, bank=bank: e.matmul(pm[bank][:], lhsT=oTs[:, c, i * 128:(i + 1) * 128], rhs=wt[:, c, :], start=(c == 0), stop=(c == 7)), sig=(c == 7),
                             reads=["oTs", wk], writes=["pm%d" % bank])
                    if 'e' in _STAGES:
                        evac_y(bank, i, n, 8 * i)
            for i in range(4 if 'r' in _STAGES else 0):
                resid_norm(i, gb[0], "gb0", 8 * i)
            for i in range(4 if 'b' in _STAGES else 0):
                norm_to_bf(i, gb[1], "gb1", 8 * i + 5)
                transposes(hb, "hb", actT, "actT", i, 8)
            for ffc in range(8 if 'c' in _STAGES else 0):
                wt, wk = wload(w1[:, ffc * 512:(ffc + 1) * 512].rearrange("(c p) f -> p c f", p=128))
                for sub in range(4):
                    bank = P.nxt("pm", 6)
                    for c in range(8):
                        P.op("pe", lambda e, c=c, sub=sub, wt=wt, bank=bank: e.matmul(pm[bank][:], lhsT=wt[:, c, sub * 128:(sub + 1) * 128], rhs=actT[:, c, :], start=(c == 0), stop=(c == 7)), sig=(c == 7),
                             reads=["actT", wk], writes=["pm%d" % bank])
                    b = P.nxt("sq", 2)
                    s = ffc * 4 + sub
                    P.op("act", lambda e, b=b, bank=bank: e.activation(out=sq[b][:], in_=pm[bank][:], func=AF.Square),
                         reads=["pm%d" % bank], writes=["sq%d" % b])
                    P.op("dve", lambda e, b=b, bank=bank, s=s: e.scalar_tensor_tensor(out=aT[:, s, :], in0=pm[bank][:], scalar=0.0, in1=sq[b][:], op0=ALU.is_gt, op1=ALU.mult),
                         reads=["pm%d" % bank, "sq%d" % b], writes=["aT%d" % s])
            for n in range(2 if 'B' in _STAGES else 0):
                banks = [P.nxt("pm", 6) for _ in range(4)]
                for q in range(4):
                    wt, wk = wload(w2[q * 1024:(q + 1) * 1024, n * 512:(n + 1) * 512].rearrange("(s p) m -> p s m", p=128))
                    for i in range(4):
                        for s in range(8):
                            ss_ = q * 8 + s
                            P.op("pe", lambda e, i=i, s=s, ss_=ss_, wt=wt, bank=banks[i], q=q: e.matmul(pm[bank][:], lhsT=aT[:, ss_, i * 128:(i + 1) * 128], rhs=wt[:, s, :], start=(ss_ == 0), stop=(ss_ == 31)),
                                 reads=["aT%d" % ss_, wk], writes=["pm%d" % banks[i]])
                for i in range(4):
                    evac_y(banks[i], i, n, 8 * i)
            for i in range(4 if 'B' in _STAGES else 0):
                resid_norm(i, gb[2], "gb2", 8 * i)
            for i in range(4 if 'd' in _STAGES else 0):
                P.op("act", lambda e, i=i: e.copy(out=hb[:], in_=xt[:, i, :]), reads=["xt%d" % i], writes=["hb"])
                transposes(hb, "hb", actT, "actT", i, 8)
                P.op("dve", lambda e, i=i: e.tensor_copy(out=pb[:], in_=pt[:, i, :]), reads=["pt"], writes=["pb"])
                transposes(pb, "pb", pT, "pT", i, 2)
            for n in range(2 if 'D' in _STAGES else 0):
                wt, wk = wload(wg[:, n * 512:(n + 1) * 512].rearrange("(c p) f -> p c f", p=128))
                P.dma("pool", "wps", wps[:], wp[:, n * 512:(n + 1) * 512].rearrange("(c p) f -> p c f", p=128), writes=["wps"])
                for i in range(4):
                    bank = P.nxt("pm", 6)
                    for c in range(8):
                        P.op("pe", lambda e, c=c, i=i, wt=wt, bank=bank: e.matmul(pm[bank][:], lhsT=actT[:, c, i * 128:(i + 1) * 128], rhs=wt[:, c, :], start=(c == 0), stop=(c == 7)), sig=(c == 7),
                             reads=["actT", wk], writes=["pm%d" % bank])
                    bank2 = P.nxt("pm", 6)
                    for c in range(2):
                        P.op("pe", lambda e, c=c, i=i, bank2=bank2: e.matmul(pm[bank2][:], lhsT=pT[:, c, i * 128:(i + 1) * 128], rhs=wps[:, c, :], start=(c == 0), stop=(c == 1)),
                             reads=["pT", "wps"], writes=["pm%d" % bank2])
                    b = P.nxt("ev", 2)
                    P.op("act", lambda e, b=b, bank=bank: e.activation(out=ev[b][:], in_=pm[bank][:], func=AF.Exp, scale=-1.0),
                         reads=["pm%d" % bank], writes=["ev%d" % b])
                    P.op("dve", lambda e, b=b: e.tensor_scalar_add(out=ev[b][:], in0=ev[b][:], scalar1=1.0), reads=[], writes=["ev%d" % b])
                    P.op("dve", lambda e, b=b: e.reciprocal(out=ev[b][:], in_=ev[b][:]), reads=[], writes=["ev%d" % b])
                    P.op("dve", lambda e, b=b, bank2=bank2: e.tensor_tensor(out=ev[b][:], in0=pm[bank2][:], in1=ev[b][:], op=ALU.mult),
                         reads=["pm%d" % bank2], writes=["ev%d" % b])
                    P.op("dve", lambda e, b=b, i=i, n=n: e.tensor_tensor(out=xt[:, i, n * 512:(n + 1) * 512], in0=xt[:, i, n * 512:(n + 1) * 512], in1=ev[b][:], op=ALU.add),
                         reads=["ev%d" % b], writes=["xt%d" % i])
            for i in range(4):
                P.dma("sp", "xo", xo[t0 + i * 128:t0 + (i + 1) * 128, :], xt[:, i, :], reads=["xt%d" % i], writes=["xo"])
        for i in range(4):
            norm_to_bf(i, gnb, "gnb", 8 * i + 5)
            transposes(hb, "hb", actT, "actT", i, 8)
        P.dma("sp", "hT", hT[:, t0:t0 + 512].rearrange("(c p) t -> p c t", p=128), actT[:], reads=["actT"], writes=["hTd"])
    P.wait_all("sp")
    P.emit()
    return nc


def build_head(kind, S, lam_init=0.0):
    nc = bass.Bass("TRN2", target_bir_lowering=False)
    NT = S // 512
    NKB = S // 128

    def din(name, shape, dt=F32):
        return nc.dram_tensor(name, list(shape), dt, kind="ExternalInput").ap()

    hT = din("hT", [D, S], BF16)
    identd = din("ident", [128, 128], BF16)
    wq = din("wq", [D, 256])
    wk = din("wk", [D, 256])
    wv = din("wv", [D, 256])
    oT = nc.dram_tensor("oT", [256, S], BF16, kind="ExternalOutput").ap()
    if kind == "da":
        tabd = din("tab", [4, 128, 1024], BF16)
        farbd = din("farb", [128, 4])
        lamd = din("lam", [256])
        sublnd = din("subln", [128])
        lamcd = din("lamc", [128, 2])
        KQ, NVH, DV, NTAB = 64, 2, 128, 4
    else:
        tabd = din("tab", [1, 128, 1024], BF16)
        wfd = din("wf", [D, 4])
        bfd = din("bf", [16])
        trid = din("tri", [128, 128])
        onesd = din("ones", [128, 128])
        KQ, NVH, DV, NTAB = 66, 4, 64, 1

    P = Prog(nc)
    ident = P.sbuf("ident", [128, 128], BF16)
    wqs = P.sbuf("wqs", [128, 8, 256], BF16)
    wks = P.sbuf("wks", [128, 8, 256], BF16)
    wvs = P.sbuf("wvs", [128, 8, 256], BF16)
    kT = [P.sbuf("kT%d" % h, [KQ, S], BF16) for h in range(4)]
    qT = [[P.sbuf("qT%d_%d" % (h, b), [KQ, 512], BF16) for b in range(2)] for h in range(4)]
    vext = P.sbuf("vext", [128, NKB, NVH, DV + 1], BF16)
    hts = [P.sbuf("hts%d" % b, [128, 8, 512], BF16) for b in range(2)]
    pstg = [P.sbuf("pstg%d" % b, [128, 4, 512], BF16) for b in range(2)]
    tabs = P.sbuf("tabs", [128, NTAB, 1024], BF16)
    pTb = [P.sbuf("pT%d" % i, [128, 512], BF16) for i in range(4)]
    small = P.sbuf("small", [128, 64], F32)
    epsb = P.sbuf("epsb", [128, 1], F32)
    ofin = P.sbuf("ofin", [128, 4, 256 if kind == "fox" else 128], BF16)
    oTs = P.sbuf("oTs", [128, 2, 512], BF16)
    gbank = [P.psum("g%d" % i, [128, 512], F32) for i in range(3)]
    if kind == "da":
        ob = [P.psum("ob%d" % i, [128, 2, DV + 1], F32) for i in range(4)]
    else:
        ob = [P.psum("ob%d" % i, [128, 4, DV + 1], F32) for i in range(3)]
        px = P.psum("px", [128, 3, 16], F32)
    tp = P.psum("tp", [128, 4, 128], BF16)

    P.dma("sp", "c0", ident[:], identd[:, :], writes=["ident"])
    P.dma("sp", "c1", tabs[:], tabd.rearrange("h p m -> p h m"), writes=["tabs"])
    P.dma("pool", "w0", wqs[:], wq.rearrange("(c p) f -> p c f", p=128), writes=["wqs"])
    P.dma("pool", "w1", wks[:], wk.rearrange("(c p) f -> p c f", p=128), writes=["wks"])
    P.dma("pool", "w2", wvs[:], wv.rearrange("(c p) f -> p c f", p=128), writes=["wvs"])
    P.op("dve", lambda e: e.memset(epsb[:], EPS), writes=["epsb"])
    P.op("pool", lambda e: e.memset(vext[:], 1.0), writes=["vext%d" % kb for kb in range(NKB)])
    smallc = [0]

    def scol(n=1):
        c = smallc[0]
        if c + n > 64:
            c = 0
        smallc[0] = c + n
        return c

    if kind == "da":
        farb = P.sbuf("farb", [128, 4], F32)
        lamb = P.sbuf("lamb", [128, 256], F32)
        gsub = P.sbuf("gsub", [128, 128], F32)
        ltmp = P.sbuf("ltmp", [128, 64], F32)
        lsm = P.sbuf("lsm", [128, 8], F32)
        on0 = P.sbuf("on0", [128, 4, 128], F32)
        comb = P.sbuf("comb", [128, 4, 128], F32)
        junk = P.sbuf("junk", [128, 128], BF16)
        ssb = P.sbuf("ssb", [128, 12], F32)
        P.dma("sp", "c2", farb[:], farbd[:, :], writes=["farb"])
        P.dma("sp", "c3", lamb[:], lamd.partition_broadcast(128), writes=["lamb"])
        P.dma("sp", "c4", gsub[:], sublnd.partition_broadcast(128), writes=["gsub"])
        lamc = P.sbuf("lamc", [128, 2], F32)
        P.dma("sp", "c5", lamc[:], lamcd[:, :], writes=["lamc"])
        P.op("dve", lambda e: e.tensor_scalar_mul(out=gsub[:], in0=gsub[:], scalar1=lamc[:, 0:1]), reads=["lamc"], writes=["gsub"])
        for q in range(2):
            P.op("dve", lambda e, q=q: e.tensor_tensor(out=ltmp[:], in0=lamb[:, q * 128:q * 128 + 64], in1=lamb[:, q * 128 + 64:q * 128 + 128], op=ALU.mult),
                 reads=["lamb"], writes=["ltmp"])
            P.op("dve", lambda e, q=q: e.reduce_sum(out=lsm[:, q:q + 1], in_=ltmp[:], axis=mybir.AxisListType.X),
                 reads=["ltmp"], writes=["lsm%d" % q])
        P.op("act", lambda e: e.activation(out=lsm[:, 2:4], in_=lsm[:, 0:2], func=AF.Exp), reads=["lsm0", "lsm1"], writes=["lsm23"])
        P.op("dve", lambda e: e.tensor_tensor(out=lsm[:, 4:5], in0=lsm[:, 3:4], in1=lsm[:, 2:3], op=ALU.subtract), reads=["lsm23"], writes=["lsm4"])
        P.op("dve", lambda e: e.tensor_tensor(out=lsm[:, 5:6], in0=lsm[:, 4:5], in1=lamc[:, 1:2], op=ALU.add), reads=["lsm4", "lamc"], writes=["nlam"])
        nlam = lsm[:, 5:6]
    else:
        wfs = P.sbuf("wfs", [128, 8, 4], BF16)
        bfb = P.sbuf("bfb", [128, 16], F32)
        tri = P.sbuf("tri", [128, 128], F32)
        onesm = P.sbuf("onesm", [128, 128], F32)
        ncall = P.sbuf("ncall", [128, NKB, 4], F32)
        ncb = P.sbuf("ncb", [128, NKB + 1, 4], F32)
        biasK = [P.sbuf("biasK%d" % i, [128, NKB, 4], F32) for i in range(2)]
        fu = P.sbuf("fu", [128, 16], F32)
        fsp = P.sbuf("fsp", [128, 16], F32)
        cwq = P.sbuf("cwq", [128, 4, 4], F32)
        chi = P.sbuf("chi", [128, 4, 4], BF16)
        chf = P.sbuf("chf", [128, 4, 4], F32)
        cwx = P.sbuf("cwx", [128, 4, 4, 66], BF16)
        P.dma("pool", "w3", wfs[:], wfd.rearrange("(c p) f -> p c f", p=128), writes=["wfs"])
        P.dma("sp", "c2", bfb[:], bfd.partition_broadcast(128), writes=["bfb"])
        P.dma("sp", "c3", tri[:], trid[:, :], writes=["tri"])
        P.dma("sp", "c4", onesm[:], onesd[:, :], writes=["onesm"])
        P.op("dve", lambda e: e.memset(cwx[:], 0.0), writes=["cwx"])
        P.op("dve", lambda e: e.memset(ncb[:, 0, :], 0.0), writes=["ncb0"])
        for h in range(4):
            P.op("dve", lambda e, h=h: e.memset(kT[h][64:66, :], 1.0), writes=["kT%d_ones" % h])

    def project(T):
        b = T % 2
        P.dma("sp", "hts%d" % b, hts[b][:], hT[:, T * 512:(T + 1) * 512].rearrange("(c p) t -> p c t", p=128), writes=["hts%d" % b])
        for hp in range(2):
            hA, hB = 2 * hp, 2 * hp + 1
            g = P.nxt("g", 3)
            for c in range(8):
                P.op("pe", lambda e, c=c, hp=hp, g=g: e.matmul(gbank[g][:, :], lhsT=wqs[:, c, hp * 128:(hp + 1) * 128], rhs=hts[b][:, c, :], start=(c == 0), stop=(c == 7)), sig=(c == 7),
                     reads=["wqs", "hts%d" % b], writes=["g%d" % g])
            P.op("dve", lambda e, hA=hA, g=g: e.tensor_scalar_mul(out=qT[hA][b][0:64, :], in0=gbank[g][0:64, :], scalar1=0.125),
                 reads=["g%d" % g], writes=["qT%d_%d" % (hA, b)])
            P.op("dve", lambda e, hp=hp, g=g: e.tensor_scalar_mul(out=pstg[b][64:128, hp, :], in0=gbank[g][64:128, :], scalar1=0.125),
                 reads=["g%d" % g], writes=["pstg%d_%d" % (b, hp)])
            P.dma("sp", "mvq%d" % hp, qT[hB][b][0:64, :], pstg[b][64:128, hp, :], reads=["pstg%d_%d" % (b, hp)], writes=["qT%d_%d" % (hB, b)])
            g = P.nxt("g", 3)
            for c in range(8):
                P.op("pe", lambda e, c=c, hp=hp, g=g: e.matmul(gbank[g][:, :], lhsT=wks[:, c, hp * 128:(hp + 1) * 128], rhs=hts[b][:, c, :], start=(c == 0), stop=(c == 7)), sig=(c == 7),
                     reads=["wks", "hts%d" % b], writes=["g%d" % g])
            P.op("dve", lambda e, hA=hA, g=g: e.tensor_copy(out=kT[hA][0:64, T * 512:(T + 1) * 512], in_=gbank[g][0:64, :]),
                 reads=["g%d" % g], writes=["kT%d_%d" % (hA, T)])
            P.op("dve", lambda e, hp=hp, g=g: e.tensor_copy(out=pstg[b][64:128, 2 + hp, :], in_=gbank[g][64:128, :]),
                 reads=["g%d" % g], writes=["pstg%d_%d" % (b, 2 + hp)])
            P.dma("sp", "mvk%d" % hp, kT[hB][0:64, T * 512:(T + 1) * 512], pstg[b][64:128, 2 + hp, :], reads=["pstg%d_%d" % (b, 2 + hp)], writes=["kT%d_%d" % (hB, T)])
        for i in range(4):
            g = P.nxt("g", 3)
            kb = 4 * T + i
            for c in range(8):
                P.op("pe", lambda e, c=c, i=i, g=g: e.matmul(gbank[g][:, 0:256], lhsT=hts[b][:, c, i * 128:(i + 1) * 128], rhs=wvs[:, c, :], start=(c == 0), stop=(c == 7)), sig=(c == 7),
                     reads=["wvs", "hts%d" % b], writes=["g%d" % g])
            P.op("dve", lambda e, g=g, kb=kb: e.tensor_copy(out=vext[:, kb, :, 0:DV], in_=gbank[g][:, 0:256].rearrange("p (v d) -> p v d", v=NVH)),
                 reads=["g%d" % g], writes=["vext%d" % kb])
        if kind == "fox":
            for i in range(4):
                for c in range(8):
                    P.op("pe", lambda e, c=c, i=i: e.matmul(px[:, 0, i * 4:(i + 1) * 4], lhsT=hts[b][:, c, i * 128:(i + 1) * 128], rhs=wfs[:, c, :], start=(c == 0 and i == 0), stop=(c == 7), skip_group_check=True),
                         reads=["wfs", "hts%d" % b], writes=["px"])
            P.op("dve", lambda e: e.tensor_tensor(out=fu[:], in0=px[:, 0, :], in1=bfb[:], op=ALU.add), reads=["px", "bfb"], writes=["fu"])
            P.op("act", lambda e: e.activation(out=fu[:], in_=fu[:], func=AF.Exp, scale=-1.0), reads=[], writes=["fu"])
            P.op("act", lambda e: e.activation(out=fsp[:], in_=fu[:], func=AF.Ln, bias=1.0), reads=["fu"], writes=["fsp"])
            P.op("pe", lambda e: e.matmul(px[:, 1, :], lhsT=tri[:], rhs=fsp[:], start=True, stop=True), reads=["tri", "fsp"], writes=["px"])
            P.op("pe", lambda e: e.matmul(px[:, 2, :], lhsT=onesm[:], rhs=fsp[:], start=True, stop=True), reads=["onesm", "fsp"], writes=["px"])
            for i in range(4):
                kb = 4 * T + i
                P.op("dve", lambda e, i=i, kb=kb: e.tensor_tensor(out=ncall[:, kb, :], in0=px[:, 1, i * 4:(i + 1) * 4], in1=ncb[:, kb, :], op=ALU.add),
                     reads=["px", "ncb%d" % kb], writes=["ncall%d" % kb])
                P.op("dve", lambda e, i=i, kb=kb: e.tensor_tensor(out=ncb[:, kb + 1, :], in0=px[:, 2, i * 4:(i + 1) * 4], in1=ncb[:, kb, :], op=ALU.add),
                     reads=["px", "ncb%d" % kb], writes=["ncb%d" % (kb + 1)])
            bk = biasK[T % 2]
            for kb in range(4 * T + 4):
                P.op("dve", lambda e, kb=kb, bk=bk: e.tensor_tensor(out=bk[:, kb, :], in0=ncall[:, kb, :], in1=ncb[:, 4 * T, :], op=ALU.subtract),
                     reads=["ncall%d" % kb, "ncb%d" % (4 * T)], writes=["biasK%d_%d" % (T % 2, kb)])
            for i in range(4):
                P.op("dve", lambda e, i=i: e.tensor_tensor(out=cwq[:, i, :], in0=ncb[:, 4 * T, :], in1=ncall[:, 4 * T + i, :], op=ALU.subtract),
                     reads=["ncall%d" % (4 * T + i), "ncb%d" % (4 * T)], writes=["cwq"])
            P.op("dve", lambda e: e.tensor_copy(out=chi[:], in_=cwq[:]), reads=["cwq"], writes=["chi"])
            P.op("dve", lambda e: e.tensor_copy(out=chf[:], in_=chi[:]), reads=["chi"], writes=["chf"])
            P.op("dve", lambda e: e.tensor_tensor(out=chf[:], in0=cwq[:], in1=chf[:], op=ALU.subtract), reads=["cwq"], writes=["chf"])
            P.op("dve", lambda e: e.tensor_copy(out=cwx[:, :, :, 64], in_=chi[:]), reads=["chi"], writes=["cwx"])
            P.op("dve", lambda e: e.tensor_copy(out=cwx[:, :, :, 65], in_=chf[:]), reads=["chf"], writes=["cwx"])
            for h in range(4):
                g = P.nxt("g", 3)
                for i in range(4):
                    P.op("pe", lambda e, i=i, h=h, g=g: e.matmul(gbank[g][0:66, i * 128:(i + 1) * 128], lhsT=cwx[:, i, h, :], rhs=ident[:], start=True, stop=True),
                         reads=["cwx", "ident"], writes=["g%d" % g])
                P.op("dve", lambda e, h=h, g=g: e.tensor_copy(out=qT[h][b][64:66, :], in_=gbank[g][64:66, :]),
                     reads=["g%d" % g], writes=["qT%d_%d" % (h, b)])

    def attention(T, h, vh, obanks, oregion):
        b = T % 2
        nkb = 4 * T + 4
        pend = []

        def pv(kb, pb):
            js = [j for j in range(4) if kb <= 4 * T + j]
            for j in js:
                oap, okey, bfirst = oregion(j)
                P.op("pe", lambda e, j=j, oap=oap, kb=kb, pb=pb, bfirst=bfirst: e.matmul(oap, lhsT=pTb[pb][:, j * 128:(j + 1) * 128], rhs=vext[:, kb, vh, :], start=(kb == 0 and bfirst), stop=(kb == 4 * T + j), skip_group_check=True),
                     reads=["pT%d" % pb, "vext%d" % kb], writes=[okey], sig=(j == js[-1]))

        for kb in range(nkb):
            a = T * 512 - kb * 128
            near = a < 256
            g = P.nxt("g", 3)
            kreads = ["kT%d_%d" % (h, kb // 4), "qT%d_%d" % (h, b)] + (["kT%d_ones" % h] if kind == "fox" else [])
            P.op("pe", lambda e, g=g, kb=kb, near=near: e.matmul(gbank[g][:], lhsT=kT[h][:, kb * 128:(kb + 1) * 128], rhs=qT[h][b][:, :], start=True, stop=not near),
                 reads=kreads, writes=["g%d" % g], sig=(not near))
            if near:
                off = a + 384
                ti = h if kind == "da" else 0
                P.op("pe", lambda e, g=g, off=off, ti=ti: e.matmul(gbank[g][:], lhsT=ident[:], rhs=tabs[:, ti, off:off + 512], start=False, stop=True),
                     reads=["ident", "tabs"], writes=["g%d" % g])
            pb = P.nxt("pT", 4)
            if kind == "da":
                bias = 0.0 if near else farb[:, h:h + 1]
                br = [] if near else ["farb"]
            else:
                bias = biasK[T % 2][:, kb, h:h + 1]
                br = ["biasK%d_%d" % (T % 2, kb)]
            P.op("act", lambda e, g=g, pb=pb, bias=bias: e.activation(out=pTb[pb][:], in_=gbank[g][:], func=AF.Exp, bias=bias, scale=1.0),
                 reads=["g%d" % g] + br, writes=["pT%d" % pb])
            pend.append((kb, pb))
            if len(pend) > 2:
                pv(*pend.pop(0))
        for it_ in pend:
            pv(*it_)

    def flush_out(T, nchunk):
        P.dma("sp", "oT", oT[:, T * 512:(T + 1) * 512].rearrange("(c p) t -> p c t", p=128), oTs[:, 0:nchunk, :], reads=["oTs"], writes=["oTd"])

    for T in range(NT):
        project(T)
        for h in range(4):
            if kind == "da":
                H, c = h // 2, h % 2
                s = (T * 4 + h) % 2
                okeys = ["ob%d" % (2 * s), "ob%d" % (2 * s + 1)]
                attention(T, h, H, None, lambda j: (ob[2 * s + j // 2][:, j % 2, :], okeys[j // 2], j % 2 == 0))
                for j in range(4):
                    Oj = ob[2 * s + j // 2][:, j % 2, :]
                    ok = okeys[j // 2]
                    cc = scol(2)
                    P.op("dve", lambda e, Oj=Oj, cc=cc: e.reciprocal(out=small[:, cc:cc + 1], in_=Oj[:, DV:DV + 1]), reads=[ok], writes=["sm%d" % cc])
                    if c == 0:
                        P.op("dve", lambda e, Oj=Oj, cc=cc, j=j: e.tensor_scalar_mul(out=on0[:, j, :], in0=Oj[:, 0:DV], scalar1=small[:, cc:cc + 1]),
                             reads=[ok, "sm%d" % cc], writes=["on0_%d" % j])
                    else:
                        P.op("dve", lambda e, cc=cc: e.tensor_tensor(out=small[:, cc + 1:cc + 2], in0=small[:, cc:cc + 1], in1=nlam, op=ALU.mult),
                             reads=["sm%d" % cc, "nlam"], writes=["sm%d" % (cc + 1)])
                        P.op("dve", lambda e, Oj=Oj, cc=cc, j=j: e.scalar_tensor_tensor(out=comb[:, j, :], in0=Oj[:, 0:DV], scalar=small[:, cc + 1:cc + 2], in1=on0[:, j, :], op0=ALU.mult, op1=ALU.add),
                             reads=[ok, "sm%d" % (cc + 1), "on0_%d" % j], writes=["comb%d" % j])
                        P.op("act", lambda e, j=j: e.activation(out=junk[:], in_=comb[:, j, :], func=AF.Square, accum_out=ssb[:, j:j + 1]),
                             reads=["comb%d" % j], writes=["junk", "ssb%d" % j])
                if c == 1:
                    _rstd(P, ssb[:, 0:4], ssb[:, 4:8], ssb[:, 8:12], epsb[:], DV, ["ssb%d" % j for j in range(4)], "rstd")
                    for j in range(4):
                        P.op("dve", lambda e, j=j: e.scalar_tensor_tensor(out=ofin[:, j, :], in0=comb[:, j, :], scalar=ssb[:, 8 + j:9 + j], in1=gsub[:], op0=ALU.mult, op1=ALU.mult),
                             reads=["comb%d" % j, "rstd", "gsub"], writes=["ofin%d" % j])
                    for j in range(4):
                        P.op("pe", lambda e, j=j: e.transpose(out=tp[:, j, :], in_=ofin[:, j, :], identity=ident[:]),
                             reads=["ofin%d" % j, "ident"], writes=["tp"])
                    P.op("dve", lambda e, H=H: e.tensor_copy(out=oTs[:, H, :].rearrange("p (j q) -> p j q", j=4), in_=tp[:]),
                         reads=["tp"], writes=["oTs"])
            else:
                bnk = (T * 4 + h) % 3
                attention(T, h, h, None, lambda j: (ob[bnk][:, j, :], "ob%d" % bnk, j == 0))
                for j in range(4):
                    cc = scol(1)
                    P.op("dve", lambda e, cc=cc, j=j, bnk=bnk: e.reciprocal(out=small[:, cc:cc + 1], in_=ob[bnk][:, j, DV:DV + 1]), reads=["ob%d" % bnk], writes=["sm%d" % cc])
                    P.op("dve", lambda e, cc=cc, j=j, h=h, bnk=bnk: e.tensor_scalar_mul(out=ofin[:, j, h * 64:(h + 1) * 64], in0=ob[bnk][:, j, 0:DV], scalar1=small[:, cc:cc + 1]),
                         reads=["ob%d" % bnk, "sm%d" % cc], writes=["ofin%d" % j])
                if h == 3:
                    for cch in range(2):
                        for j in range(4):
                            P.op("pe", lambda e, j=j, cch=cch: e.transpose(out=tp[:, j, :], in_=ofin[:, j, cch * 128:(cch + 1) * 128], identity=ident[:]),
                                 reads=["ofin%d" % j, "ident"], writes=["tp"])
                        P.op("dve", lambda e, cch=cch: e.tensor_copy(out=oTs[:, cch, :].rearrange("p (j q) -> p j q", j=4), in_=tp[:]),
                             reads=["tp"], writes=["oTs"])
        flush_out(T, 2)
    P.wait_all("sp")
    P.emit()
    return nc


def t5_bucket_np(d):
    n = np.maximum(d, 0)
    nf = np.maximum(n, 1).astype(np.float32)
    large = 16 + (np.log(nf / np.float32(16)) / np.float32(math.log(8.0)) * np.float32(16)).astype(np.int32)
    large = np.minimum(large, 31)
    return np.where(n < 16, n, large)


def toeplitz_idx(width, amin, dmax=None):
    i = np.arange(128)[:, None]
    m = np.arange(width)[None, :]
    d = m + amin - i
    idx = t5_bucket_np(d)
    bad = d < 0
    if dmax is not None:
        bad = bad | (d >= dmax)
    return np.where(bad, 32, idx)


def build_nsa(S):
    nc = bass.Bass("TRN2", target_bir_lowering=False)
    NT = S // 512
    NKB = S // 128
    NCB = 4 if S >= 8192 else max(1, (S // 16 + 127) // 128)
    NCP = NCB * 128

    def din(name, shape, dt=F32):
        return nc.dram_tensor(name, list(shape), dt, kind="ExternalInput").ap()

    hT = din("hT", [D, S], BF16)
    identd = din("ident", [128, 128], BF16)
    wq = din("wq", [D, 256])
    wkv = din("wkv", [D, 384])
    wgt = din("wgt", [D, 12])
    peTd = din("peT", [64, 2, 32])
    cw1d = din("cw1", [64, 2, 32, 256])
    cw2d = din("cw2", [128, 2, 2, 64])
    tseld = din("tsel", [4, 128, 1024], BF16)
    tcmpd = din("tcmp", [4, 128, 2560], BF16)
    twind = din("twin", [4, 128, 1408], BF16)
    farbd = din("farb", [128, 4])
    gseld = din("gsel", [128, S], BF16)
    ovld = din("ovl", [128, NCB, 128], BF16)
    mMd = din("mM", [S, 128])
    mACd = din("mAC", [S, 128])
    oT = nc.dram_tensor("oT", [256, S], BF16, kind="ExternalOutput").ap()

    P = Prog(nc)
    ident = P.sbuf("ident", [128, 128], BF16)
    wqs = P.sbuf("wqs", [128, 8, 256], BF16)
    wkvs = P.sbuf("wkvs", [128, 8, 384], BF16)
    wgs = P.sbuf("wgs", [128, 8, 12], BF16)
    peT = P.sbuf("peT", [64, 2, 32], BF16)
    cw1 = P.sbuf("cw1", [64, 2, 32, 256], BF16)
    cw2 = P.sbuf("cw2", [128, 2, 2, 64], BF16)
    hbias = P.sbuf("hbias", [128, 4], F32)
    tsel = P.sbuf("tsel", [128, 4, 1024], BF16)
    tcmp = P.sbuf("tcmp", [128, 4, 2560], BF16)
    twin = P.sbuf("twin", [128, 4, 1408], BF16)
    farb = P.sbuf("farb", [128, 4], F32)
    gsel = P.sbuf("gsel", [128, S], BF16)
    kselT = P.sbuf("kselT", [64, S], BF16)
    kwinT = P.sbuf("kwinT", [64, S], BF16)
    vsw = P.sbuf("vsw", [128, NKB, 2, 65], BF16)
    cwin = [P.sbuf("cwin%d" % b, [64, 2, 528], BF16) for b in range(2)]
    kcT = P.sbuf("kcT", [64, NCP], BF16)
    vcT = P.sbuf("vcT", [64, NCP], BF16)
    vcx = P.sbuf("vcx", [128, NCB, 193], BF16)
    qT = [[P.sbuf("qT%d_%d" % (h, b), [64, 512], BF16) for b in range(2)] for h in range(4)]
    hts = [P.sbuf("hts%d" % b, [128, 8, 512], BF16) for b in range(2)]
    pstg = [P.sbuf("pstg%d" % b, [128, 4, 512], BF16) for b in range(2)]
    pTb = [P.sbuf("pT%d" % i, [128, 512], BF16) for i in range(4)]
    small = P.sbuf("small", [128, 64], F32)
    gates = P.sbuf("gates", [128, 4, 12], F32)
    ofin32 = P.sbuf("ofin32", [128, 4, 256], F32)
    ofin = P.sbuf("ofin", [128, 4, 256], BF16)
    oTs = P.sbuf("oTs", [128, 2, 512], BF16)
    impacc = P.sbuf("impacc", [128, 4, 128], F32)
    mM = P.sbuf("mM", [128, 4, 128], F32)
    mAC = P.sbuf("mAC", [128, 4, 128], F32)
    imp2 = P.sbuf("imp2", [128, 128], F32)
    m8 = P.sbuf("m8", [128, 16], F32)
    selb = P.sbuf("selb", [128, 128], BF16)
    selbT = P.sbuf("selbT", [128, 512], BF16)
    gh = P.sbuf("gh", [128, 2, 32], F32)
    gt = P.sbuf("gt", [128, 2, 32], F32)
    ghb = P.sbuf("ghb", [128, 2, 32], BF16)
    gbank = [P.psum("g%d" % i, [128, 512], F32) for i in range(3)]
    ob = [P.psum("ob%d" % i, [128, 512], F32) for i in range(3)]
    tp = P.psum("tp", [128, 4, 128], BF16)
    px = P.psum("px", [128, 128], F32)

    P.dma("sp", "c0", ident[:], identd[:, :], writes=["ident"])
    P.dma("sp", "c1", tsel[:], tseld.rearrange("h p m -> p h m"), writes=["tsel"])
    P.dma("sp", "c2", tcmp[:], tcmpd.rearrange("h p m -> p h m"), writes=["tcmp"])
    P.dma("sp", "c3", twin[:], twind.rearrange("h p m -> p h m"), writes=["twin"])
    P.dma("sp", "c4", farb[:], farbd[:, :], writes=["farb"])
    P.dma("sp", "c5", gsel[:], gseld[:, :], writes=["gsel"])
    P.dma("pool", "w0", wqs[:], wq.rearrange("(c p) f -> p c f", p=128), writes=["wqs"])
    P.dma("pool", "w1", wkvs[:], wkv.rearrange("(c p) f -> p c f", p=128), writes=["wkvs"])
    P.dma("pool", "w2", wgs[:], wgt.rearrange("(c p) f -> p c f", p=128), writes=["wgs"])
    P.dma("pool", "w3", peT[:], peTd[:, :, :], writes=["peT"])
    P.dma("pool", "w4", cw1[:], cw1d[:, :, :, :], writes=["cw1"])
    P.dma("pool", "w5", cw2[:], cw2d[:, :, :, :], writes=["cw2"])
    P.op("pool", lambda e: e.memset(vsw[:], 1.0), writes=["vsw%d" % kb for kb in range(NKB)])
    P.op("pool", lambda e: e.memset(vcx[:], 1.0), writes=["vcx"])
    P.dma("sp", "c6", vcx[:, :, 65:193], ovld[:, :, :], writes=["vcx"])
    P.op("dve", lambda e: e.memset(kcT[:], 0.0), writes=["kcT"])
    P.op("dve", lambda e: e.memset(vcT[:], 0.0), writes=["vcT"])
    P.op("dve", lambda e: e.memset(cwin[1][:], 0.0), writes=["cwin1"])
    for kv in range(2):
        for hc in range(2):
            col = kv * 2 + hc
            for l in range(32):
                P.op("pe", lambda e, kv=kv, hc=hc, l=l, col=col: e.matmul(px[:, col:col + 1], lhsT=cw1[:, kv, l, hc * 128:(hc + 1) * 128], rhs=peT[:, kv, l:l + 1], start=(l == 0 and col == 0), stop=(l == 31), skip_group_check=True),
                     reads=["cw1", "peT"], writes=["px"])
    P.op("dve", lambda e: e.tensor_copy(out=hbias[:], in_=px[:, 0:4]), reads=["px"], writes=["hbias"])
    smallc = [0]

    def scol(n=1):
        c = smallc[0]
        if c + n > 64:
            c = 0
        smallc[0] = c + n
        return c

    def project(T):
        b = T % 2
        P.dma("sp", "hts%d" % b, hts[b][:], hT[:, T * 512:(T + 1) * 512].rearrange("(c p) t -> p c t", p=128), writes=["hts%d" % b])
        P.dma("sp", "mM", mM[:], mMd[T * 512:(T + 1) * 512, :].rearrange("(j p) n -> p j n", p=128), writes=["mM"])
        P.dma("sp", "mAC", mAC[:], mACd[T * 512:(T + 1) * 512, :].rearrange("(j p) n -> p j n", p=128), writes=["mAC"])
        for hp in range(2):
            hA, hB = 2 * hp, 2 * hp + 1
            g = P.nxt("g", 3)
            for c in range(8):
                P.op("pe", lambda e, c=c, hp=hp, g=g: e.matmul(gbank[g][:, :], lhsT=wqs[:, c, hp * 128:(hp + 1) * 128], rhs=hts[b][:, c, :], start=(c == 0), stop=(c == 7)), sig=(c == 7),
                     reads=["wqs", "hts%d" % b], writes=["g%d" % g])
            P.op("dve", lambda e, hA=hA, g=g: e.tensor_scalar_mul(out=qT[hA][b][:, :], in0=gbank[g][0:64, :], scalar1=0.125),
                 reads=["g%d" % g], writes=["qT%d_%d" % (hA, b)])
            P.op("dve", lambda e, hp=hp, g=g: e.tensor_scalar_mul(out=pstg[b][64:128, hp, :], in0=gbank[g][64:128, :], scalar1=0.125),
                 reads=["g%d" % g], writes=["pstg%d_%d" % (b, hp)])
            P.dma("sp", "mvq%d" % hp, qT[hB][b][:, :], pstg[b][64:128, hp, :], reads=["pstg%d_%d" % (b, hp)], writes=["qT%d_%d" % (hB, b)])
        if T > 0:
            P.op("dve", lambda e: e.tensor_copy(out=cwin[b][:, :, 0:16], in_=cwin[1 - b][:, :, 512:528]), reads=["cwin%d" % (1 - b)], writes=["cwin%d" % b])
        dsts = [(cwin[b][:, 0, 16:528], "cwin%d" % b), (cwin[b][:, 1, 16:528], "cwin%d" % b),
                (kselT[:, T * 512:(T + 1) * 512], "kselT%d" % T), (kwinT[:, T * 512:(T + 1) * 512], "kwinT%d" % T)]
        for qp in range(2):
            g = P.nxt("g", 3)
            for c in range(8):
                P.op("pe", lambda e, c=c, qp=qp, g=g: e.matmul(gbank[g][:, :], lhsT=wkvs[:, c, qp * 128:(qp + 1) * 128], rhs=hts[b][:, c, :], start=(c == 0), stop=(c == 7)), sig=(c == 7),
                     reads=["wkvs", "hts%d" % b], writes=["g%d" % g])
            dA, dkA = dsts[2 * qp]
            dB, dkB = dsts[2 * qp + 1]
            P.op("dve", lambda e, dA=dA, g=g: e.tensor_copy(out=dA, in_=gbank[g][0:64, :]), reads=["g%d" % g], writes=[dkA])
            P.op("dve", lambda e, qp=qp, g=g: e.tensor_copy(out=pstg[b][64:128, 2 + qp, :], in_=gbank[g][64:128, :]), reads=["g%d" % g], writes=["pstg%d_%d" % (b, 2 + qp)])
            P.dma("sp", "mvk%d" % qp, dB, pstg[b][64:128, 2 + qp, :], reads=["pstg%d_%d" % (b, 2 + qp)], writes=[dkB])
        for i in range(4):
            g = P.nxt("g", 3)
            kb = 4 * T + i
            for c in range(8):
                P.op("pe", lambda e, c=c, i=i, g=g: e.matmul(gbank[g][:, 0:128], lhsT=hts[b][:, c, i * 128:(i + 1) * 128], rhs=wkvs[:, c, 256:384], start=(c == 0), stop=(c == 7)), sig=(c == 7),
                     reads=["wkvs", "hts%d" % b], writes=["g%d" % g])
            P.op("dve", lambda e, g=g, kb=kb: e.tensor_copy(out=vsw[:, kb, :, 0:64], in_=gbank[g][:, 0:128].rearrange("p (v d) -> p v d", v=2)),
                 reads=["g%d" % g], writes=["vsw%d" % kb])
        for i in range(4):
            for c in range(8):
                P.op("pe", lambda e, c=c, i=i: e.matmul(px[:, i * 12:(i + 1) * 12], lhsT=hts[b][:, c, i * 128:(i + 1) * 128], rhs=wgs[:, c, :], start=(c == 0 and i == 0), stop=(c == 7), skip_group_check=True),
                     reads=["wgs", "hts%d" % b], writes=["px"])
        P.op("act", lambda e: e.activation(out=gates[:].rearrange("p i c -> p (i c)"), in_=px[:, 0:48], func=AF.Exp, scale=-1.0), reads=["px"], writes=["gates"])
        P.op("dve", lambda e: e.tensor_scalar_add(out=gates[:], in0=gates[:], scalar1=1.0), reads=[], writes=["gates"])
        P.op("dve", lambda e: e.reciprocal(out=gates[:], in_=gates[:]), reads=[], writes=["gates"])

    def compress(T):
        b = T % 2
        u0 = 1 if T == 0 else 0
        NU = 32 - u0
        n0 = 32 * T - 1 + u0
        for kv in range(2):
            for hc in range(2):
                g = P.nxt("g", 3)
                for l in range(32):
                    c0 = 16 * u0 + l
                    P.op("pe", lambda e, kv=kv, hc=hc, l=l, c0=c0, g=g: e.matmul(gbank[g][:, 0:NU], lhsT=cw1[:, kv, l, hc * 128:(hc + 1) * 128], rhs=cwin[b][:, kv, c0:c0 + 16 * (NU - 1) + 1:16], start=(l == 0), stop=(l == 31)),
                         reads=["cw1", "cwin%d" % b], writes=["g%d" % g])
                P.op("dve", lambda e, kv=kv, hc=hc, g=g: e.tensor_scalar_add(out=gh[:, hc, 0:NU], in0=gbank[g][:, 0:NU], scalar1=hbias[:, kv * 2 + hc:kv * 2 + hc + 1]),
                     reads=["g%d" % g, "hbias"], writes=["gh%d" % hc])
                P.op("dve", lambda e, hc=hc: e.tensor_tensor(out=gt[:, hc, 0:NU], in0=gh[:, hc, 0:NU], in1=gh[:, hc, 0:NU], op=ALU.mult), reads=["gh%d" % hc], writes=["gt%d" % hc])
                P.op("dve", lambda e, hc=hc: e.tensor_scalar(out=gt[:, hc, 0:NU], in0=gt[:, hc, 0:NU], scalar1=0.044715, scalar2=1.0, op0=ALU.mult, op1=ALU.add), reads=[], writes=["gt%d" % hc])
                P.op("dve", lambda e, hc=hc: e.tensor_tensor(out=gt[:, hc, 0:NU], in0=gt[:, hc, 0:NU], in1=gh[:, hc, 0:NU], op=ALU.mult), reads=["gh%d" % hc], writes=["gt%d" % hc])
                P.op("act", lambda e, hc=hc: e.activation(out=gt[:, hc, 0:NU], in_=gt[:, hc, 0:NU], func=AF.Exp, scale=-1.5957691216057308), reads=[], writes=["gt%d" % hc])
                P.op("dve", lambda e, hc=hc: e.tensor_scalar_add(out=gt[:, hc, 0:NU], in0=gt[:, hc, 0:NU], scalar1=1.0), reads=[], writes=["gt%d" % hc])
                P.op("dve", lambda e, hc=hc: e.reciprocal(out=gt[:, hc, 0:NU], in_=gt[:, hc, 0:NU]), reads=[], writes=["gt%d" % hc])
                P.op("dve", lambda e, hc=hc: e.tensor_tensor(out=ghb[:, hc, 0:NU], in0=gt[:, hc, 0:NU], in1=gh[:, hc, 0:NU], op=ALU.mult), reads=["gt%d" % hc, "gh%d" % hc], writes=["ghb%d" % hc])
            g = P.nxt("g", 3)
            for hc in range(2):
                P.op("pe", lambda e, kv=kv, hc=hc, g=g: e.matmul(gbank[g][0:64, 0:NU], lhsT=cw2[:, kv, hc, :], rhs=ghb[:, hc, 0:NU], start=(hc == 0), stop=(hc == 1)),
                     reads=["cw2", "ghb%d" % hc], writes=["g%d" % g])
            dstT = kcT if kv == 0 else vcT
            P.op("dve", lambda e, g=g, dstT=dstT: e.tensor_copy(out=dstT[:, n0:n0 + NU], in_=gbank[g][0:64, 0:NU]),
                 reads=["g%d" % g], writes=["kcT" if kv == 0 else "vcT"])
        for nb in sorted(set([n0 // 128, (n0 + NU - 1) // 128])):
            P.op("pe", lambda e, nb=nb: e.transpose(out=tp[0:128, 0, 0:64], in_=vcT[:, nb * 128:(nb + 1) * 128], identity=ident[0:64, 0:64]),
                 reads=["vcT", "ident"], writes=["tp"])
            P.op("dve", lambda e, nb=nb: e.tensor_copy(out=vcx[:, nb, 0:64], in_=tp[:, 0, 0:64]), reads=["tp"], writes=["vcx"])

    def attention(T, h, kblist, qk, extra, biasf, vr, first, last, oregion):
        b = T % 2
        pend = []

        def pv(kb, pb):
            rhs, rk = vr(kb)
            js = [j for j in range(4) if first(j) <= kb <= last(j)]
            for j in js:
                oap, okey, bfirst = oregion(j)
                st_ = bool(kb == first(j) and bfirst)
                sp_ = bool(kb == last(j))
                P.op("pe", lambda e, j=j, oap=oap, kb=kb, pb=pb, st_=st_, sp_=sp_, rhs=rhs: e.matmul(oap, lhsT=pTb[pb][:, j * 128:(j + 1) * 128], rhs=rhs, start=st_, stop=sp_, skip_group_check=True),
                     reads=["pT%d" % pb] + rk, writes=[okey], sig=(j == js[-1]))

        for kb in kblist:
            g = P.nxt("g", 3)
            lhsT, kreads = qk(kb)
            ex = extra(kb)
            P.op("pe", lambda e, g=g, lhsT=lhsT, ex=ex: e.matmul(gbank[g][:], lhsT=lhsT, rhs=qT[h][b][:, :], start=True, stop=(len(ex) == 0)),
                 reads=kreads + ["qT%d_%d" % (h, b)], writes=["g%d" % g], sig=(len(ex) == 0))
            for xi, (xl, xr, xk) in enumerate(ex):
                P.op("pe", lambda e, g=g, xl=xl, xr=xr, xi=xi, ex=ex: e.matmul(gbank[g][:], lhsT=xl, rhs=xr, start=False, stop=(xi == len(ex) - 1)),
                     reads=xk, writes=["g%d" % g], sig=(xi == len(ex) - 1))
            pb = P.nxt("pT", 4)
            bias, br = biasf(kb)
            P.op("act", lambda e, g=g, pb=pb, bias=bias: e.activation(out=pTb[pb][:], in_=gbank[g][:], func=AF.Exp, bias=bias, scale=1.0),
                 reads=["g%d" % g] + br, writes=["pT%d" % pb])
            pend.append((kb, pb))
            if len(pend) > 2:
                pv(*pend.pop(0))
        for it_ in pend:
            pv(*it_)

    for T in range(NT):
        b = T % 2
        project(T)
        compress(T)
        nbs = [nb for nb in range(NCB) if T - 4 * nb >= 0]
        for r in range(4):
            bA = P.nxt("ob", 3)
            bB = P.nxt("ob", 3)
            bks = [bA, bB]

            def c_qk(nb):
                return kcT[:, nb * 128:(nb + 1) * 128], ["kcT"]

            def c_extra(nb, r=r):
                dl = T - 4 * nb
                if dl >= 5:
                    return []
                return [(ident[:], tcmp[:, r, 512 * dl:512 * dl + 512], ["ident", "tcmp"])]

            def c_bias(nb, r=r):
                if T - 4 * nb >= 5:
                    return farb[:, r:r + 1], ["farb"]
                return 0.0, []

            attention(T, r, nbs, c_qk, c_extra, c_bias, lambda nb: (vcx[:, nb, :], ["vcx"]),
                      lambda j: nbs[0], lambda j: nbs[-1],
                      lambda j, bks=bks: (ob[bks[j // 2]][:, (j % 2) * 193:(j % 2) * 193 + 193], "ob%d" % bks[j // 2], j % 2 == 0))
            for j in range(4):
                Oj = ob[bks[j // 2]][:, (j % 2) * 193:(j % 2) * 193 + 193]
                ok = "ob%d" % bks[j // 2]
                cc = scol(3)
                P.op("dve", lambda e, Oj=Oj, cc=cc: e.tensor_scalar_max(out=small[:, cc:cc + 1], in0=Oj[:, 64:65], scalar1=1e-30), reads=[ok], writes=["sm%d" % cc])
                P.op("dve", lambda e, cc=cc: e.reciprocal(out=small[:, cc + 1:cc + 2], in_=small[:, cc:cc + 1]), reads=["sm%d" % cc], writes=["sm%d" % (cc + 1)])
                P.op("dve", lambda e, cc=cc, j=j, r=r: e.tensor_tensor(out=small[:, cc + 2:cc + 3], in0=small[:, cc + 1:cc + 2], in1=gates[:, j, r * 3:r * 3 + 1], op=ALU.mult),
                     reads=["sm%d" % (cc + 1), "gates"], writes=["sm%d" % (cc + 2)])
                P.op("dve", lambda e, Oj=Oj, cc=cc, j=j, r=r: e.tensor_scalar_mul(out=ofin32[:, j, r * 64:(r + 1) * 64], in0=Oj[:, 0:64], scalar1=small[:, cc + 2:cc + 3]),
                     reads=[ok, "sm%d" % (cc + 2)], writes=["ofin32_%d_%d" % (j, r)])
                if r == 0:
                    P.op("dve", lambda e, Oj=Oj, cc=cc, j=j: e.tensor_scalar_mul(out=impacc[:, j, :], in0=Oj[:, 65:193], scalar1=small[:, cc + 1:cc + 2]),
                         reads=[ok, "sm%d" % (cc + 1)], writes=["imp%d" % j])
                else:
                    P.op("dve", lambda e, Oj=Oj, cc=cc, j=j: e.scalar_tensor_tensor(out=impacc[:, j, :], in0=Oj[:, 65:193], scalar=small[:, cc + 1:cc + 2], in1=impacc[:, j, :], op0=ALU.mult, op1=ALU.add),
                         reads=[ok, "sm%d" % (cc + 1)], writes=["imp%d" % j])
        for j in range(4):
            P.op("dve", lambda e, j=j: e.tensor_tensor(out=imp2[:], in0=impacc[:, j, :], in1=mM[:, j, :], op=ALU.mult), reads=["imp%d" % j, "mM"], writes=["imp2"])
            P.op("dve", lambda e, j=j: e.tensor_tensor(out=imp2[:], in0=imp2[:], in1=mAC[:, j, :], op=ALU.add), reads=["mAC"], writes=["imp2"])
            P.op("dve", lambda e: e.max(out=m8[:, 0:8], in_=imp2[:]), reads=["imp2"], writes=["m8a"])
            P.op("dve", lambda e, j=j: e.match_replace(out=impacc[:, j, :], in_to_replace=m8[:, 0:8], in_values=imp2[:], imm_value=-1e30), reads=["imp2", "m8a"], writes=["imp%d" % j])
            P.op("dve", lambda e, j=j: e.max(out=m8[:, 8:16], in_=impacc[:, j, :]), reads=["imp%d" % j], writes=["m8b"])
            P.op("dve", lambda e: e.tensor_scalar(out=selb[:], in0=imp2[:], scalar1=m8[:, 15:16], scalar2=NEG, op0=ALU.is_lt, op1=ALU.mult), reads=["imp2", "m8b"], writes=["selb"])
            P.op("pe", lambda e, j=j: e.transpose(out=tp[:, j, :], in_=selb[:], identity=ident[:]), reads=["selb", "ident"], writes=["tp"])
        P.op("dve", lambda e: e.tensor_copy(out=selbT[:].rearrange("p (j q) -> p j q", j=4), in_=tp[:]), reads=["tp"], writes=["selbT"])
        for br_i, (kT_, tab_, name) in enumerate(((kselT, tsel, "sel"), (kwinT, twin, "win"))):
            for r in range(4):
                bnk = P.nxt("ob", 3)
                if name == "sel":
                    kbl = list(range(4 * T + 4))
                    fst = lambda j: 0
                else:
                    kbl = list(range(max(0, 4 * T - 4), 4 * T + 4))
                    fst = lambda j: max(0, 4 * T + j - 4)

                def s_qk(kb, kT_=kT_, name=name):
                    return kT_[:, kb * 128:(kb + 1) * 128], ["k%sT%d" % (name, kb // 4)]

                def s_extra(kb, r=r, name=name, tab_=tab_):
                    a = T * 512 - kb * 128
                    ex = []
                    if name == "win" or a < 256:
                        ex.append((ident[:], tab_[:, r, a + 384:a + 384 + 512], ["ident", "t" + name]))
                    if name == "sel":
                        ex.append((gsel[:, kb * 128:(kb + 1) * 128], selbT[:], ["gsel", "selbT"]))
                    return ex

                def s_bias(kb, r=r, name=name):
                    a = T * 512 - kb * 128
                    if name == "sel" and a >= 256:
                        return farb[:, r:r + 1], ["farb"]
                    return 0.0, []

                attention(T, r, kbl, s_qk, s_extra, s_bias, lambda kb, br_i=br_i: (vsw[:, kb, br_i, :], ["vsw%d" % kb]),
                          fst, lambda j: 4 * T + j,
                          lambda j, bnk=bnk: (ob[bnk][:, j * 65:(j + 1) * 65], "ob%d" % bnk, j == 0))
                for j in range(4):
                    cc = scol(2)
                    Oj = ob[bnk][:, j * 65:(j + 1) * 65]
                    P.op("dve", lambda e, Oj=Oj, cc=cc: e.reciprocal(out=small[:, cc:cc + 1], in_=Oj[:, 64:65]), reads=["ob%d" % bnk], writes=["sm%d" % cc])
                    P.op("dve", lambda e, cc=cc, j=j, r=r, br_i=br_i: e.tensor_tensor(out=small[:, cc + 1:cc + 2], in0=small[:, cc:cc + 1], in1=gates[:, j, r * 3 + 1 + br_i:r * 3 + 2 + br_i], op=ALU.mult),
                         reads=["sm%d" % cc, "gates"], writes=["sm%d" % (cc + 1)])
                    P.op("dve", lambda e, Oj=Oj, cc=cc, j=j, r=r: e.scalar_tensor_tensor(out=ofin32[:, j, r * 64:(r + 1) * 64], in0=Oj[:, 0:64], scalar=small[:, cc + 1:cc + 2], in1=ofin32[:, j, r * 64:(r + 1) * 64], op0=ALU.mult, op1=ALU.add),
                         reads=["ob%d" % bnk, "sm%d" % (cc + 1)], writes=["ofin32_%d_%d" % (j, r)])
        for j in range(4):
            P.op("dve", lambda e, j=j: e.tensor_copy(out=ofin[:, j, :], in_=ofin32[:, j, :]), reads=["ofin32_%d_%d" % (j, r) for r in range(4)], writes=["ofin%d" % j])
        for cch in range(2):
            for j in range(4):
                P.op("pe", lambda e, j=j, cch=cch: e.transpose(out=tp[:, j, :], in_=ofin[:, j, cch * 128:(cch + 1) * 128], identity=ident[:]),
                     reads=["ofin%d" % j, "ident"], writes=["tp"])
            P.op("dve", lambda e, cch=cch: e.tensor_copy(out=oTs[:, cch, :].rearrange("p (j q) -> p j q", j=4), in_=tp[:]), reads=["tp"], writes=["oTs"])
        P.dma("sp", "oT", oT[:, T * 512:(T + 1) * 512].rearrange("(c p) t -> p c t", p=128), oTs[:], reads=["oTs"], writes=["oTd"])
    P.wait_all("sp")
    P.emit()
    return nc


def nsa_consts(S):
    n_sel = S // 64
    ncb = 4 if S >= 8192 else max(1, (S // 16 + 127) // 128)
    n_cmp = (S - 32) // 16 + 1
    j = np.arange(128)[:, None]
    m = np.arange(S)[None, :]
    gsel = ((m // 64) == j).astype(np.float32)
    n = np.arange(ncb * 128)[:, None]
    jb = np.arange(128)[None, :]
    cs = n * 16
    ce = cs + 31
    ovl = ((cs < jb * 64 + 64) & (ce >= jb * 64) & (n < n_cmp) & (jb < n_sel)).astype(np.float32)
    ovl = ovl.reshape(ncb, 128, 128).transpose(1, 0, 2)
    t = np.arange(S)[:, None]
    cur = t // 64
    valid = (jb * 64 <= t) & (jb < n_sel)
    f0 = (jb == 0)
    f1 = (jb == cur)
    f2 = (jb == cur - 1)
    forced = f0 | f1 | f2
    mM = (valid & ~forced).astype(np.float32)
    fv = np.where(f2, 3e4, np.where(f1, 2e4, 1e4)).astype(np.float32)
    mAC = np.where(valid, np.where(forced, fv, 0.0), -1.0).astype(np.float32)
    i = np.arange(128)[:, None]
    mm = np.arange(2560)[None, :]
    d = mm - 16 * i - 31
    idx_cmp = np.where(d < 0, 32, t5_bucket_np(d))
    return dict(gsel=gsel.astype(NPBF), ovl=ovl.astype(NPBF), mM=mM, mAC=mAC, idx_cmp=idx_cmp,
                idx_sel=toeplitz_idx(1024, -384), idx_win=toeplitz_idx(1408, -384, 512))


_PROGS = {}


def _prog(key, fn):
    if key not in _PROGS:
        _PROGS[key] = fn()
    return _PROGS[key]


def _run(nc, in_maps):
    return run_bass_kernel_spmd(nc, in_maps, core_ids=list(range(8))).results


def kernel(x, p, rel_bias, norm_g, mlp_w1, mlp_w2, ple_w, ple_gate_w,
           da_w_in, da_lambda, da_subln, da_w_out,
           nsa_w_in, nsa_cmp_pe, nsa_cmp_w1, nsa_cmp_w2, nsa_w_out,
           fox_w_in, fox_b_f, fox_w_out):
    f32 = lambda a: np.ascontiguousarray(np.asarray(a, dtype=np.float32))
    x = f32(x)
    B, S, _ = x.shape
    TS = (B * S) // 8
    QS = S // TS
    depth = norm_g.shape[0]
    p = f32(p); rel_bias = f32(rel_bias); norm_g = f32(norm_g)
    ident = np.eye(128, dtype=np.float32).astype(NPBF)
    rel_ext = np.concatenate([rel_bias, np.full((1, 16), NEG, np.float32)], 0)
    idx_da = toeplitz_idx(1024, -384)
    xs = x.reshape(B * S, D)
    xsh = [np.ascontiguousarray(xs[c * TS:(c + 1) * TS]) for c in range(8)]

    nc0 = _prog(("tok", TS, True), lambda: build_token(TS, True))
    res = _run(nc0, [{"x": xsh[c], "gn": f32(norm_g[0, 0]), "ident": ident} for c in range(8)])
    hTs = [res[c]["hT"] for c in range(8)]
    ia = ib = ic = 0
    for i in range(depth):
        hTb = [np.ascontiguousarray(np.concatenate(hTs[b * QS:(b + 1) * QS], axis=1)) for b in range(B)]
        kind = i % 3
        ims = []
        if kind == 0:
            lam_init = 0.8 - 0.6 * math.exp(-0.3 * i)
            w_in = f32(da_w_in[ia]); w_o = f32(da_w_out[ia])
            lamc = np.empty((128, 2), np.float32); lamc[:, 0] = 1.0 - lam_init; lamc[:, 1] = -lam_init
            for b in range(B):
                for hp in range(4):
                    cols = [4 * hp + s_ for s_ in range(4)]
                    ims.append({"hT": hTb[b], "ident": ident,
                                "wq": np.ascontiguousarray(w_in[:, 256 * hp:256 * hp + 256]),
                                "wk": np.ascontiguousarray(w_in[:, 1024 + 256 * hp:1024 + 256 * hp + 256]),
                                "wv": np.ascontiguousarray(w_in[:, 2048 + 256 * hp:2048 + 256 * hp + 256]),
                                "tab": np.stack([rel_ext[idx_da, c] for c in cols], 0).astype(NPBF),
                                "farb": np.ascontiguousarray(np.broadcast_to(rel_bias[31, cols][None, :], (128, 4))),
                                "lam": f32(da_lambda[ia]).reshape(-1), "subln": f32(da_subln[ia]), "lamc": lamc})
            nch = _prog(("da", S), lambda: build_head("da", S))
            ia += 1
        elif kind == 1:
            w_in = f32(nsa_w_in[ib]); w_o = f32(nsa_w_out[ib])
            C = _prog(("nsac", S), lambda: nsa_consts(S))
            pe_ = f32(nsa_cmp_pe[ib]); c1 = f32(nsa_cmp_w1[ib]); c2 = f32(nsa_cmp_w2[ib])
            peT = np.ascontiguousarray(pe_.transpose(2, 0, 1))
            cw1 = np.ascontiguousarray(c1.reshape(2, 32, 64, 256).transpose(2, 0, 1, 3))
            cw2 = np.ascontiguousarray(c2.reshape(2, 2, 128, 64).transpose(2, 0, 1, 3))
            for b in range(B):
                for g in range(4):
                    cols = [4 * g + r for r in range(4)]
                    kvc = lambda k_: w_in[:, 1024 + k_ * 256 + g * 64: 1024 + k_ * 256 + g * 64 + 64]
                    ims.append({"hT": hTb[b], "ident": ident,
                                "wq": np.ascontiguousarray(w_in[:, g * 256:(g + 1) * 256]),
                                "wkv": np.ascontiguousarray(np.concatenate([kvc(0), kvc(1), kvc(2), kvc(4), kvc(3), kvc(5)], 1)),
                                "wgt": np.ascontiguousarray(w_in[:, 2560 + g * 12: 2560 + g * 12 + 12]),
                                "peT": peT, "cw1": cw1, "cw2": cw2,
                                "tsel": np.stack([rel_ext[C["idx_sel"], c] for c in cols], 0).astype(NPBF),
                                "tcmp": np.stack([rel_ext[C["idx_cmp"], c] for c in cols], 0).astype(NPBF),
                                "twin": np.stack([rel_ext[C["idx_win"], c] for c in cols], 0).astype(NPBF),
                                "farb": np.ascontiguousarray(np.broadcast_to(rel_bias[31, cols][None, :], (128, 4))),
                                "gsel": C["gsel"], "ovl": C["ovl"], "mM": C["mM"], "mAC": C["mAC"]})
            nch = _prog(("nsa", S), lambda: build_nsa(S))
            ib += 1
        else:
            w_in = f32(fox_w_in[ic]); w_o = f32(fox_w_out[ic]); b_f = f32(fox_b_f[ic])
            mask_ext = np.concatenate([np.zeros((32,), np.float32), np.full((1,), NEG, np.float32)], 0)
            tab = mask_ext[idx_da][None].astype(NPBF)
            tri = np.triu(np.ones((128, 128), np.float32))
            ones = np.ones((128, 128), np.float32)
            for b in range(B):
                for hg in range(4):
                    ims.append({"hT": hTb[b], "ident": ident,
                                "wq": np.ascontiguousarray(w_in[:, 256 * hg:256 * hg + 256]),
                                "wk": np.ascontiguousarray(w_in[:, 1024 + 256 * hg:1024 + 256 * hg + 256]),
                                "wv": np.ascontiguousarray(w_in[:, 2048 + 256 * hg:2048 + 256 * hg + 256]),
                                "tab": tab, "wf": np.ascontiguousarray(w_in[:, 3072 + 4 * hg:3072 + 4 * hg + 4]),
                                "bf": np.ascontiguousarray(np.tile(b_f[4 * hg:4 * hg + 4], 4)), "tri": tri, "ones": ones})
            nch = _prog(("fox", S), lambda: build_head("fox", S))
            ic += 1
        res = _run(nch, ims)
        oTb = [np.concatenate([res[b * 4 + g]["oT"] for g in range(4)], axis=0) for b in range(B)]
        gn = f32(norm_g[i + 1, 0]) if i + 1 < depth else f32(norm_g[i, 0])
        ims = []
        for c in range(8):
            b, q = c // QS, c % QS
            ims.append({"x": xsh[c], "gn": gn, "ident": ident,
                        "oT": np.ascontiguousarray(oTb[b][:, q * TS:(q + 1) * TS]),
                        "p": np.ascontiguousarray(p[i].reshape(B * S, -1)[c * TS:(c + 1) * TS]),
                        "w_out": w_o, "w1": f32(mlp_w1[i]), "w2": f32(mlp_w2[i]),
                        "ple_w": f32(ple_w[i]), "gate_w": f32(ple_gate_w[i]), "g": f32(norm_g[i, 1:4])})
        nct = _prog(("tok", TS, False), lambda: build_token(TS, False))
        res = _run(nct, ims)
        xsh = [res[c]["xo"] for c in range(8)]
        hTs = [res[c]["hT"] for c in range(8)]
    return np.concatenate(xsh, axis=0).reshape(B, S, D).astype(np.float32)
```
